# Optimizing a Trainium2 kernel written in Bass

```python
import math, functools
import jax, jax.numpy as jnp
from jax import lax
import numpy as np

D_MODEL = 1024
BATCH = 8
SEQ = 4096
DEPTH = 2
DEC_BATCH = 128
DEC_SEQ = 4
PAST_LEN = 16384
PAGE_SIZE = 128

ATTN_WIDTH = D_MODEL // 2
SSM_WIDTH = D_MODEL - ATTN_WIDTH
HEAD_DIM = 64
N_HEADS = ATTN_WIDTH // HEAD_DIM
N_KV_HEADS = 2
GQA_GROUP = N_HEADS // N_KV_HEADS
KV_WIDTH = N_KV_HEADS * HEAD_DIM
WINDOW = 128
BLOCK = WINDOW
GROUP_CH = 16
N_GROUPS = SSM_WIDTH // GROUP_CH
STATE = 64
IN_WIDTH = ATTN_WIDTH + 2 * KV_WIDTH + SSM_WIDTH
D_FF = 2688
CONV_W = 3
EPS = 1e-5
DT_MIN = 1e-3
DT_MAX = 1e-1
MASK_VALUE = -1e30

kernel_name = "hymba_swa_s5_convffn_step"


def _rmsnorm(x, g):
    xf = x.astype(jnp.float32)
    y = xf * lax.rsqrt(jnp.mean(xf * xf, axis=-1, keepdims=True) + EPS)
    return (y * g.astype(jnp.float32)).astype(x.dtype)


def _alibi_slopes():
    h = jnp.arange(1, N_HEADS + 1, dtype=jnp.float32)
    return jnp.exp2(-8.0 * h / N_HEADS).reshape(N_KV_HEADS, GQA_GROUP)


def _sink_softmax(s, sink):
    sk = sink[:, :, None, None]
    m = jnp.maximum(jnp.max(s, axis=-1, keepdims=True), sk)
    e = jnp.exp(s - m)
    return e / (jnp.sum(e, axis=-1, keepdims=True) + jnp.exp(sk - m))


def _swa_prompt(q, k, v, sinks):
    b, L = q.shape[0], q.shape[1]
    nb = L // BLOCK
    scale = HEAD_DIM ** -0.5
    qb = q.reshape(b, nb, BLOCK, N_KV_HEADS, GQA_GROUP, HEAD_DIM)
    pad = jnp.zeros((b, BLOCK, N_KV_HEADS, HEAD_DIM), k.dtype)
    kb = jnp.concatenate([pad, k], axis=1).reshape(b, nb + 1, BLOCK, N_KV_HEADS, HEAD_DIM)
    vb = jnp.concatenate([pad.astype(v.dtype), v], axis=1).reshape(b, nb + 1, BLOCK, N_KV_HEADS, HEAD_DIM)
    kband = jnp.concatenate([kb[:, :-1], kb[:, 1:]], axis=2)
    vband = jnp.concatenate([vb[:, :-1], vb[:, 1:]], axis=2)
    s = jnp.einsum('bnqhgd,bnkhd->bnhgqk', qb, kband,
                   preferred_element_type=jnp.float32) * scale
    blk = jnp.arange(nb)[:, None] * BLOCK
    qpos = blk + jnp.arange(BLOCK)[None, :]
    kpos = blk - BLOCK + jnp.arange(2 * BLOCK)[None, :]
    dist = qpos[:, :, None] - kpos[:, None, :]
    valid = (dist >= 0) & (dist < WINDOW) & (kpos[:, None, :] >= 0)
    slopes = _alibi_slopes()
    s = s - slopes[None, None, :, :, None, None] * dist[None, :, None, None].astype(jnp.float32)
    s = jnp.where(valid[None, :, None, None], s, MASK_VALUE)
    p = _sink_softmax(s, sinks)
    o = jnp.einsum('bnhgqk,bnkhd->bnqhgd', p.astype(v.dtype), vband)
    w = min(WINDOW, L)
    return o.reshape(b, L, ATTN_WIDTH), k[:, -w:], v[:, -w:]


def _swa_sample(q, k, v, sinks, k_buf, v_buf):
    b, T = q.shape[0], q.shape[1]
    w = k_buf.shape[1]
    scale = HEAD_DIM ** -0.5
    kc = jnp.concatenate([k_buf.astype(k.dtype), k], axis=1)
    vc = jnp.concatenate([v_buf.astype(v.dtype), v], axis=1)
    qg = q.reshape(b, T, N_KV_HEADS, GQA_GROUP, HEAD_DIM)
    s = jnp.einsum('bqhgd,bkhd->bhgqk', qg, kc,
                   preferred_element_type=jnp.float32) * scale
    qpos = PAST_LEN + jnp.arange(T)
    kpos = jnp.concatenate([PAST_LEN - w + jnp.arange(w), PAST_LEN + jnp.arange(T)])
    dist = qpos[:, None] - kpos[None, :]
    valid = (dist >= 0) & (dist < WINDOW)
    slopes = _alibi_slopes()
    s = s - slopes[None, :, :, None, None] * dist[None, None, None].astype(jnp.float32)
    s = jnp.where(valid[None, None, None], s, MASK_VALUE)
    p = _sink_softmax(s, sinks)
    o = jnp.einsum('bhgqk,bkhd->bqhgd', p.astype(v.dtype), vc)
    return o.reshape(b, T, ATTN_WIDTH), kc[:, -w:], vc[:, -w:]


def _lin_combine(left, right):
    a1, b1 = left
    a2, b2 = right
    return a1 * a2, a2 * b1 + b2


def _s5(u, h0_re, h0_im, lam_re, lam_im, log_step, b_re, b_im, c_re, c_im, d_skip):
    f32 = jnp.float32
    b, L = u.shape[0], u.shape[1]
    ug = u.astype(f32).reshape(b, L, N_GROUPS, GROUP_CH)
    lam = lax.complex(lam_re.astype(f32), lam_im.astype(f32))
    dt = jnp.exp(log_step.astype(f32))[:, None]
    lam_bar = jnp.exp(lam * dt)
    b_bar = ((lam_bar - 1.0) / lam)[..., None] * lax.complex(b_re.astype(f32), b_im.astype(f32))
    bu = jnp.einsum('blgh,gnh->blgn', ug.astype(jnp.complex64), b_bar)
    a = jnp.broadcast_to(lam_bar, (1, L, N_GROUPS, STATE))
    a_cum, xs = lax.associative_scan(_lin_combine, (a, bu), axis=1)
    h0 = lax.complex(h0_re.astype(f32), h0_im.astype(f32))
    xs = xs + a_cum * h0[:, None]
    c = lax.complex(c_re.astype(f32), c_im.astype(f32))
    y = jnp.einsum('blgn,ghn->blgh', xs, c).real \
        + d_skip.astype(f32).reshape(N_GROUPS, GROUP_CH) * ug
    h_last = xs[:, -1]
    return y.reshape(b, L, SSM_WIDTH).astype(u.dtype), jnp.real(h_last), jnp.imag(h_last)


def _conv_ffn(h, prev, w_up, conv_w, conv_b, w_down):
    L = h.shape[1]
    up = h @ w_up
    buf = jnp.concatenate([prev.astype(up.dtype), up], axis=1)
    c = conv_b + sum(conv_w[i] * buf[:, i:i + L] for i in range(CONV_W))
    a, g = jnp.split(c, 2, axis=-1)
    return (jax.nn.silu(g) * a) @ w_down, buf[:, -(CONV_W - 1):]


def _layer(x, p, attend, h0_re, h0_im, conv_prev):
    b, L = x.shape[0], x.shape[1]
    h = _rmsnorm(x, p['norm_mix'])
    proj = h @ p['w_in']
    q = proj[..., :ATTN_WIDTH].reshape(b, L, N_HEADS, HEAD_DIM)
    k = proj[..., ATTN_WIDTH:ATTN_WIDTH + KV_WIDTH].reshape(b, L, N_KV_HEADS, HEAD_DIM)
    v = proj[..., ATTN_WIDTH + KV_WIDTH:ATTN_WIDTH + 2 * KV_WIDTH].reshape(b, L, N_KV_HEADS, HEAD_DIM)
    u = proj[..., ATTN_WIDTH + 2 * KV_WIDTH:]
    sinks = p['sinks'].astype(jnp.float32).reshape(N_KV_HEADS, GQA_GROUP)
    attn, k_win, v_win = attend(q, k, v, sinks)
    ssm, h_re, h_im = _s5(u, h0_re, h0_im, p['lam_re'], p['lam_im'], p['log_step'],
                          p['b_re'], p['b_im'], p['c_re'], p['c_im'], p['d_skip'])
    ssm = jax.nn.gelu(ssm)
    ssm = ssm * jax.nn.sigmoid(ssm @ p['w_glu'] + p['b_glu'])
    merged = jnp.concatenate([_rmsnorm(attn, p['g_attn']), _rmsnorm(ssm, p['g_ssm'])], axis=-1)
    x = x + merged @ p['w_out']
    f, conv_new = _conv_ffn(_rmsnorm(x, p['norm_ffn']), conv_prev,
                            p['w_up'], p['conv_w'], p['conv_b'], p['w_down'])
    x = x + f
    return x, (k_win, v_win, h_re, h_im, conv_new)


def setup_inputs(seed: int = 0) -> dict:
    key = jax.random.key(seed)
    ks = jax.random.split(key, 32)
    f32 = jnp.float32
    win_buf = min(WINDOW, PAST_LEN)

    def nrm(k, shape, scale):
        return jax.random.normal(k, shape, f32) * scale

    n_idx = jnp.arange(STATE, dtype=f32)
    lam_re = -0.5 + nrm(ks[10], (DEPTH, N_GROUPS, STATE), 0.01)
    lam_im = math.pi * n_idx[None, None, :] + nrm(ks[11], (DEPTH, N_GROUPS, STATE), 0.01)
    log_step = jax.random.uniform(ks[12], (DEPTH, N_GROUPS), f32,
                                  math.log(DT_MIN), math.log(DT_MAX))
    return {
        "x_prompt": nrm(ks[0], (BATCH, SEQ, D_MODEL), 1.0),
        "x_sample": nrm(ks[1], (DEC_BATCH, DEC_SEQ, D_MODEL), 1.0),
        "cache_k_win": nrm(ks[2], (DEPTH, DEC_BATCH, win_buf, N_KV_HEADS, HEAD_DIM), 1.0),
        "cache_v_win": nrm(ks[3], (DEPTH, DEC_BATCH, win_buf, N_KV_HEADS, HEAD_DIM), 1.0),
        "state_ssm_re": nrm(ks[4], (DEPTH, DEC_BATCH, N_GROUPS, STATE), 0.5),
        "state_ssm_im": nrm(ks[5], (DEPTH, DEC_BATCH, N_GROUPS, STATE), 0.5),
        "state_conv": nrm(ks[6], (DEPTH, DEC_BATCH, CONV_W - 1, 2 * D_FF), 1.0),
        "norm_mix": 1.0 + nrm(ks[7], (DEPTH, D_MODEL), 0.02),
        "w_in": nrm(ks[8], (DEPTH, D_MODEL, IN_WIDTH), D_MODEL ** -0.5),
        "sinks": nrm(ks[9], (DEPTH, N_HEADS), 0.5),
        "lam_re": lam_re,
        "lam_im": lam_im,
        "log_step": log_step,
        "b_re": nrm(ks[13], (DEPTH, N_GROUPS, STATE, GROUP_CH), (2 * GROUP_CH) ** -0.5),
        "b_im": nrm(ks[14], (DEPTH, N_GROUPS, STATE, GROUP_CH), (2 * GROUP_CH) ** -0.5),
        "c_re": nrm(ks[15], (DEPTH, N_GROUPS, GROUP_CH, STATE), (2 * STATE) ** -0.5),
        "c_im": nrm(ks[16], (DEPTH, N_GROUPS, GROUP_CH, STATE), (2 * STATE) ** -0.5),
        "d_skip": nrm(ks[17], (DEPTH, SSM_WIDTH), 1.0),
        "w_glu": nrm(ks[18], (DEPTH, SSM_WIDTH, SSM_WIDTH), SSM_WIDTH ** -0.5),
        "b_glu": nrm(ks[19], (DEPTH, SSM_WIDTH), 0.01),
        "g_attn": 1.0 + nrm(ks[20], (DEPTH, ATTN_WIDTH), 0.02),
        "g_ssm": 1.0 + nrm(ks[21], (DEPTH, SSM_WIDTH), 0.02),
        "w_out": nrm(ks[22], (DEPTH, ATTN_WIDTH + SSM_WIDTH, D_MODEL), (ATTN_WIDTH + SSM_WIDTH) ** -0.5),
        "norm_ffn": 1.0 + nrm(ks[23], (DEPTH, D_MODEL), 0.02),
        "w_up": nrm(ks[24], (DEPTH, D_MODEL, 2 * D_FF), D_MODEL ** -0.5),
        "conv_w": nrm(ks[25], (DEPTH, CONV_W, 2 * D_FF), CONV_W ** -0.5),
        "conv_b": nrm(ks[26], (DEPTH, 2 * D_FF), 0.01),
        "w_down": nrm(ks[27], (DEPTH, D_FF, D_MODEL), D_FF ** -0.5),
        "norm_final": 1.0 + nrm(ks[28], (D_MODEL,), 0.02),
    }


def reference(x_prompt, x_sample, cache_k_win, cache_v_win, state_ssm_re, state_ssm_im, state_conv,
              norm_mix, w_in, sinks, lam_re, lam_im, log_step, b_re, b_im, c_re, c_im, d_skip,
              w_glu, b_glu, g_attn, g_ssm, w_out, norm_ffn, w_up, conv_w, conv_b, w_down, norm_final):
    xp, xs = x_prompt, x_sample
    bp = x_prompt.shape[0]
    zero_h = jnp.zeros((bp, N_GROUPS, STATE), jnp.float32)
    zero_conv = jnp.zeros((bp, CONV_W - 1, 2 * D_FF), x_prompt.dtype)
    prompt_states, sample_states = [], []
    for l in range(DEPTH):
        p = {
            'norm_mix': norm_mix[l], 'w_in': w_in[l], 'sinks': sinks[l],
            'lam_re': lam_re[l], 'lam_im': lam_im[l], 'log_step': log_step[l],
            'b_re': b_re[l], 'b_im': b_im[l], 'c_re': c_re[l], 'c_im': c_im[l],
            'd_skip': d_skip[l], 'w_glu': w_glu[l], 'b_glu': b_glu[l],
            'g_attn': g_attn[l], 'g_ssm': g_ssm[l], 'w_out': w_out[l],
            'norm_ffn': norm_ffn[l], 'w_up': w_up[l], 'conv_w': conv_w[l],
            'conv_b': conv_b[l], 'w_down': w_down[l],
        }
        xp, sp = _layer(xp, p, _swa_prompt, zero_h, zero_h, zero_conv)
        attend_s = functools.partial(_swa_sample, k_buf=cache_k_win[l], v_buf=cache_v_win[l])
        xs, ss = _layer(xs, p, attend_s, state_ssm_re[l], state_ssm_im[l], state_conv[l])
        prompt_states.append(sp)
        sample_states.append(ss)
    y_prompt = _rmsnorm(xp, norm_final)
    y_sample = _rmsnorm(xs, norm_final)
    k_win_p, v_win_p, ssm_re_p, ssm_im_p, conv_p = (jnp.stack(z) for z in zip(*prompt_states))
    k_win_s, v_win_s, ssm_re_s, ssm_im_s, conv_s = (jnp.stack(z) for z in zip(*sample_states))
    return (y_prompt, y_sample, k_win_p, v_win_p, ssm_re_p, ssm_im_p, conv_p,
            k_win_s, v_win_s, ssm_re_s, ssm_im_s, conv_s)
```

```python
import numpy as np
import concourse.bass as bass
import concourse.mybir as mybir

F32 = mybir.dt.float32
BF16 = mybir.dt.bfloat16
I32 = mybir.dt.int32
ALU = mybir.AluOpType
AF = mybir.ActivationFunctionType
AX = mybir.AxisListType

ENGS = ("pe", "act", "dve", "pool", "sp")


class Trk:
    __slots__ = ("name", "w", "rs", "excl")

    def __init__(self, name="", excl=False):
        self.name = name
        self.excl = excl
        self.w = None
        self.rs = []


class Op:
    __slots__ = ("eng", "idx", "fn", "deps", "flag", "dma", "val")

    def __init__(self, eng, idx, fn, deps, dma):
        self.eng, self.idx, self.fn, self.deps, self.dma = eng, idx, fn, deps, dma
        self.flag = False
        self.val = None


class Prog:
    def __init__(self, nc):
        self.nc = nc
        self.ops = {e: [] for e in ENGS}
        self.dsems = []
        self._ctx = []

    def enter(self, cm):
        v = cm.__enter__()
        self._ctx.append(cm)
        return v

    def sbuf(self, name, shape, dt):
        return self.enter(self.nc.sbuf_tensor(name, list(shape), dt))

    def psum(self, name, shape, dt=F32):
        return self.enter(self.nc.psum_tensor(name, list(shape), dt))

    def new_dsem(self, name):
        s = self.enter(self.nc.semaphore(name))
        d = {"sem": s, "cnt": 0}
        self.dsems.append(d)
        return d

    def _deps(self, eng, reads, writes):
        deps = []
        for t in reads:
            if t.w is not None:
                deps.append(t.w)
        for t in writes:
            if t.w is not None:
                deps.append(t.w)
            deps.extend(t.rs)
        return deps

    def op(self, eng, fn, reads=(), writes=()):
        ex = [t for t in reads if t.excl]
        if ex:
            reads = [t for t in reads if not t.excl]
            writes = list(writes) + ex
        deps = self._deps(eng, reads, writes)
        o = Op(eng, len(self.ops[eng]), fn, deps, None)
        self.ops[eng].append(o)
        for t in reads:
            t.rs.append(o)
        for t in writes:
            t.w = o
            t.rs = []
        return o

    def dma(self, q, dsem, out, in_, reads=(), writes=(), **kw):
        deps = self._deps(q, reads, writes)

        def fn(e):
            return e.dma_start(out=out, in_=in_, **kw)
        o = Op(q, len(self.ops[q]), fn, deps, dsem)
        dsem["cnt"] += 16
        o.val = dsem["cnt"]
        o.flag = True
        self.ops[q].append(o)
        for t in reads:
            t.rs.append(o)
        for t in writes:
            t.w = o
            t.rs = []
        return o

    def barrier(self, eng, dsem, fn, trks):
        dep = Op("sp", -1, None, [], dsem)
        dep.val = dsem["cnt"]
        dep.flag = True
        deps = [dep] + [t.w for t in trks if t.w is not None]
        o = Op(eng, len(self.ops[eng]), fn, deps, None)
        self.ops[eng].append(o)
        for t in trks:
            t.w = o
            t.rs = []
        return o

    def finish(self, final_waits=()):
        nc = self.nc
        for e in ENGS:
            for o in self.ops[e]:
                for d in o.deps:
                    if d.dma is None:
                        if d.eng == "pe" and e == "pe":
                            continue
                        d.flag = True
        esem = {}
        for e in ENGS:
            esem[e] = self.enter(nc.semaphore("esem_" + e))
            c = 0
            for o in self.ops[e]:
                if o.dma is None:
                    if o.flag:
                        c += 1
                        o.val = c
                    else:
                        o.val = None
        nxt = {}
        for e in ENGS:
            arr = [None] * len(self.ops[e])
            cur = None
            for i in range(len(self.ops[e]) - 1, -1, -1):
                o = self.ops[e][i]
                if o.dma is None and o.flag:
                    cur = o.val
                arr[i] = cur
            nxt[e] = arr
        self.nwaits = 0
        prog = self

        def emit(ename):
            def body(eh):
                seen = {}
                for o in prog.ops[ename]:
                    need = {}
                    for d in o.deps:
                        if d.dma is not None:
                            key = ("d", id(d.dma))
                            sem, val = d.dma["sem"], d.val
                        else:
                            if d.eng == "pe" and ename == "pe":
                                continue
                            key = ("e", d.eng)
                            sem, val = esem[d.eng], d.val
                            assert val is not None
                        if seen.get(key, 0) >= val:
                            continue
                        if key not in need or need[key][1] < val:
                            need[key] = (sem, val)
                    for key, (sem, val) in need.items():
                        eh.wait_ge(sem, val)
                        seen[key] = val
                        prog.nwaits += 1
                    if o.fn is None:
                        continue
                    ins = o.fn(eh)
                    if o.dma is not None:
                        ins.then_inc(o.dma["sem"], 16)
                    elif o.flag:
                        ins.then_inc(esem[ename], 1)
                if ename == "sp":
                    for d in prog.dsems:
                        if d["cnt"] > 0:
                            eh.wait_ge(d["sem"], d["cnt"])
            return body

        with nc.Block() as block:
            block.tensor(emit("pe"))
            block.scalar(emit("act"))
            block.vector(emit("dve"))
            block.gpsimd(emit("pool"))
            block.sync(emit("sp"))
        for cm in reversed(self._ctx):
            cm.__exit__(None, None, None)
        self._ctx = []

import math
from concourse.bass_utils import run_bass_kernel_spmd

D = 1024; L = 2; SEQ = 4096; NB = 16; NT = 256; TLP = 64; NTI = 2; DFF = 2688; MT_UP = 42
NEG = -30000.0
EPS = 1e-5
PI = math.pi


def build_program(n_batches=NB, do_sample=True):
    nc = bass.Bass("TRN2", target_bir_lowering=False)
    P = Prog(nc)

    def din(name, shape):
        return nc.dram_tensor(name, list(shape), F32, kind="ExternalInput").ap()

    def dout(name, shape):
        return nc.dram_tensor(name, list(shape), F32, kind="ExternalOutput").ap()

    xpT = din("xpT", [128, 8, SEQ]); xsT = din("xsT", [128, 8, 64])
    ck = din("ck", [L, 128, 16, 128]); cv = din("cv", [L, 128, 16, 128])
    ckr = din("ckr", [L, 16, 128, 128]); cvr = din("cvr", [L, 16, 128, 128])
    sre = din("sre", [L, 128, 16, 16]); sim = din("sim", [L, 128, 16, 16])
    sconv = din("sconv", [L, 128, MT_UP, 16, 2])
    w_in = din("w_in", [L, 128, 8, 1280])
    w_outa = din("w_outa", [L, 64, 8, 1024]); w_outb = din("w_outb", [L, 128, 4, 1024])
    w_glu = din("w_glu", [L, 128, 4, 512])
    w_up = din("w_up", [L, 128, 8, 2 * DFF]); w_down = din("w_down", [L, 128, 21, 1024])
    g_mix = din("g_mix", [128, L, 8]); g_ffn = din("g_ffn", [128, L, 8]); g_fin = din("g_fin", [128, 8])
    g_attn = din("g_attn", [64, L, 8]); g_ssm = din("g_ssm", [128, L, 4])
    b_glu = din("b_glu", [128, L, 4]); d_skip = din("d_skip", [128, L, 4])
    cw = din("cw", [128, L, MT_UP, 3]); cb = din("cb", [128, L, MT_UP])
    sinkP = din("sinkP", [64, L, 8])
    lamre = din("lamre", [128, L, 16]); lamim = din("lamim", [128, L, 16]); lstep = din("lstep", [128, L, 16])
    bre = din("bre", [128, L, 16, 16]); bim = din("bim", [128, L, 16, 16])
    cre = din("cre", [128, L, 16, 16]); cim = din("cim", [128, L, 16, 16])
    c_ident = din("c_ident", [128, 128]); c_ones = din("c_ones", [128, 128])
    c_biasP = din("c_biasP", [128, 2, 2, 512]); c_biasSc = din("c_biasSc", [128, 2, 256]); c_biasSn = din("c_biasSn", [4, 2, 256])

    o_ypT = dout("o_ypT", [128, 8, SEQ]); o_ysT = dout("o_ysT", [128, 8, 64])
    o_kp = dout("o_kp", [L, 128, 128]); o_vp = dout("o_vp", [L, 128, 128])
    o_srp = dout("o_srp", [L, 128, 16]); o_sip = dout("o_sip", [L, 128, 16])
    o_cp = dout("o_cp", [L, 128, MT_UP, 2])
    o_ks = dout("o_ks", [L, 16, 128, 128]); o_vs = dout("o_vs", [L, 16, 128, 128])
    o_srs = dout("o_srs", [L, 128, 16, 16]); o_sis = dout("o_sis", [L, 128, 16, 16])
    o_cs = dout("o_cs", [L, 128, MT_UP, 16, 2])

    def dscr(name, shape):
        return nc.dram_tensor(name, list(shape), BF16, kind="Internal").ap()
    wb_in = dscr("wb_in", [L, 128, 8, 1280]); wb_outa = dscr("wb_outa", [L, 64, 8, 1024]); wb_outb = dscr("wb_outb", [L, 128, 4, 1024])
    wb_glu = dscr("wb_glu", [L, 128, 4, 512]); wb_up = dscr("wb_up", [L, 128, 8, 2 * DFF]); wb_down = dscr("wb_down", [L, 128, 21, 1024])

    def T(n=1):
        return [Trk() for _ in range(n)] if n > 1 else Trk()

    def act(out, in_, func, reads, writes, bias=None, scale=None):
        kw = {}
        if bias is not None:
            kw["bias"] = bias
        if scale is not None:
            kw["scale"] = scale
        return P.op("act", lambda e: e.activation(out, in_, func, **kw), reads, writes)

    def tt(eng, out, a, b, op, reads, writes):
        return P.op(eng, lambda e: e.tensor_tensor(out, a, b, op), reads, writes)

    def ts(eng, out, a, s1, s2, op0, op1, reads, writes):
        return P.op(eng, lambda e: e.tensor_scalar(out, a, s1, s2, op0, op1), reads, writes)

    def stt(out, a, s, b, op0, op1, reads, writes):
        return P.op("dve", lambda e: e.scalar_tensor_tensor(out, a, s, b, op0, op1), reads, writes)

    def cp(eng, out, in_, reads, writes):
        if eng == "act":
            return P.op("act", lambda e: e.activation(out, in_, AF.Copy), reads, writes)
        return P.op(eng, lambda e: e.tensor_copy(out, in_), reads, writes)

    def mm(out, lhsT, rhs, start, stop, reads, writes):
        return P.op("pe", lambda e: e.matmul(out, lhsT, rhs, start=start, stop=stop), reads, writes)

    def mset(eng, ap, val, writes):
        return P.op(eng, lambda e: e.memset(ap, val), (), writes)

    def recip(out, in_, reads, writes):
        return P.op("dve", lambda e: e.reciprocal(out, in_), reads, writes)

    d_pre = P.new_dsem("d_pre"); d_kc = P.new_dsem("d_kc"); d_vc = P.new_dsem("d_vc")
    d_kvn = [P.new_dsem("d_kvn0"), P.new_dsem("d_kvn1")]; d_so = [P.new_dsem("d_so0"), P.new_dsem("d_so1")]
    d_out = P.new_dsem("d_out")

    d_pre2 = P.new_dsem("d_pre2")

    def load(dst, src, trk, q="sp"):
        return P.dma(q, d_pre if q == "sp" else d_pre2, dst, src, writes=[trk])

    twbs = {}
    for l_ in range(L):
        for nm_, dst_, src_, kt_ in (("in", wb_in, w_in, 8), ("outa", wb_outa, w_outa, 8), ("outb", wb_outb, w_outb, 4),
                                     ("glu", wb_glu, w_glu, 4), ("up", wb_up, w_up, 8), ("down", wb_down, w_down, 21)):
            d_cvt = P.new_dsem("d_cvt_%s%d" % (nm_, l_)); t_ = Trk()
            twbs[(nm_, l_)] = t_
            for k_ in range(kt_):
                P.dma("pool", d_cvt, dst_[l_, :, k_, :], src_[l_, :, k_, :], writes=[t_])

    ident_f = P.sbuf("ident_f", [128, 128], F32); ident_b = P.sbuf("ident_b", [128, 128], BF16)
    ones_b = P.sbuf("ones_b", [128, 128], BF16)
    biasP = P.sbuf("biasP", [128, 2, 2, 512], BF16)
    biasSc = P.sbuf("biasSc", [128, 2, 256], BF16); biasSn = P.sbuf("biasSn", [4, 2, 256], BF16)
    tconst = T()
    load(ident_f[:], c_ident, tconst)
    load(ident_b[:], c_ident, tconst, "pool"); load(ones_b[:], c_ones, tconst, "pool")
    load(biasP[:], c_biasP, tconst, "pool"); load(biasSc[:], c_biasSc, tconst, "pool"); load(biasSn[:], c_biasSn, tconst, "pool")
    gmix = P.sbuf("gmix", [128, L, 8], F32); gffn = P.sbuf("gffn", [128, L, 8], F32); gfin = P.sbuf("gfin", [128, 8], F32)
    gattn = P.sbuf("gattn", [64, L, 8], F32); gssm = P.sbuf("gssm", [128, L, 4], F32)
    bglu = P.sbuf("bglu", [128, L, 4], F32); hbglu = P.sbuf("hbglu", [128, L, 4], F32); dsk = P.sbuf("dsk", [128, L, 4], F32)
    cws = P.sbuf("cws", [128, L, MT_UP, 3], F32); cbs = P.sbuf("cbs", [128, L, MT_UP], F32)
    esP = P.sbuf("esP", [64, L, 8], F32)
    for dst, src in ((gmix, g_mix), (gffn, g_ffn), (gfin, g_fin), (gattn, g_attn), (gssm, g_ssm), (bglu, b_glu),
                     (dsk, d_skip), (cws, cw), (cbs, cb), (esP, sinkP)):
        load(dst[:], src, tconst)
    epsT = P.sbuf("epsT", [128, 1], F32); hpiT = P.sbuf("hpiT", [128, 1], F32)
    P.barrier("dve", d_pre, lambda e: e.memset(epsT[:], EPS), [tconst])
    P.barrier("dve", d_pre2, lambda e: e.memset(hpiT[:], PI / 2), [tconst])
    mset("dve", epsT[:], EPS, [tconst]); mset("dve", hpiT[:], PI / 2, [tconst])
    act(esP[:], esP[:], AF.Exp, [tconst], [tconst])
    ts("dve", hbglu[:], bglu[:], 0.5, None, ALU.mult, ALU.bypass, [tconst], [tconst])

    XT = lambda n=1: [Trk(excl=True) for _ in range(n)] if n > 1 else Trk(excl=True)
    psA = [P.psum("psA%d" % i, [128, 512]) for i in range(2)]; tA = XT(2)
    psS = [P.psum("psS%d" % i, [128, 512]) for i in range(2)]; tS = XT(2)
    psO = P.psum("psO", [128, 512]); tO = XT()
    psD = P.psum("psD", [128, 512]); tD = XT()
    psM = [P.psum("psM%d" % i, [128, 512]) for i in range(2)]; tM = XT(2)
    cntM = [0]

    def nextM():
        i = cntM[0] % 2
        cntM[0] += 1
        return psM[i], tM[i]

    s5 = []
    scr = [P.sbuf("s5scr%d" % i, [128, 16], F32) for i in range(8)]
    G12 = P.sbuf("G12", [128, 8, NT], F32)
    bmf = G12[:].rearrange("p a (b c) -> p (a b) c", c=128)
    S5tmp = P.sbuf("S5tmp", [128, 4, 16, TLP], F32); ttmp = T(); ttq = T(4); ttqP_g = T(4)
    braw = S5tmp[:, 0].rearrange("p g t -> p (g t)")[:, 0:512].rearrange("p (a g h) -> p a g h", a=2, g=16)
    bbar = S5tmp[:, 1].rearrange("p g t -> p (g t)")[:, 0:512].rearrange("p (a g h) -> p a g h", a=2, g=16)
    craw = S5tmp[:, 2].rearrange("p g t -> p (g t)")[:, 0:512].rearrange("p (a g h) -> p a g h", a=2, g=16)
    tab = T()
    for l in range(L):
        lr = P.sbuf("lr%d" % l, [128, 16], F32); li = P.sbuf("li%d" % l, [128, 16], F32); ls = P.sbuf("ls%d" % l, [128, 16], F32)
        d_tab = P.new_dsem("d_tab%d" % l)
        for dst_, src_ in ((lr[:], lamre[:, l, :]), (li[:], lamim[:, l, :]), (ls[:], lstep[:, l, :]), (braw[:, 0], bre[:, l]),
                           (braw[:, 1], bim[:, l]), (craw[:, 0], cre[:, l]), (craw[:, 1], cim[:, l])):
            P.dma("sp", d_tab, dst_, src_, writes=[tab])
        P.barrier("dve", d_tab, (lambda l_: lambda e: e.memset(scr[0][:], 0.0))(l), [tab])
        AR = P.sbuf("AR%d" % l, [128, 16], F32); AI = P.sbuf("AI%d" % l, [128, 16], F32)
        AR16 = P.sbuf("AR16_%d" % l, [128, 16, 16], F32); AI16 = P.sbuf("AI16_%d" % l, [128, 16, 16], F32)
        BT = [P.sbuf("BT%d_%d" % (l, ri), [128, 16, 128], BF16) for ri in range(2)]
        CT = [P.sbuf("CT%d_%d" % (l, ri), [128, 16, 128], BF16) for ri in range(2)]
        dt_, zr, th, rr, cc, ss, t0, t1 = [s[:] for s in scr]
        R, W = [tab], [tab]
        act(dt_, ls[:], AF.Exp, R, W)
        tt("dve", zr, lr[:], dt_, ALU.mult, R, W)
        tt("dve", th, li[:], dt_, ALU.mult, R, W)
        act(rr, zr, AF.Exp, R, W)
        act(cc, th, AF.Sin, R, W, bias=hpiT[:], scale=1.0 / 32)
        act(ss, th, AF.Sin, R, W, scale=1.0 / 32)
        for _ in range(5):
            tt("dve", t0, cc, cc, ALU.mult, R, W)
            tt("dve", t1, ss, ss, ALU.mult, R, W)
            tt("dve", ss, ss, cc, ALU.mult, R, W)
            ts("dve", ss, ss, 2.0, None, ALU.mult, ALU.bypass, R, W)
            tt("dve", cc, t0, t1, ALU.subtract, R, W)
        tt("dve", AR[:], rr, cc, ALU.mult, R, W)
        tt("dve", AI[:], rr, ss, ALU.mult, R, W)
        for s_ in range(16):
            cp("dve", AR16[:, :, s_], AR[:], R, W); cp("dve", AI16[:, :, s_], AI[:], R, W)
        RR = P.sbuf("RR%d" % l, [128, 16], F32)
        cp("dve", RR[:], rr, R, W)
        C1 = P.sbuf("C1_%d" % l, [128, 16, TLP], F32); S1 = P.sbuf("S1_%d" % l, [128, 16, TLP], F32)
        cp("dve", C1[:, :, 0], cc, R, W); cp("dve", S1[:, :, 0], ss, R, W)
        kk_ = 1
        while kk_ < TLP:
            cp("dve", t0, C1[:, :, kk_ - 1], R, W); cp("dve", t1, S1[:, :, kk_ - 1], R, W)
            ts("dve", zr, t1, -1.0, None, ALU.mult, ALU.bypass, R, W)
            for gp in range(16):
                ts("dve", C1[:, gp, kk_:2 * kk_], C1[:, gp, 0:kk_], t0[:, gp:gp + 1], None, ALU.mult, ALU.bypass, R, W)
                stt(C1[:, gp, kk_:2 * kk_], S1[:, gp, 0:kk_], zr[:, gp:gp + 1], C1[:, gp, kk_:2 * kk_], ALU.mult, ALU.add, R, W)
                ts("dve", S1[:, gp, kk_:2 * kk_], S1[:, gp, 0:kk_], t0[:, gp:gp + 1], None, ALU.mult, ALU.bypass, R, W)
                stt(S1[:, gp, kk_:2 * kk_], C1[:, gp, 0:kk_], t1[:, gp:gp + 1], S1[:, gp, kk_:2 * kk_], ALU.mult, ALU.add, R, W)
            kk_ *= 2
        nr, den, cr, ci = dt_, zr, th, rr
        ts("dve", nr, AR[:], -1.0, None, ALU.add, ALU.bypass, R, W)
        tt("dve", t0, lr[:], lr[:], ALU.mult, R, W)
        tt("dve", t1, li[:], li[:], ALU.mult, R, W)
        tt("dve", den, t0, t1, ALU.add, R, W)
        recip(den, den, R, W)
        tt("dve", t0, nr, lr[:], ALU.mult, R, W)
        tt("dve", t1, AI[:], li[:], ALU.mult, R, W)
        tt("dve", t0, t0, t1, ALU.add, R, W)
        tt("dve", cr, t0, den, ALU.mult, R, W)
        tt("dve", t0, AI[:], lr[:], ALU.mult, R, W)
        tt("dve", t1, nr, li[:], ALU.mult, R, W)
        tt("dve", t0, t0, t1, ALU.subtract, R, W)
        tt("dve", ci, t0, den, ALU.mult, R, W)
        nci = cc
        ts("dve", nci, ci, -1.0, None, ALU.mult, ALU.bypass, R, W)
        for gp in range(16):
            ts("dve", bbar[:, 0, gp, :], braw[:, 0, gp, :], cr[:, gp:gp + 1], None, ALU.mult, ALU.bypass, R, W)
            stt(bbar[:, 0, gp, :], braw[:, 1, gp, :], nci[:, gp:gp + 1], bbar[:, 0, gp, :], ALU.mult, ALU.add, R, W)
            ts("dve", bbar[:, 1, gp, :], braw[:, 1, gp, :], cr[:, gp:gp + 1], None, ALU.mult, ALU.bypass, R, W)
            stt(bbar[:, 1, gp, :], braw[:, 0, gp, :], ci[:, gp:gp + 1], bbar[:, 1, gp, :], ALU.mult, ALU.add, R, W)
        for ri in range(2):
            mset("dve", bmf[:], 0.0, W)
            for gp in range(16):
                c0 = 32 * (gp % 4)
                cp("dve", bmf[0:64, gp, c0:c0 + 16], bbar[0:64, ri, gp, :], R, W)
                cp("dve", bmf[64:128, gp, c0 + 16:c0 + 32], bbar[64:128, ri, gp, :], R, W)
            for gp in range(16):
                pm, tm = nextM()
                P.op("pe", (lambda pm_, gp_: lambda e: e.transpose(pm_[:, 0:128], bmf[:, gp_, :], ident_f[:]))(pm, gp),
                     [tab, tconst], [tm])
                cp("act", BT[ri][:, gp, :], pm[:, 0:128], [tm], [tab])
            mset("dve", CT[ri][:], 0.0, W)
            for gp in range(16):
                c0 = 32 * (gp % 4)
                sc = 1.0 if ri == 0 else -1.0
                ts("dve", CT[ri][0:64, gp, c0:c0 + 16], craw[0:64, ri, gp, :], sc, None, ALU.mult, ALU.bypass, R, W)
                ts("dve", CT[ri][64:128, gp, c0 + 16:c0 + 32], craw[64:128, ri, gp, :], sc, None, ALU.mult, ALU.bypass, R, W)
        s5.append(dict(AR=AR, AI=AI, AR16=AR16, AI16=AI16, BT=BT, CT=CT, RR=RR, C1=C1, S1=S1))

    xT = P.sbuf("xT", [128, 8, NT], F32); tx = T(8)
    hT = P.sbuf("hT", [128, 8, NT], BF16); th_ = T(8)
    rstd = P.sbuf("rstd", [128, NT], F32); trs = T()
    qT = P.sbuf("qT", [64, 8, NT], BF16); tq = T()
    kT = [P.sbuf("kT%d" % l, [64, 2, 128 + NT], BF16) for l in range(L)]; tk = T(2)
    vtok = [P.sbuf("vtok%d" % l, [128, NTI + 1, 128], BF16) for l in range(L)]; tv = T(2)
    kvf = P.sbuf("kvf", [128, 256], F32); tkvf = T()
    wkv = P.sbuf("wkv", [128, 8, 256], BF16); twkv = T(); d_wkv = P.new_dsem("d_wkv")
    uT = P.sbuf("uT", [128, 4, NT], BF16); tu = T()
    PT = P.sbuf("PT", [128, 2, 512], BF16); tPT = T()
    den_sb = P.sbuf("den_sb", [64, 512], F32); tden = T()
    attnT = P.sbuf("attnT", [64, 8, NT], F32); tat = T()
    attnB = P.sbuf("attnB", [64, 8, NT], BF16); tatb = T()
    bu = [P.sbuf("bu%d" % ri, [128, 16, 64], F32) for ri in range(2)]; tbu = T(2)
    xs5 = [P.sbuf("xs5_%d" % ri, [128, 16, 64], BF16) for ri in range(2)]; txs = T(2)
    Xs_ = [[P.sbuf("Xs_%d_%d" % (ri, pp), [128, 16, 16], F32) for pp in range(2)] for ri in range(2)]
    tXs_ = [[T() for pp in range(2)] for ri in range(2)]
    Xst = [Xs_, Xs_]; tX = [tXs_, tXs_]
    Xp = [[P.sbuf("Xp%d_%d" % (l, ri), [128, 16], F32) for ri in range(2)] for l in range(L)]
    tXp = [[T() for ri in range(2)] for l in range(L)]
    stmp = [P.sbuf("stmp%d" % i, [128, 16, 16], F32) for i in range(4)]; tst = T(4)
    yT = P.sbuf("yT", [128, 4, NT], F32); ty = T()
    tg1 = T(); tg2 = T()
    g1 = G12[:, 0:4, :]; g2 = G12[:, 4:8, :]
    sbf = P.sbuf("sbf", [128, 4, NT], BF16); tsb = T()
    ssmB = P.sbuf("ssmB", [128, 4, NT], BF16); tsmb = T()
    upb2 = [P.sbuf("upb%d" % i, [128, 2, NT + 32], F32) for i in range(2)]; tup2 = [T(2), T(2)]
    cs2 = [P.sbuf("cs_%d" % i, [128, 2, NT], F32) for i in range(2)]; tcs2 = [T(2), T(2)]; tgs = T(2)
    hid = P.sbuf("hid", [128, 21, NT], BF16); thid = T()
    sq = hid[:, 0:8, :]; tsq = thid
    carry = [P.sbuf("carry%d" % l, [128, MT_UP, 2], F32) for l in range(L)]; tcar = [T(MT_UP), T(MT_UP)]
    scarry1 = P.sbuf("scarry", [128, MT_UP, 32], F32); scarry = [scarry1, scarry1]
    kc_f = S5tmp[:, 0:2].rearrange("p a g t -> p (a g t)").rearrange("p (s c) -> p s c", c=128); tkc = ttmp
    kcT = P.sbuf("kcT", [64, 16, 128], BF16); tkcT = T()
    vc_b = P.sbuf("vc_b", [128, 16, 128], BF16); tvc = T()
    kvn_f = P.sbuf("kvn_f", [4, 2, 256], F32); tkvn = T(2)
    vn_b = P.sbuf("vn_b", [4, 16, 128], BF16); tvn = T()
    yout = G12; tyo = T()

    NSLOT = 3
    wsl = [P.sbuf("wsl%d" % i, [128, 12, 128], BF16) for i in range(NSLOT)]
    twsl = T(NSLOT); dwsl = [P.new_dsem("dw%d" % i) for i in range(NSLOT)]
    wcnt = [0]

    def linear(nblk, load_fn, mm_fn, cb_fn, rtrks, group=1, wtrk=()):
        base = wcnt[0]
        wcnt[0] += nblk

        def issue(b):
            si = (base + b) % NSLOT
            for dst, src in load_fn(b, wsl[si]):
                P.dma("sp", dwsl[si], dst, src, reads=list(wtrk), writes=[twsl[si]])
        for b in range(min(NSLOT - 1, nblk)):
            issue(b)
        for b in range(nblk):
            if b + NSLOT - 1 < nblk:
                issue(b + NSLOT - 1)
            si = (base + b) % NSLOT
            ps, tps = psA[(b // group) % 2], tA[(b // group) % 2]
            pairs = mm_fn(b, wsl[si])
            out_ap = pairs[0][2]
            for i, (lt, rh, _) in enumerate(pairs):
                mm(out_ap(ps), lt, rh, i == 0 and b % group == 0, i == len(pairs) - 1 and b % group == group - 1,
                   [twsl[si]] + rtrks, [tps])
            if b % group == group - 1:
                cb_fn(b // group, ps, tps)

    def rmsnorm(src, tsrc, nk, npart, gain_fn, ntok, dst_fn, tdst, nfeat):
        act(sq[0:npart, 0:nk, 0:ntok], src[0:npart, 0:nk, 0:ntok], AF.Square, tsrc, [tsq])
        pm, tm = nextM()
        for k in range(nk):
            mm(pm[:, 0:ntok], ones_b[0:npart, :], sq[0:npart, k, 0:ntok], k == 0, k == nk - 1, [tsq, tconst], [tm])
        act(rstd[:, 0:ntok], pm[:, 0:ntok], AF.Sqrt, [tm, tconst], [trs], bias=epsT[:], scale=1.0 / nfeat)
        recip(rstd[:, 0:ntok], rstd[:, 0:ntok], [trs], [trs])
        for k in range(nk):
            stt(dst_fn(k), src[0:npart, k, 0:ntok], gain_fn(k), rstd[0:npart, 0:ntok], ALU.mult, ALU.mult,
                tsrc + [trs, tconst], tdst)

    def run_layer(l, grp):
        sample = grp["sample"]
        ntok = grp["ntok"]; NS = grp["NS"]; TL = grp["TL"]
        bidx = grp.get("b", 0)
        tb = s5[l]
        rmsnorm(xT, tx, 8, 128, lambda k: gmix[:, l, k:k + 1], ntok, lambda k: hT[:, k, 0:ntok], th_, D)
        blocks = [(h * 64, 64) for h in range(8)] + [(512 + kv * 64, 64) for kv in range(2)] + [(768 + q * 128, 128) for q in range(4)]
        koff = 0 if sample else 128

        def w_in_load(b, slot):
            c0, msz = blocks[b]
            return [(slot[:, 0:8, 0:msz], wb_in[l, :, :, c0:c0 + msz])]

        def w_in_mm(b, slot):
            c0, msz = blocks[b]
            return [(slot[:, k, 0:msz], hT[:, k, 0:ntok], (lambda ps, msz=msz: ps[0:msz, 0:ntok])) for k in range(8)]

        def w_in_cb(b, ps, tps):
            if b < 8:
                act(qT[:, b, 0:ntok], ps[0:64, 0:ntok], AF.Copy, [tps], [tq], scale=0.125)
            elif b < 10:
                cp("act", kT[l][:, b - 8, koff:koff + ntok], ps[0:64, 0:ntok], [tps], [tk[l]])
            else:
                cp("act", uT[:, b - 10, 0:ntok], ps[:, 0:ntok], [tps], [tu])
        linear(14, w_in_load, w_in_mm, w_in_cb, th_, wtrk=[twbs[("in", l)]])
        P.dma("sp", d_wkv, wkv[:], wb_in[l, :, :, 512:768], reads=[twbs[("in", l)]], writes=[twkv])
        if not sample:
            for i in range(NTI):
                pm, tm = nextM()
                for k in range(8):
                    mm(pm[:, 0:256], hT[:, k, i * 128:(i + 1) * 128], wkv[:, k, :], k == 0, k == 7, th_ + [twkv], [tm])
                cp("act", vtok[l][:, i + 1, :], pm[:, 128:256], [tm], [tv[l]])
                if bidx == n_batches - 1 and i == NTI - 1:
                    cp("dve", kvf[:], pm[:, 0:256], [tm], [tkvf])
                    P.dma("pool", d_out, o_kp[l], kvf[:, 0:128], reads=[tkvf])
                    P.dma("pool", d_out, o_vp[l], kvf[:, 128:256], reads=[tkvf])
        else:
            for s_ in range(16):
                pm, tm = nextM()
                for k in range(8):
                    mm(pm[0:4, 0:256], hT[:, k, 4 * s_:4 * s_ + 4], wkv[:, k, :], k == 0, k == 7, th_ + [twkv], [tm])
                cp("act", kvn_f[:, s_ % 2, :], pm[0:4, 0:256], [tm], [tkvn[s_ % 2]])
                cp("dve", vn_b[:, s_, :], kvn_f[:, s_ % 2, 128:256], [tkvn[s_ % 2]], [tvn])
                P.dma("pool", d_kvn[s_ % 2], o_ks[l, s_, 124:128, :], kvn_f[:, s_ % 2, 0:128], reads=[tkvn[s_ % 2]])
                P.dma("pool", d_kvn[s_ % 2], o_vs[l, s_, 124:128, :], kvn_f[:, s_ % 2, 128:256], reads=[tkvn[s_ % 2]])
            P.dma("pool", d_out, o_ks[l, :, 0:124, :], ckr[l, :, 4:128, :])
            P.dma("pool", d_out, o_vs[l, :, 0:124, :], cvr[l, :, 4:128, :])
        attn_units = []
        if not sample:
            def mk_unit(i, kv):
                first = (bidx == 0 and i == 0)
                blks = [1] if first else [0, 1]

                def p1():
                    for blk in blks:
                        kcol = i * 128 + blk * 128
                        mm(psS[blk][:, :], kT[l][:, kv, kcol:kcol + 128], qT[:, 4 * kv:4 * kv + 4, i * 128:(i + 1) * 128],
                           True, False, [tk[l], tq], [tS[blk]])
                        mm(psS[blk][:, :], ident_b[:], biasP[:, blk, kv, :], False, True, [tconst], [tS[blk]])
                        act(PT[:, blk, :], psS[blk][:, :], AF.Exp, [tS[blk]], [tPT])
                    for j, blk in enumerate(blks):
                        mm(psO[0:64, :], vtok[l][:, i + blk, kv * 64:(kv + 1) * 64], PT[:, blk, :], j == 0, j == len(blks) - 1,
                           [tv[l], tPT], [tO])
                    for j, blk in enumerate(blks):
                        mm(psD[0:64, :], ones_b[:, 0:64], PT[:, blk, :], j == 0, j == len(blks) - 1, [tconst, tPT], [tD])

                def p2():
                    for g_ in range(4):
                        ts("dve", den_sb[:, g_ * 128:(g_ + 1) * 128], psD[0:64, g_ * 128:(g_ + 1) * 128], esP[:, l, 4 * kv + g_:4 * kv + g_ + 1], None,
                           ALU.add, ALU.bypass, [tD, tconst], [tden])
                    recip(den_sb[:, :], den_sb[:, :], [tden], [tden])
                    tt("dve", attnT[:, 4 * kv:4 * kv + 4, i * 128:(i + 1) * 128], psO[0:64, :].rearrange("p (g q) -> p g q", g=4),
                       den_sb[:, :].rearrange("p (g q) -> p g q", g=4), ALU.mult, [tO, tden], [tat])
                return p1, p2
            for i in range(NTI):
                for kv in range(2):
                    attn_units.append(mk_unit(i, kv))
        if sample:
            pass
        else:
            pass
        if not sample:
            pass
        else:
            P.dma("sp", d_kc, kc_f[:], ck[l], writes=[tkc, ttq[0], ttq[1], ttqP_g[0], ttqP_g[1]])
            P.dma("pool", d_vc, vc_b[:], cv[l], writes=[tvc])
            for kv in range(2):
                for s_ in range(16):
                    pm, tm = nextM()
                    P.op("pe", (lambda pm_, s2, kv2: lambda e: e.transpose(pm_[0:64, 0:128], kc_f[:, s2, kv2 * 64:(kv2 + 1) * 64], ident_f[:]))(pm, s_, kv),
                         [tkc, tconst], [tm])
                    cp("act", kcT[:, s_, :], pm[0:64, 0:128], [tm], [tkcT])
                mm(psS[0][:, 0:256], ident_b[:], biasSc[:, kv, :], True, False, [tconst], [tS[0]])
                for s_ in range(16):
                    mm(psS[0][:, 16 * s_:16 * s_ + 16], kcT[:, s_, :], qT[:, 4 * kv:4 * kv + 4, 4 * s_:4 * s_ + 4],
                       False, s_ == 15, [tkcT, tq], [tS[0]])
                mm(psS[1][0:4, 0:256], ident_b[0:4, 0:4], biasSn[:, kv, :], True, False, [tconst], [tS[1]])
                for s_ in range(16):
                    mm(psS[1][0:4, 16 * s_:16 * s_ + 16], kT[l][:, kv, 4 * s_:4 * s_ + 4], qT[:, 4 * kv:4 * kv + 4, 4 * s_:4 * s_ + 4],
                       False, s_ == 15, [tk[l], tq], [tS[1]])
                act(PT[:, 0, 0:256], psS[0][:, 0:256], AF.Exp, [tS[0]], [tPT])
                act(PT[0:4, 1, 0:256], psS[1][0:4, 0:256], AF.Exp, [tS[1]], [tPT])
                for s_ in range(16):
                    c = slice(16 * s_, 16 * s_ + 16)
                    mm(psO[0:64, c], vc_b[:, s_, kv * 64:(kv + 1) * 64], PT[:, 0, c], True, False, [tvc, tPT], [tO])
                    mm(psO[0:64, c], vn_b[:, s_, kv * 64:(kv + 1) * 64], PT[0:4, 1, c], False, True, [tvn, tPT], [tO])
                for s_ in range(16):
                    c = slice(16 * s_, 16 * s_ + 16)
                    mm(psD[0:64, c], ones_b[:, 0:64], PT[:, 0, c], True, False, [tconst, tPT], [tD])
                    mm(psD[0:64, c], ones_b[0:4, 0:64], PT[0:4, 1, c], False, True, [tconst, tPT], [tD])
                for g_ in range(4):
                    ts("dve", den_sb[:, 0:256].rearrange("p (s g t) -> p g s t", g=4, t=4)[:, g_],
                       psD[0:64, 0:256].rearrange("p (s g t) -> p g s t", g=4, t=4)[:, g_], esP[:, l, 4 * kv + g_:4 * kv + g_ + 1], None,
                       ALU.add, ALU.bypass, [tD, tconst], [tden])
                recip(den_sb[:, 0:256], den_sb[:, 0:256], [tden], [tden])
                tt("dve", attnT[:, 4 * kv:4 * kv + 4, 0:64].rearrange("p g (s t) -> p s g t", t=4),
                   psO[0:64, 0:256].rearrange("p (s g t) -> p s g t", g=4, t=4),
                   den_sb[:, 0:256].rearrange("p (s g t) -> p s g t", g=4, t=4), ALU.mult, [tO, tden], [tat])
        NTL = NS * TL
        ntile = ntok // NTL

        def s5_bu(it):
            tok0 = it * NTL
            kk = 0
            for ri in range(2):
                for qd in range(4):
                    ps, tps = psA[kk % 2], tA[kk % 2]
                    kk += 1
                    for r4 in range(4):
                        gp = 4 * qd + r4
                        mm(ps[:, r4 * NTL:(r4 + 1) * NTL], tb["BT"][ri][:, gp, :], uT[:, qd, tok0:tok0 + NTL], True, True,
                           [tab, tu], [tps])
                    cp("act", bu[ri][:, 4 * qd:4 * qd + 4, 0:NTL], ps[:, 0:4 * NTL].rearrange("p (g t) -> p g t", t=NTL),
                       [tps], [tbu[ri]])

        def s5_y(it):
            tok0 = it * NTL
            for qd in range(4):
                pm, tm = nextM()
                for r4 in range(4):
                    gp = 4 * qd + r4
                    mm(pm[:, 0:NTL], tb["CT"][0][:, gp, :], xs5[0][:, gp, 0:NTL], r4 == 0, False, [tab, txs[0]], [tm])
                    mm(pm[:, 0:NTL], tb["CT"][1][:, gp, :], xs5[1][:, gp, 0:NTL], False, r4 == 3, [tab, txs[1]], [tm])
                stt(yT[:, qd, tok0:tok0 + NTL], uT[:, qd, tok0:tok0 + NTL], dsk[:, l, qd:qd + 1], pm[:, 0:NTL], ALU.mult, ALU.add,
                    [tu, tconst, tm], [ty])

        if sample:
            it = 0
            tok0 = 0
            s5_bu(0)
            ARt = tb["AR16"][:, :, 0:NS]; AIt = tb["AI16"][:, :, 0:NS]
            for t in range(TL):
                stp = grp["step"]
                cur, nxt = stp % 2, 1 - stp % 2
                grp["step"] += 1
                Xr_c, Xi_c = Xst[l][0][cur][:, :, 0:NS], Xst[l][1][cur][:, :, 0:NS]
                Xr_n, Xi_n = Xst[l][0][nxt][:, :, 0:NS], Xst[l][1][nxt][:, :, 0:NS]
                tXr_c, tXi_c, tXr_n, tXi_n = tX[l][0][cur], tX[l][1][cur], tX[l][0][nxt], tX[l][1][nxt]
                bur = bu[0][:, :, 0:NTL].rearrange("p g (s t) -> p g s t", t=TL)[:, :, :, t]
                bui = bu[1][:, :, 0:NTL].rearrange("p g (s t) -> p g s t", t=TL)[:, :, :, t]
                a0, a1, a2, a3 = [s[:, :, 0:NS] for s in stmp]
                tt("dve", a0, Xr_c, ARt, ALU.mult, [tXr_c, tab], [tst[0]])
                tt("dve", a1, Xi_c, AIt, ALU.mult, [tXi_c, tab], [tst[1]])
                tt("dve", a0, a0, a1, ALU.subtract, [tst[0], tst[1]], [tst[0]])
                tt("dve", Xr_n, a0, bur, ALU.add, [tst[0], tbu[0]], [tXr_n])
                tt("dve", a2, Xr_c, AIt, ALU.mult, [tXr_c, tab], [tst[2]])
                tt("dve", a3, Xi_c, ARt, ALU.mult, [tXi_c, tab], [tst[3]])
                tt("dve", a2, a2, a3, ALU.add, [tst[2], tst[3]], [tst[2]])
                tt("dve", Xi_n, a2, bui, ALU.add, [tst[2], tbu[1]], [tXi_n])
                xr_o = xs5[0][:, :, 0:NTL].rearrange("p g (s t) -> p g s t", t=TL)[:, :, :, t]
                xi_o = xs5[1][:, :, 0:NTL].rearrange("p g (s t) -> p g s t", t=TL)[:, :, :, t]
                cp("act", xr_o, Xr_n, [tXr_n], [txs[0]])
                cp("act", xi_o, Xi_n, [tXi_n], [txs[1]])
            s5_y(0)
        else:
            A_, B_, C_, D_ = S5tmp[:, 0], S5tmp[:, 1], S5tmp[:, 2], S5tmp[:, 3]
            tA_, tB_, tC_, tD_ = ttq
            c1 = tb["C1"][:]; s1 = tb["S1"][:]
            bur = bu[0][:, :, 0:TL]; bui = bu[1][:, :, 0:TL]
            Xcr = Xp[l][0]; Xci = Xp[l][1]; tXcr = tXp[l][0]; tXci = tXp[l][1]
            GS = 11
            ttqP = ttqP_g
            halves = (("dve", 0, GS, ttq), ("pool", GS, 16, ttqP))
            for it in range(ntile):
                s5_bu(it)
                if it < len(attn_units):
                    attn_units[it][0]()
                for eng_, g0, g1_, tq_ in halves:
                    tA_, tB_, tC_, tD_ = tq_
                    a_, b_, c_ = A_[:, g0:g1_, :], B_[:, g0:g1_, :], C_[:, g0:g1_, :]
                    cc_, ss_ = c1[:, g0:g1_, :], s1[:, g0:g1_, :]
                    br_, bi_ = bur[:, g0:g1_, :], bui[:, g0:g1_, :]
                    tt(eng_, a_, cc_, br_, ALU.mult, [tab, tbu[0], ttmp], [tA_])
                    tt(eng_, b_, ss_, bi_, ALU.mult, [tab, tbu[1], ttmp], [tB_])
                    tt(eng_, a_, a_, b_, ALU.add, [tB_], [tA_])
                    tt(eng_, b_, cc_, bi_, ALU.mult, [tab, tbu[1]], [tB_])
                    tt(eng_, c_, ss_, br_, ALU.mult, [tab, tbu[0]], [tC_])
                    tt(eng_, b_, b_, c_, ALU.subtract, [tC_], [tB_])
                for gp in range(16):
                    tq_ = ttq if gp < GS else ttqP
                    P.op("dve", (lambda gp: lambda e: e.tensor_tensor_scan(
                        A_[:, gp, :], tb["RR"][:, gp:gp + 1].to_broadcast([128, TL]), A_[:, gp, :],
                        Xcr[:, gp:gp + 1], ALU.mult, ALU.add))(gp), [tab, tXcr], [tq_[0]])
                    P.op("dve", (lambda gp: lambda e: e.tensor_tensor_scan(
                        B_[:, gp, :], tb["RR"][:, gp:gp + 1].to_broadcast([128, TL]), B_[:, gp, :],
                        Xci[:, gp:gp + 1], ALU.mult, ALU.add))(gp), [tab, tXci], [tq_[1]])
                if it < len(attn_units):
                    attn_units[it][1]()
                if it > 0:
                    s5_y(it - 1)
                for eng_, g0, g1_, tq_ in halves:
                    tA_, tB_, tC_, tD_ = tq_
                    a_, b_, c_, d_ = A_[:, g0:g1_, :], B_[:, g0:g1_, :], C_[:, g0:g1_, :], D_[:, g0:g1_, :]
                    cc_, ss_ = c1[:, g0:g1_, :], s1[:, g0:g1_, :]
                    tt(eng_, c_, cc_, a_, ALU.mult, [tab, tA_], [tC_])
                    tt(eng_, d_, ss_, b_, ALU.mult, [tab, tB_], [tD_])
                    tt(eng_, c_, c_, d_, ALU.subtract, [tD_], [tC_])
                    tt(eng_, d_, ss_, a_, ALU.mult, [tab, tA_, tC_], [tD_])
                    tt(eng_, a_, cc_, b_, ALU.mult, [tab, tB_], [tA_])
                    tt(eng_, d_, d_, a_, ALU.add, [tA_], [tD_])
                cp("act", xs5[0][:, :, 0:TL], C_, [ttq[2], ttqP[2]], [txs[0]])
                cp("act", xs5[1][:, :, 0:TL], D_, [ttq[3], ttqP[3]], [txs[1]])
                cp("act", Xcr[:], C_[:, :, TL - 1], [ttq[2], ttqP[2]], [tXcr])
                cp("act", Xci[:], D_[:, :, TL - 1], [ttq[3], ttqP[3]], [tXci])
            s5_y(ntile - 1)
        if not sample:
            for u_ in attn_units[ntile:]:
                u_[0](); u_[1]()
            cp("dve", kT[l][:, :, 0:128], kT[l][:, :, NT:NT + 128], [tk[l]], [tk[l]])
            cp("dve", vtok[l][:, 0, :], vtok[l][:, NTI, :], [tv[l]], [tv[l]])
        fin = grp["step"] % 2 if sample else 0
        if sample:
            P.dma("pool", d_so[l], o_srs[l], Xst[l][0][fin][:], reads=[tX[l][0][fin]])
            P.dma("pool", d_so[l], o_sis[l], Xst[l][1][fin][:], reads=[tX[l][1][fin]])
        elif bidx == n_batches - 1:
            P.dma("pool", d_out, o_srp[l], Xp[l][0][:], reads=[tXp[l][0]])
            P.dma("pool", d_out, o_sip[l], Xp[l][1][:], reads=[tXp[l][1]])
        Y = yT[:, :, 0:ntok]; G1 = g1[:, :, 0:ntok]; G2 = g2[:, :, 0:ntok]
        tt("dve", G1, Y, Y, ALU.mult, [ty], [tg1])
        ts("dve", G1, G1, 0.044715, 1.0, ALU.mult, ALU.add, [tg1], [tg1])
        tt("dve", G1, G1, Y, ALU.mult, [tg1, ty], [tg1])
        act(G1, G1, AF.Tanh, [tg1], [tg1], scale=0.7978845608028654)
        ts("dve", G2, Y, 0.5, None, ALU.mult, ALU.bypass, [ty], [tg2])
        stt(G1, G1, 1.0, G2, ALU.add, ALU.mult, [tg1, tg2], [tg1])
        cp("act", sbf[:, :, 0:ntok], G1, [tg1], [tsb])
        ts("dve", G2, G1, 0.5, None, ALU.mult, ALU.bypass, [tg1], [tg2])

        def glu_load(b, slot):
            return [(slot[:, 0:4, :], wb_glu[l, :, :, b * 128:(b + 1) * 128])]

        def glu_mm(b, slot):
            return [(slot[:, k, :], sbf[:, k, 0:ntok], (lambda ps: ps[:, 0:ntok])) for k in range(4)]

        def glu_cb(b, ps, tps):
            act(g1[:, b, 0:ntok], ps[:, 0:ntok], AF.Tanh, [tps, tconst], [tg1], bias=hbglu[:, l, b:b + 1], scale=0.5)
            stt(yT[:, b, 0:ntok], g1[:, b, 0:ntok], 1.0, g2[:, b, 0:ntok], ALU.add, ALU.mult, [tg1, tg2], [ty])
        linear(4, glu_load, glu_mm, glu_cb, [tsb], wtrk=[twbs[("glu", l)]])
        rmsnorm(attnT, [tat], 8, 64, lambda k: gattn[:, l, k:k + 1], ntok, lambda k: attnB[:, k, 0:ntok], [tatb], 512)
        rmsnorm(yT, [ty], 4, 128, lambda k: gssm[:, l, k:k + 1], ntok, lambda k: ssmB[:, k, 0:ntok], [tsmb], 512)

        def wo_load(b, slot):
            return [(slot[0:64, 0:8, :], wb_outa[l, :, :, b * 128:(b + 1) * 128]),
                    (slot[:, 8:12, :], wb_outb[l, :, :, b * 128:(b + 1) * 128])]

        def wo_mm(b, slot):
            o = (lambda ps: ps[:, 0:ntok])
            return [(slot[0:64, k, :], attnB[:, k, 0:ntok], o) for k in range(8)] + \
                   [(slot[:, 8 + k, :], ssmB[:, k, 0:ntok], o) for k in range(4)]

        def wo_cb(b, ps, tps):
            tt("dve", xT[:, b, 0:ntok], ps[:, 0:ntok], xT[:, b, 0:ntok], ALU.add, [tps, tx[b]], [tx[b]])
        linear(8, wo_load, wo_mm, wo_cb, [tatb, tsmb], wtrk=[twbs[("outa", l)], twbs[("outb", l)]])
        rmsnorm(xT, tx, 8, 128, lambda k: gffn[:, l, k:k + 1], ntok, lambda k: hT[:, k, 0:ntok], th_, D)
        car = scarry[l] if sample else carry[l]
        CTL = ntok // NS
        W2 = CTL + 2

        def up_load(b, slot):
            m = (b // 2) + 21 * (b % 2)
            return [(slot[:, 0:8, :], wb_up[l, :, :, m * 128:(m + 1) * 128])]

        def up_mm(b, slot):
            return [(slot[:, k, :], hT[:, k, 0:ntok], (lambda ps: ps[:, 0:ntok])) for k in range(8)]

        def up_cb(b, ps, tps):
            w = b % 2
            m = (b // 2) + 21 * w
            pp = (b // 2) % 2
            upb = upb2[pp]; cs_ = cs2[pp]; tup = tup2[pp]; tcs = tcs2[pp]
            ub = upb[:, w, 0:NS * W2].rearrange("p (s t) -> p s t", t=W2)
            cv_ = car[:, m, 0:NS * 2].rearrange("p (s t) -> p s t", t=2)
            cp("act", ub[:, :, 0:2], cv_, [tcar[l][m]], [tup[w]])
            cp("act", ub[:, :, 2:W2], ps[:, 0:ntok].rearrange("p (s t) -> p s t", t=CTL), [tps], [tup[w]])
            cp("dve", cv_, ub[:, :, CTL:CTL + 2], [tup[w]], [tcar[l][m]])
            cv3 = cs_[:, w, 0:ntok].rearrange("p (s t) -> p s t", t=CTL)
            act(cv3, ub[:, :, 2:W2], AF.Identity, [tup[w], tconst], [tcs[w]], bias=cbs[:, l, m:m + 1], scale=cws[:, l, m, 2:3])
            stt(cv3, ub[:, :, 1:1 + CTL], cws[:, l, m, 1:2], cv3, ALU.mult, ALU.add, [tup[w], tconst, tcs[w]], [tcs[w]])
            stt(cv3, ub[:, :, 0:CTL], cws[:, l, m, 0:1], cv3, ALU.mult, ALU.add, [tup[w], tconst, tcs[w]], [tcs[w]])
            if w == 1:
                mh = b // 2
                A_ = cs_[:, 0, 0:ntok]; G_ = cs_[:, 1, 0:ntok]
                act(g1[:, pp, 0:ntok], G_, AF.Tanh, [tcs[1], tg1], [tgs[pp]], scale=0.5)
                stt(g1[:, pp, 0:ntok], g1[:, pp, 0:ntok], 1.0, G_, ALU.add, ALU.mult, [tgs[pp], tcs[1]], [tgs[pp]])
                stt(hid[:, mh, 0:ntok], g1[:, pp, 0:ntok], 0.5, A_, ALU.mult, ALU.mult, [tgs[pp], tcs[0]], [thid])
        linear(42, up_load, up_mm, up_cb, th_, wtrk=[twbs[("up", l)]])
        if sample:
            P.dma("pool", d_out, o_cs[l], car[:, :, 0:32].rearrange("p m (s t) -> p m s t", t=2), reads=tcar[l])
        elif bidx == n_batches - 1:
            P.dma("pool", d_out, o_cp[l], car[:, :, 0:2], reads=tcar[l])

        def dn_load(b, slot):
            k0, nk = (0, 11) if b % 2 == 0 else (11, 10)
            return [(slot[:, 0:nk, :], wb_down[l, :, k0:k0 + nk, (b // 2) * 128:(b // 2 + 1) * 128])]

        def dn_mm(b, slot):
            k0, nk = (0, 11) if b % 2 == 0 else (11, 10)
            return [(slot[:, k, :], hid[:, k0 + k, 0:ntok], (lambda ps: ps[:, 0:ntok])) for k in range(nk)]

        def dn_cb(b, ps, tps):
            tt("dve", xT[:, b, 0:ntok], ps[:, 0:ntok], xT[:, b, 0:ntok], ALU.add, [tps, tx[b]], [tx[b]])
        linear(16, dn_load, dn_mm, dn_cb, [thid], group=2, wtrk=[twbs[("down", l)]])

    d_x = P.new_dsem("d_x"); d_y = P.new_dsem("d_y")
    for l in range(L):
        mset("dve", carry[l][:], 0.0, tcar[l])
        for ri in range(2):
            mset("dve", Xp[l][ri][:], 0.0, [tXp[l][ri]])
    pstep = [0, 0]
    for b in range(n_batches):
        P.dma("pool", d_x, xT[:], xpT[:, :, b * NT:(b + 1) * NT], writes=tx)
        for l in range(L):
            grp = dict(sample=False, ntok=NT, NS=1, TL=TLP, b=b, step=pstep[l])
            run_layer(l, grp)
            pstep[l] = grp["step"]
        rmsnorm(xT, tx, 8, 128, lambda k: gfin[:, k:k + 1], NT, lambda k: yout[:, k, :], [tyo, tg1, tg2] + tgs, D)
        P.dma("pool", d_y, o_ypT[:, :, b * NT:(b + 1) * NT], yout[:], reads=[tyo, tg1, tg2] + tgs)
    if do_sample:
        P.dma("pool", d_x, xT[:, :, 0:64], xsT, writes=tx)
        for l in range(L):
            cur = pstep[l] % 2
            d_s1 = P.new_dsem("d_s1_%d" % l); d_s2 = P.new_dsem("d_s2_%d" % l); d_s3 = P.new_dsem("d_s3_%d" % l)
            P.dma("pool", d_s1, Xst[l][0][cur][:], sre[l], writes=[tX[l][0][cur]])
            P.dma("pool", d_s2, Xst[l][1][cur][:], sim[l], writes=[tX[l][1][cur]])
            P.dma("pool", d_s3, scarry[l][:, :, 0:32].rearrange("p m (s t) -> p m s t", t=2), sconv[l], writes=tcar[l])
            grp = dict(sample=True, ntok=64, NS=16, TL=4, step=pstep[l])
            run_layer(l, grp)
        rmsnorm(xT, tx, 8, 128, lambda k: gfin[:, k:k + 1], 64, lambda k: yout[:, k, 0:64], [tyo, tg1, tg2] + tgs, D)
        P.dma("pool", d_y, o_ysT, yout[:, :, 0:64], reads=[tyo, tg1, tg2] + tgs)
    P.finish()
    return nc


def _consts():
    ident = np.eye(128, dtype=np.float32)
    ones = np.ones((128, 128), np.float32)
    slopes = np.array([2.0 ** -(h + 1) for h in range(8)], np.float32).reshape(2, 4)
    k = np.arange(128)[:, None]; q = np.arange(128)[None, :]
    biasP = np.full((128, 2, 2, 4, 128), NEG, np.float32)
    for kv in range(2):
        for g in range(4):
            sl = slopes[kv, g]
            dcur = q - k
            biasP[:, 1, kv, g, :] = np.where(dcur >= 0, -sl * dcur, NEG)
            dprev = q + 128 - k
            biasP[:, 0, kv, g, :] = np.where(dprev < 128, -sl * dprev, NEG)
    biasP = biasP.reshape(128, 2, 2, 512)
    j = np.arange(128)[:, None]; tq = np.arange(4)[None, :]
    bSc = np.full((128, 2, 16, 4, 4), NEG, np.float32)
    bSn = np.full((4, 2, 16, 4, 4), NEG, np.float32)
    tk_ = np.arange(4)[:, None]
    for kv in range(2):
        for g in range(4):
            sl = slopes[kv, g]
            dist = tq + 128 - j
            bSc[:, kv, :, g, :] = np.where(dist < 128, -sl * dist, NEG)[:, None, :]
            dn = tq - tk_
            bSn[:, kv, :, g, :] = np.where(dn >= 0, -sl * dn, NEG)[:, None, :]
    return dict(c_ident=ident, c_ones=ones, c_biasP=biasP, c_biasSc=bSc.reshape(128, 2, 256), c_biasSn=bSn.reshape(4, 2, 256))


def _ktile(w, kt):
    Lw, K, N = w.shape
    return np.ascontiguousarray(w.reshape(Lw, kt, K // kt, N).transpose(0, 2, 1, 3))


def _vec(v, kt):
    Lw, F = v.shape
    return np.ascontiguousarray(v.reshape(Lw, kt, F // kt).transpose(2, 0, 1))


def _l0(a):
    sh = a.shape
    a = a.reshape(sh[0], 16, 2, 64, *sh[3:])
    perm = (2, 3, 0, 1) + tuple(range(4, a.ndim))
    a = a.transpose(perm)
    return np.ascontiguousarray(a.reshape(128, sh[0], 16, *sh[3:]))


_PROG = {}


def kernel(x_prompt, x_sample, cache_k_win, cache_v_win, state_ssm_re, state_ssm_im, state_conv,
           norm_mix, w_in, sinks, lam_re, lam_im, log_step, b_re, b_im, c_re, c_im, d_skip,
           w_glu, b_glu, g_attn, g_ssm, w_out, norm_ffn, w_up, conv_w, conv_b, w_down, norm_final):
    f = lambda a: np.ascontiguousarray(np.asarray(a, dtype=np.float32))
    (x_prompt, x_sample, cache_k_win, cache_v_win, state_ssm_re, state_ssm_im, state_conv, norm_mix, w_in, sinks,
     lam_re, lam_im, log_step, b_re, b_im, c_re, c_im, d_skip, w_glu, b_glu, g_attn, g_ssm, w_out, norm_ffn, w_up,
     conv_w, conv_b, w_down, norm_final) = [f(a) for a in (
        x_prompt, x_sample, cache_k_win, cache_v_win, state_ssm_re, state_ssm_im, state_conv, norm_mix, w_in, sinks,
        lam_re, lam_im, log_step, b_re, b_im, c_re, c_im, d_skip, w_glu, b_glu, g_attn, g_ssm, w_out, norm_ffn, w_up,
        conv_w, conv_b, w_down, norm_final)]
    if "nc" not in _PROG:
        _PROG["nc"] = build_program()
    nc = _PROG["nc"]
    shared = dict(_consts())
    shared.update(
        w_in=_ktile(w_in, 8), w_outa=_ktile(w_out[:, :512], 8), w_outb=_ktile(w_out[:, 512:], 4),
        w_glu=_ktile(w_glu, 4), w_up=_ktile(w_up, 8), w_down=_ktile(w_down, 21),
        g_mix=_vec(norm_mix, 8), g_ffn=_vec(norm_ffn, 8), g_fin=_vec(norm_final[None], 8)[:, 0],
        g_attn=_vec(g_attn, 8), g_ssm=_vec(g_ssm, 4), b_glu=_vec(b_glu, 4), d_skip=_vec(d_skip, 4),
        cw=np.ascontiguousarray(conv_w.reshape(L, 3, MT_UP, 128).transpose(3, 0, 2, 1)),
        cb=_vec(conv_b, MT_UP),
        sinkP=np.ascontiguousarray(np.broadcast_to(sinks[None], (64, L, 8))),
        lamre=_l0(lam_re), lamim=_l0(lam_im),
        lstep=_l0(np.broadcast_to(log_step[:, :, None], (L, 32, 64))),
        bre=_l0(b_re), bim=_l0(b_im),
        cre=_l0(c_re.transpose(0, 1, 3, 2)), cim=_l0(c_im.transpose(0, 1, 3, 2)),
    )
    shared = {k: np.ascontiguousarray(v, dtype=np.float32) for k, v in shared.items()}
    in_maps = []
    for c in range(8):
        s0 = 16 * c
        m = dict(shared)
        m["xpT"] = np.ascontiguousarray(x_prompt[c].T.reshape(8, 128, SEQ).transpose(1, 0, 2))
        m["xsT"] = np.ascontiguousarray(x_sample[s0:s0 + 16].reshape(64, D).T.reshape(8, 128, 64).transpose(1, 0, 2))
        ckc = cache_k_win[:, s0:s0 + 16].reshape(L, 16, 128, 128); cvc = cache_v_win[:, s0:s0 + 16].reshape(L, 16, 128, 128)
        m["ck"] = np.ascontiguousarray(ckc.transpose(0, 2, 1, 3)); m["cv"] = np.ascontiguousarray(cvc.transpose(0, 2, 1, 3))
        m["ckr"] = np.ascontiguousarray(ckc); m["cvr"] = np.ascontiguousarray(cvc)
        def st(a):
            a = a[:, s0:s0 + 16].reshape(L, 16, 16, 2, 64).transpose(0, 3, 4, 2, 1)
            return np.ascontiguousarray(a.reshape(L, 128, 16, 16))
        m["sre"] = st(state_ssm_re); m["sim"] = st(state_ssm_im)
        m["sconv"] = np.ascontiguousarray(state_conv[:, s0:s0 + 16].reshape(L, 16, 2, MT_UP, 128).transpose(0, 4, 3, 1, 2))
        in_maps.append(m)
    res = run_bass_kernel_spmd(nc, in_maps, core_ids=list(range(8)))
    R = res.results
    B = 8
    y_p = np.stack([R[c]["o_ypT"].transpose(1, 0, 2).reshape(D, SEQ).T for c in range(B)])
    y_s = np.concatenate([R[c]["o_ysT"].transpose(1, 0, 2).reshape(D, 64).T.reshape(16, 4, D) for c in range(B)])
    k_p = np.stack([R[c]["o_kp"] for c in range(B)], 1).reshape(L, B, 128, 2, 64)
    v_p = np.stack([R[c]["o_vp"] for c in range(B)], 1).reshape(L, B, 128, 2, 64)

    def unst_p(key):
        a = np.stack([R[c][key] for c in range(B)], 1)
        a = a.reshape(L, B, 2, 64, 16).transpose(0, 1, 4, 2, 3)
        return np.ascontiguousarray(a.reshape(L, B, 32, 64))
    sr_p = unst_p("o_srp"); si_p = unst_p("o_sip")
    c_p = np.stack([R[c]["o_cp"] for c in range(B)], 1)
    c_p = np.ascontiguousarray(c_p.transpose(0, 1, 4, 3, 2).reshape(L, B, 2, 2 * DFF))
    k_s = np.concatenate([R[c]["o_ks"] for c in range(B)], 1).reshape(L, 128, 128, 2, 64)
    v_s = np.concatenate([R[c]["o_vs"] for c in range(B)], 1).reshape(L, 128, 128, 2, 64)

    def unst_s(key):
        a = np.stack([R[c][key] for c in range(B)], 1)
        a = a.reshape(L, B, 2, 64, 16, 16).transpose(0, 1, 5, 4, 2, 3)
        return np.ascontiguousarray(a.reshape(L, B * 16, 32, 64))
    sr_s = unst_s("o_srs"); si_s = unst_s("o_sis")
    c_s = np.stack([R[c]["o_cs"] for c in range(B)], 1)
    c_s = np.ascontiguousarray(c_s.transpose(0, 1, 4, 5, 3, 2).reshape(L, B * 16, 2, 2 * DFF))
    outs = (y_p, y_s, k_p, v_p, sr_p, si_p, c_p, k_s, v_s, sr_s, si_s, c_s)
    return tuple(np.ascontiguousarray(o, dtype=np.float32) for o in outs)
```

```python
import numpy as np
import concourse.bass as bass
import concourse.mybir as mybir

F32 = mybir.dt.float32
BF16 = mybir.dt.bfloat16
I32 = mybir.dt.int32
ALU = mybir.AluOpType
AF = mybir.ActivationFunctionType
AX = mybir.AxisListType

ENGS = ("pe", "act", "dve", "pool", "sp")


class Trk:
    __slots__ = ("name", "w", "rs", "excl")

    def __init__(self, name="", excl=False):
        self.name = name
        self.excl = excl
        self.w = None
        self.rs = []


class Op:
    __slots__ = ("eng", "idx", "fn", "deps", "flag", "dma", "val")

    def __init__(self, eng, idx, fn, deps, dma):
        self.eng, self.idx, self.fn, self.deps, self.dma = eng, idx, fn, deps, dma
        self.flag = False
        self.val = None


class Prog:
    def __init__(self, nc):
        self.nc = nc
        self.ops = {e: [] for e in ENGS}
        self.dsems = []
        self._ctx = []

    def enter(self, cm):
        v = cm.__enter__()
        self._ctx.append(cm)
        return v

    def sbuf(self, name, shape, dt):
        return self.enter(self.nc.sbuf_tensor(name, list(shape), dt))

    def psum(self, name, shape, dt=F32):
        return self.enter(self.nc.psum_tensor(name, list(shape), dt))

    def new_dsem(self, name):
        s = self.enter(self.nc.semaphore(name))
        d = {"sem": s, "cnt": 0}
        self.dsems.append(d)
        return d

    def _deps(self, eng, reads, writes):
        deps = []
        for t in reads:
            if t.w is not None:
                deps.append(t.w)
        for t in writes:
            if t.w is not None:
                deps.append(t.w)
            deps.extend(t.rs)
        return deps

    def op(self, eng, fn, reads=(), writes=()):
        ex = [t for t in reads if t.excl]
        if ex:
            reads = [t for t in reads if not t.excl]
            writes = list(writes) + ex
        deps = self._deps(eng, reads, writes)
        o = Op(eng, len(self.ops[eng]), fn, deps, None)
        self.ops[eng].append(o)
        for t in reads:
            t.rs.append(o)
        for t in writes:
            t.w = o
            t.rs = []
        return o

    def dma(self, q, dsem, out, in_, reads=(), writes=(), **kw):
        deps = self._deps(q, reads, writes)

        def fn(e):
            return e.dma_start(out=out, in_=in_, **kw)
        o = Op(q, len(self.ops[q]), fn, deps, dsem)
        dsem["cnt"] += 16
        o.val = dsem["cnt"]
        o.flag = True
        self.ops[q].append(o)
        for t in reads:
            t.rs.append(o)
        for t in writes:
            t.w = o
            t.rs = []
        return o

    def barrier(self, eng, dsem, fn, trks):
        dep = Op("sp", -1, None, [], dsem)
        dep.val = dsem["cnt"]
        dep.flag = True
        deps = [dep] + [t.w for t in trks if t.w is not None]
        o = Op(eng, len(self.ops[eng]), fn, deps, None)
        self.ops[eng].append(o)
        for t in trks:
            t.w = o
            t.rs = []
        return o

    def finish(self, final_waits=()):
        nc = self.nc
        for e in ENGS:
            for o in self.ops[e]:
                for d in o.deps:
                    if d.dma is None:
                        if d.eng == "pe" and e == "pe":
                            continue
                        d.flag = True
        esem = {}
        for e in ENGS:
            esem[e] = self.enter(nc.semaphore("esem_" + e))
            c = 0
            for o in self.ops[e]:
                if o.dma is None:
                    if o.flag:
                        c += 1
                        o.val = c
                    else:
                        o.val = None
        nxt = {}
        for e in ENGS:
            arr = [None] * len(self.ops[e])
            cur = None
            for i in range(len(self.ops[e]) - 1, -1, -1):
                o = self.ops[e][i]
                if o.dma is None and o.flag:
                    cur = o.val
                arr[i] = cur
            nxt[e] = arr
        self.nwaits = 0
        prog = self

        def emit(ename):
            def body(eh):
                seen = {}
                for o in prog.ops[ename]:
                    need = {}
                    for d in o.deps:
                        if d.dma is not None:
                            key = ("d", id(d.dma))
                            sem, val = d.dma["sem"], d.val
                        else:
                            if d.eng == "pe" and ename == "pe":
                                continue
                            key = ("e", d.eng)
                            sem, val = esem[d.eng], d.val
                            assert val is not None
                        if seen.get(key, 0) >= val:
                            continue
                        if key not in need or need[key][1] < val:
                            need[key] = (sem, val)
                    for key, (sem, val) in need.items():
                        eh.wait_ge(sem, val)
                        seen[key] = val
                        prog.nwaits += 1
                    if o.fn is None:
                        continue
                    ins = o.fn(eh)
                    if o.dma is not None:
                        ins.then_inc(o.dma["sem"], 16)
                    elif o.flag:
                        ins.then_inc(esem[ename], 1)
                if ename == "sp":
                    for d in prog.dsems:
                        if d["cnt"] > 0:
                            eh.wait_ge(d["sem"], d["cnt"])
            return body

        with nc.Block() as block:
            block.tensor(emit("pe"))
            block.scalar(emit("act"))
            block.vector(emit("dve"))
            block.gpsimd(emit("pool"))
            block.sync(emit("sp"))
        for cm in reversed(self._ctx):
            cm.__exit__(None, None, None)
        self._ctx = []

import math
from concourse.bass_utils import run_bass_kernel_spmd

D = 1024; L = 2; SEQ = 4096; NB = 16; NT = 256; TLP = 64; NTI = 2; DFF = 2688; MT_UP = 42
NEG = -30000.0
EPS = 1e-5
PI = math.pi


def build_program(n_batches=NB, do_sample=True):
    nc = bass.Bass("TRN2", target_bir_lowering=False)
    P = Prog(nc)

    def din(name, shape):
        return nc.dram_tensor(name, list(shape), F32, kind="ExternalInput").ap()

    def dout(name, shape):
        return nc.dram_tensor(name, list(shape), F32, kind="ExternalOutput").ap()

    xpT = din("xpT", [128, 8, SEQ]); xsT = din("xsT", [128, 8, 64])
    ck = din("ck", [L, 128, 16, 128]); cv = din("cv", [L, 128, 16, 128])
    ckr = din("ckr", [L, 16, 128, 128]); cvr = din("cvr", [L, 16, 128, 128])
    sre = din("sre", [L, 128, 16, 16]); sim = din("sim", [L, 128, 16, 16])
    sconv = din("sconv", [L, 128, MT_UP, 16, 2])
    w_in = din("w_in", [L, 128, 8, 1280])
    w_outa = din("w_outa", [L, 64, 8, 1024]); w_outb = din("w_outb", [L, 128, 4, 1024])
    w_glu = din("w_glu", [L, 128, 4, 512])
    w_up = din("w_up", [L, 128, 8, 2 * DFF]); w_down = din("w_down", [L, 128, 21, 1024])
    g_mix = din("g_mix", [128, L, 8]); g_ffn = din("g_ffn", [128, L, 8]); g_fin = din("g_fin", [128, 8])
    g_attn = din("g_attn", [64, L, 8]); g_ssm = din("g_ssm", [128, L, 4])
    b_glu = din("b_glu", [128, L, 4]); d_skip = din("d_skip", [128, L, 4])
    cw = din("cw", [128, L, MT_UP, 3]); cb = din("cb", [128, L, MT_UP])
    sinkP = din("sinkP", [64, L, 8])
    lamre = din("lamre", [128, L, 16]); lamim = din("lamim", [128, L, 16]); lstep = din("lstep", [128, L, 16])
    bre = din("bre", [128, L, 16, 16]); bim = din("bim", [128, L, 16, 16])
    cre = din("cre", [128, L, 16, 16]); cim = din("cim", [128, L, 16, 16])
    c_ident = din("c_ident", [128, 128]); c_ones = din("c_ones", [128, 128])
    c_biasP = din("c_biasP", [128, 2, 2, 512]); c_biasSc = din("c_biasSc", [128, 2, 256]); c_biasSn = din("c_biasSn", [4, 2, 256])

    o_ypT = dout("o_ypT", [128, 8, SEQ]); o_ysT = dout("o_ysT", [128, 8, 64])
    o_kp = dout("o_kp", [L, 128, 128]); o_vp = dout("o_vp", [L, 128, 128])
    o_srp = dout("o_srp", [L, 128, 16]); o_sip = dout("o_sip", [L, 128, 16])
    o_cp = dout("o_cp", [L, 128, MT_UP, 2])
    o_ks = dout("o_ks", [L, 16, 128, 128]); o_vs = dout("o_vs", [L, 16, 128, 128])
    o_srs = dout("o_srs", [L, 128, 16, 16]); o_sis = dout("o_sis", [L, 128, 16, 16])
    o_cs = dout("o_cs", [L, 128, MT_UP, 16, 2])

    def dscr(name, shape):
        return nc.dram_tensor(name, list(shape), BF16, kind="Internal").ap()
    wb_in = dscr("wb_in", [L, 128, 8, 1280]); wb_outa = dscr("wb_outa", [L, 64, 8, 1024]); wb_outb = dscr("wb_outb", [L, 128, 4, 1024])
    wb_glu = dscr("wb_glu", [L, 128, 4, 512]); wb_up = dscr("wb_up", [L, 128, 8, 2 * DFF]); wb_down = dscr("wb_down", [L, 128, 21, 1024])

    def T(n=1):
        return [Trk() for _ in range(n)] if n > 1 else Trk()

    def act(out, in_, func, reads, writes, bias=None, scale=None):
        kw = {}
        if bias is not None:
            kw["bias"] = bias
        if scale is not None:
            kw["scale"] = scale
        return P.op("act", lambda e: e.activation(out, in_, func, **kw), reads, writes)

    def tt(eng, out, a, b, op, reads, writes):
        return P.op(eng, lambda e: e.tensor_tensor(out, a, b, op), reads, writes)

    def ts(eng, out, a, s1, s2, op0, op1, reads, writes):
        return P.op(eng, lambda e: e.tensor_scalar(out, a, s1, s2, op0, op1), reads, writes)

    def stt(out, a, s, b, op0, op1, reads, writes):
        return P.op("dve", lambda e: e.scalar_tensor_tensor(out, a, s, b, op0, op1), reads, writes)

    def cp(eng, out, in_, reads, writes):
        if eng == "act":
            return P.op("act", lambda e: e.activation(out, in_, AF.Copy), reads, writes)
        return P.op(eng, lambda e: e.tensor_copy(out, in_), reads, writes)

    def mm(out, lhsT, rhs, start, stop, reads, writes):
        return P.op("pe", lambda e: e.matmul(out, lhsT, rhs, start=start, stop=stop), reads, writes)

    def mset(eng, ap, val, writes):
        return P.op(eng, lambda e: e.memset(ap, val), (), writes)

    def recip(out, in_, reads, writes):
        return P.op("dve", lambda e: e.reciprocal(out, in_), reads, writes)

    d_pre = P.new_dsem("d_pre"); d_kc = P.new_dsem("d_kc"); d_vc = P.new_dsem("d_vc")
    d_kvn = [P.new_dsem("d_kvn0"), P.new_dsem("d_kvn1")]; d_so = [P.new_dsem("d_so0"), P.new_dsem("d_so1")]
    d_out = P.new_dsem("d_out")

    d_pre2 = P.new_dsem("d_pre2")

    def load(dst, src, trk, q="sp"):
        return P.dma(q, d_pre if q == "sp" else d_pre2, dst, src, writes=[trk])

    ident_f = P.sbuf("ident_f", [128, 128], F32); ident_b = P.sbuf("ident_b", [128, 128], BF16)
    ones_b = P.sbuf("ones_b", [128, 128], BF16)
    biasP = P.sbuf("biasP", [128, 2, 2, 512], BF16)
    biasSc = P.sbuf("biasSc", [128, 2, 256], BF16); biasSn = P.sbuf("biasSn", [4, 2, 256], BF16)
    tconst = T()
    load(ident_f[:], c_ident, tconst)
    load(ident_b[:], c_ident, tconst, "pool"); load(ones_b[:], c_ones, tconst, "pool")
    load(biasP[:], c_biasP, tconst, "pool"); load(biasSc[:], c_biasSc, tconst, "pool"); load(biasSn[:], c_biasSn, tconst, "pool")
    gmix = P.sbuf("gmix", [128, L, 8], F32); gffn = P.sbuf("gffn", [128, L, 8], F32); gfin = P.sbuf("gfin", [128, 8], F32)
    gattn = P.sbuf("gattn", [64, L, 8], F32); gssm = P.sbuf("gssm", [128, L, 4], F32)
    bglu = P.sbuf("bglu", [128, L, 4], F32); hbglu = P.sbuf("hbglu", [128, L, 4], F32); dsk = P.sbuf("dsk", [128, L, 4], F32)
    cws = P.sbuf("cws", [128, L, MT_UP, 3], F32); cbs = P.sbuf("cbs", [128, L, MT_UP], F32)
    esP = P.sbuf("esP", [64, L, 8], F32)
    for dst, src in ((gmix, g_mix), (gffn, g_ffn), (gfin, g_fin), (gattn, g_attn), (gssm, g_ssm), (bglu, b_glu),
                     (dsk, d_skip), (cws, cw), (cbs, cb), (esP, sinkP)):
        load(dst[:], src, tconst)
    epsT = P.sbuf("epsT", [128, 1], F32); hpiT = P.sbuf("hpiT", [128, 1], F32)
    P.barrier("dve", d_pre, lambda e: e.memset(epsT[:], EPS), [tconst])
    P.barrier("dve", d_pre2, lambda e: e.memset(hpiT[:], PI / 2), [tconst])
    mset("dve", epsT[:], EPS, [tconst]); mset("dve", hpiT[:], PI / 2, [tconst])
    act(esP[:], esP[:], AF.Exp, [tconst], [tconst])
    ts("dve", hbglu[:], bglu[:], 0.5, None, ALU.mult, ALU.bypass, [tconst], [tconst])

    twbs = {}
    for l_ in range(L):
        for nm_, dst_, src_, kt_ in (("in", wb_in, w_in, 8), ("outa", wb_outa, w_outa, 8), ("outb", wb_outb, w_outb, 4),
                                     ("glu", wb_glu, w_glu, 4), ("up", wb_up, w_up, 8), ("down", wb_down, w_down, 21)):
            d_cvt = P.new_dsem("d_cvt_%s%d" % (nm_, l_)); t_ = Trk()
            twbs[(nm_, l_)] = t_
            for k_ in range(kt_):
                P.dma("pool", d_cvt, dst_[l_, :, k_, :], src_[l_, :, k_, :], writes=[t_])


    XT = lambda n=1: [Trk(excl=True) for _ in range(n)] if n > 1 else Trk(excl=True)
    psA = [P.psum("psA%d" % i, [128, 512]) for i in range(2)]; tA = XT(2)
    psS = [P.psum("psS%d" % i, [128, 512]) for i in range(2)]; tS = XT(2)
    psO = P.psum("psO", [128, 512]); tO = XT()
    psD = P.psum("psD", [128, 512]); tD = XT()
    psM = [P.psum("psM%d" % i, [128, 512]) for i in range(2)]; tM = XT(2)
    cntM = [0]

    def nextM():
        i = cntM[0] % 2
        cntM[0] += 1
        return psM[i], tM[i]

    s5 = []
    scr = [P.sbuf("s5scr%d" % i, [128, 16], F32) for i in range(8)]
    G12 = P.sbuf("G12", [128, 8, NT], F32)
    bmf = G12[:].rearrange("p a (b c) -> p (a b) c", c=128)
    S5tmp = P.sbuf("S5tmp", [128, 4, 16, TLP], F32); ttmp = T(); ttq = T(4)
    braw = S5tmp[:, 0].rearrange("p g t -> p (g t)")[:, 0:512].rearrange("p (a g h) -> p a g h", a=2, g=16)
    bbar = S5tmp[:, 1].rearrange("p g t -> p (g t)")[:, 0:512].rearrange("p (a g h) -> p a g h", a=2, g=16)
    craw = S5tmp[:, 2].rearrange("p g t -> p (g t)")[:, 0:512].rearrange("p (a g h) -> p a g h", a=2, g=16)
    tab = T()
    for l in range(L):
        lr = P.sbuf("lr%d" % l, [128, 16], F32); li = P.sbuf("li%d" % l, [128, 16], F32); ls = P.sbuf("ls%d" % l, [128, 16], F32)
        d_tab = P.new_dsem("d_tab%d" % l)
        for dst_, src_ in ((lr[:], lamre[:, l, :]), (li[:], lamim[:, l, :]), (ls[:], lstep[:, l, :]), (braw[:, 0], bre[:, l]),
                           (braw[:, 1], bim[:, l]), (craw[:, 0], cre[:, l]), (craw[:, 1], cim[:, l])):
            P.dma("sp", d_tab, dst_, src_, writes=[tab])
        P.barrier("dve", d_tab, (lambda l_: lambda e: e.memset(scr[0][:], 0.0))(l), [tab])
        AR = P.sbuf("AR%d" % l, [128, 16], F32); AI = P.sbuf("AI%d" % l, [128, 16], F32)
        AR16 = P.sbuf("AR16_%d" % l, [128, 16, 16], F32); AI16 = P.sbuf("AI16_%d" % l, [128, 16, 16], F32)
        BT = [P.sbuf("BT%d_%d" % (l, ri), [128, 16, 128], BF16) for ri in range(2)]
        CT = [P.sbuf("CT%d_%d" % (l, ri), [128, 16, 128], BF16) for ri in range(2)]
        dt_, zr, th, rr, cc, ss, t0, t1 = [s[:] for s in scr]
        R, W = [tab], [tab]
        act(dt_, ls[:], AF.Exp, R, W)
        tt("dve", zr, lr[:], dt_, ALU.mult, R, W)
        tt("dve", th, li[:], dt_, ALU.mult, R, W)
        act(rr, zr, AF.Exp, R, W)
        act(cc, th, AF.Sin, R, W, bias=hpiT[:], scale=1.0 / 32)
        act(ss, th, AF.Sin, R, W, scale=1.0 / 32)
        for _ in range(5):
            tt("dve", t0, cc, cc, ALU.mult, R, W)
            tt("dve", t1, ss, ss, ALU.mult, R, W)
            tt("dve", ss, ss, cc, ALU.mult, R, W)
            ts("dve", ss, ss, 2.0, None, ALU.mult, ALU.bypass, R, W)
            tt("dve", cc, t0, t1, ALU.subtract, R, W)
        tt("dve", AR[:], rr, cc, ALU.mult, R, W)
        tt("dve", AI[:], rr, ss, ALU.mult, R, W)
        for s_ in range(16):
            cp("dve", AR16[:, :, s_], AR[:], R, W); cp("dve", AI16[:, :, s_], AI[:], R, W)
        RR = P.sbuf("RR%d" % l, [128, 16], F32)
        cp("dve", RR[:], rr, R, W)
        C1 = P.sbuf("C1_%d" % l, [128, 16, TLP], F32); S1 = P.sbuf("S1_%d" % l, [128, 16, TLP], F32)
        cp("dve", C1[:, :, 0], cc, R, W); cp("dve", S1[:, :, 0], ss, R, W)
        kk_ = 1
        while kk_ < TLP:
            cp("dve", t0, C1[:, :, kk_ - 1], R, W); cp("dve", t1, S1[:, :, kk_ - 1], R, W)
            ts("dve", zr, t1, -1.0, None, ALU.mult, ALU.bypass, R, W)
            for gp in range(16):
                ts("dve", C1[:, gp, kk_:2 * kk_], C1[:, gp, 0:kk_], t0[:, gp:gp + 1], None, ALU.mult, ALU.bypass, R, W)
                stt(C1[:, gp, kk_:2 * kk_], S1[:, gp, 0:kk_], zr[:, gp:gp + 1], C1[:, gp, kk_:2 * kk_], ALU.mult, ALU.add, R, W)
                ts("dve", S1[:, gp, kk_:2 * kk_], S1[:, gp, 0:kk_], t0[:, gp:gp + 1], None, ALU.mult, ALU.bypass, R, W)
                stt(S1[:, gp, kk_:2 * kk_], C1[:, gp, 0:kk_], t1[:, gp:gp + 1], S1[:, gp, kk_:2 * kk_], ALU.mult, ALU.add, R, W)
            kk_ *= 2
        nr, den, cr, ci = dt_, zr, th, rr
        ts("dve", nr, AR[:], -1.0, None, ALU.add, ALU.bypass, R, W)
        tt("dve", t0, lr[:], lr[:], ALU.mult, R, W)
        tt("dve", t1, li[:], li[:], ALU.mult, R, W)
        tt("dve", den, t0, t1, ALU.add, R, W)
        recip(den, den, R, W)
        tt("dve", t0, nr, lr[:], ALU.mult, R, W)
        tt("dve", t1, AI[:], li[:], ALU.mult, R, W)
        tt("dve", t0, t0, t1, ALU.add, R, W)
        tt("dve", cr, t0, den, ALU.mult, R, W)
        tt("dve", t0, AI[:], lr[:], ALU.mult, R, W)
        tt("dve", t1, nr, li[:], ALU.mult, R, W)
        tt("dve", t0, t0, t1, ALU.subtract, R, W)
        tt("dve", ci, t0, den, ALU.mult, R, W)
        nci = cc
        ts("dve", nci, ci, -1.0, None, ALU.mult, ALU.bypass, R, W)
        for gp in range(16):
            ts("dve", bbar[:, 0, gp, :], braw[:, 0, gp, :], cr[:, gp:gp + 1], None, ALU.mult, ALU.bypass, R, W)
            stt(bbar[:, 0, gp, :], braw[:, 1, gp, :], nci[:, gp:gp + 1], bbar[:, 0, gp, :], ALU.mult, ALU.add, R, W)
            ts("dve", bbar[:, 1, gp, :], braw[:, 1, gp, :], cr[:, gp:gp + 1], None, ALU.mult, ALU.bypass, R, W)
            stt(bbar[:, 1, gp, :], braw[:, 0, gp, :], ci[:, gp:gp + 1], bbar[:, 1, gp, :], ALU.mult, ALU.add, R, W)
        for ri in range(2):
            mset("dve", bmf[:], 0.0, W)
            for gp in range(16):
                c0 = 32 * (gp % 4)
                cp("dve", bmf[0:64, gp, c0:c0 + 16], bbar[0:64, ri, gp, :], R, W)
                cp("dve", bmf[64:128, gp, c0 + 16:c0 + 32], bbar[64:128, ri, gp, :], R, W)
            for gp in range(16):
                pm, tm = nextM()
                P.op("pe", (lambda pm_, gp_: lambda e: e.transpose(pm_[:, 0:128], bmf[:, gp_, :], ident_f[:]))(pm, gp),
                     [tab, tconst], [tm])
                cp("act", BT[ri][:, gp, :], pm[:, 0:128], [tm], [tab])
            mset("dve", CT[ri][:], 0.0, W)
            for gp in range(16):
                c0 = 32 * (gp % 4)
                sc = 1.0 if ri == 0 else -1.0
                ts("dve", CT[ri][0:64, gp, c0:c0 + 16], craw[0:64, ri, gp, :], sc, None, ALU.mult, ALU.bypass, R, W)
                ts("dve", CT[ri][64:128, gp, c0 + 16:c0 + 32], craw[64:128, ri, gp, :], sc, None, ALU.mult, ALU.bypass, R, W)
        s5.append(dict(AR=AR, AI=AI, AR16=AR16, AI16=AI16, BT=BT, CT=CT, RR=RR, C1=C1, S1=S1))

    xT = P.sbuf("xT", [128, 8, NT], F32); tx = T(8)
    hT = P.sbuf("hT", [128, 8, NT], BF16); th_ = T(8)
    rstd = P.sbuf("rstd", [128, NT], F32); trs = T()
    qT = P.sbuf("qT", [64, 8, NT], BF16); tq = T()
    kT = [P.sbuf("kT%d" % l, [64, 2, 128 + NT], BF16) for l in range(L)]; tk = T(2)
    vtok = [P.sbuf("vtok%d" % l, [128, NTI + 1, 128], BF16) for l in range(L)]; tv = T(2)
    kvf = P.sbuf("kvf", [128, 256], F32); tkvf = T()
    wkv = P.sbuf("wkv", [128, 8, 256], BF16); twkv = T(); d_wkv = P.new_dsem("d_wkv")
    uT = P.sbuf("uT", [128, 4, NT], BF16); tu = T()
    PT = P.sbuf("PT", [128, 2, 512], BF16); tPT = T()
    den_sb = P.sbuf("den_sb", [64, 512], F32); tden = T()
    attnT = P.sbuf("attnT", [64, 8, NT], F32); tat = T()
    attnB = P.sbuf("attnB", [64, 8, NT], BF16); tatb = T()
    bu = [P.sbuf("bu%d" % ri, [128, 16, 64], F32) for ri in range(2)]; tbu = T(2)
    xs5 = [P.sbuf("xs5_%d" % ri, [128, 16, 64], BF16) for ri in range(2)]; txs = T(2)
    Xs_ = [[P.sbuf("Xs_%d_%d" % (ri, pp), [128, 16, 16], F32) for pp in range(2)] for ri in range(2)]
    tXs_ = [[T() for pp in range(2)] for ri in range(2)]
    Xst = [Xs_, Xs_]; tX = [tXs_, tXs_]
    Xp = [[P.sbuf("Xp%d_%d" % (l, ri), [128, 16], F32) for ri in range(2)] for l in range(L)]
    tXp = [[T() for ri in range(2)] for l in range(L)]
    stmp = [P.sbuf("stmp%d" % i, [128, 16, 16], F32) for i in range(4)]; tst = T(4)
    yT = P.sbuf("yT", [128, 4, NT], F32); ty = T()
    tg1 = T(); tg2 = T()
    g1 = G12[:, 0:4, :]; g2 = G12[:, 4:8, :]
    sbf = P.sbuf("sbf", [128, 4, NT], BF16); tsb = T()
    ssmB = P.sbuf("ssmB", [128, 4, NT], BF16); tsmb = T()
    upb2 = [P.sbuf("upb%d" % i, [128, 2, NT + 32], F32) for i in range(2)]; tup2 = [T(2), T(2)]
    cs2 = [P.sbuf("cs_%d" % i, [128, 2, NT], F32) for i in range(2)]; tcs2 = [T(2), T(2)]; tgs = T(2)
    hid = P.sbuf("hid", [128, 21, NT], BF16); thid = T()
    sq = hid[:, 0:8, :]; tsq = thid
    carry = [P.sbuf("carry%d" % l, [128, MT_UP, 2], F32) for l in range(L)]; tcar = [T(MT_UP), T(MT_UP)]
    scarry1 = P.sbuf("scarry", [128, MT_UP, 32], F32); scarry = [scarry1, scarry1]
    kc_f = S5tmp[:, 0:2].rearrange("p a g t -> p (a g t)").rearrange("p (s c) -> p s c", c=128); tkc = ttmp
    kcT = P.sbuf("kcT", [64, 16, 128], BF16); tkcT = T()
    vc_b = P.sbuf("vc_b", [128, 16, 128], BF16); tvc = T()
    kvn_f = P.sbuf("kvn_f", [4, 2, 256], F32); tkvn = T(2)
    vn_b = P.sbuf("vn_b", [4, 16, 128], BF16); tvn = T()
    yout = G12; tyo = T()

    NSLOT = 3
    wsl = [P.sbuf("wsl%d" % i, [128, 12, 128], BF16) for i in range(NSLOT)]
    twsl = T(NSLOT); dwsl = [P.new_dsem("dw%d" % i) for i in range(NSLOT)]
    wcnt = [0]

    def linear(nblk, load_fn, mm_fn, cb_fn, rtrks, group=1, wtrk=()):
        base = wcnt[0]
        wcnt[0] += nblk

        def issue(b):
            si = (base + b) % NSLOT
            for dst, src in load_fn(b, wsl[si]):
                P.dma("sp", dwsl[si], dst, src, reads=list(wtrk), writes=[twsl[si]])
        for b in range(min(NSLOT - 1, nblk)):
            issue(b)
        for b in range(nblk):
            if b + NSLOT - 1 < nblk:
                issue(b + NSLOT - 1)
            si = (base + b) % NSLOT
            ps, tps = psA[(b // group) % 2], tA[(b // group) % 2]
            pairs = mm_fn(b, wsl[si])
            out_ap = pairs[0][2]
            for i, (lt, rh, _) in enumerate(pairs):
                mm(out_ap(ps), lt, rh, i == 0 and b % group == 0, i == len(pairs) - 1 and b % group == group - 1,
                   [twsl[si]] + rtrks, [tps])
            if b % group == group - 1:
                cb_fn(b // group, ps, tps)

    def rmsnorm(src, tsrc, nk, npart, gain_fn, ntok, dst_fn, tdst, nfeat):
        act(sq[0:npart, 0:nk, 0:ntok], src[0:npart, 0:nk, 0:ntok], AF.Square, tsrc, [tsq])
        pm, tm = nextM()
        for k in range(nk):
            mm(pm[:, 0:ntok], ones_b[0:npart, :], sq[0:npart, k, 0:ntok], k == 0, k == nk - 1, [tsq, tconst], [tm])
        act(rstd[:, 0:ntok], pm[:, 0:ntok], AF.Sqrt, [tm, tconst], [trs], bias=epsT[:], scale=1.0 / nfeat)
        recip(rstd[:, 0:ntok], rstd[:, 0:ntok], [trs], [trs])
        for k in range(nk):
            stt(dst_fn(k), src[0:npart, k, 0:ntok], gain_fn(k), rstd[0:npart, 0:ntok], ALU.mult, ALU.mult,
                tsrc + [trs, tconst], tdst)

    def run_layer(l, grp):
        sample = grp["sample"]
        ntok = grp["ntok"]; NS = grp["NS"]; TL = grp["TL"]
        bidx = grp.get("b", 0)
        tb = s5[l]
        rmsnorm(xT, tx, 8, 128, lambda k: gmix[:, l, k:k + 1], ntok, lambda k: hT[:, k, 0:ntok], th_, D)
        blocks = [(h * 64, 64) for h in range(8)] + [(512 + kv * 64, 64) for kv in range(2)] + [(768 + q * 128, 128) for q in range(4)]
        koff = 0 if sample else 128

        def w_in_load(b, slot):
            c0, msz = blocks[b]
            return [(slot[:, 0:8, 0:msz], wb_in[l, :, :, c0:c0 + msz])]

        def w_in_mm(b, slot):
            c0, msz = blocks[b]
            return [(slot[:, k, 0:msz], hT[:, k, 0:ntok], (lambda ps, msz=msz: ps[0:msz, 0:ntok])) for k in range(8)]

        def w_in_cb(b, ps, tps):
            if b < 8:
                act(qT[:, b, 0:ntok], ps[0:64, 0:ntok], AF.Copy, [tps], [tq], scale=0.125)
            elif b < 10:
                cp("act", kT[l][:, b - 8, koff:koff + ntok], ps[0:64, 0:ntok], [tps], [tk[l]])
            else:
                cp("act", uT[:, b - 10, 0:ntok], ps[:, 0:ntok], [tps], [tu])
        linear(14, w_in_load, w_in_mm, w_in_cb, th_, wtrk=[twbs[("in", l)]])
        P.dma("sp", d_wkv, wkv[:], wb_in[l, :, :, 512:768], reads=[twbs[("in", l)]], writes=[twkv])
        if not sample:
            for i in range(NTI):
                pm, tm = nextM()
                for k in range(8):
                    mm(pm[:, 0:256], hT[:, k, i * 128:(i + 1) * 128], wkv[:, k, :], k == 0, k == 7, th_ + [twkv], [tm])
                cp("act", vtok[l][:, i + 1, :], pm[:, 128:256], [tm], [tv[l]])
                if bidx == n_batches - 1 and i == NTI - 1:
                    cp("dve", kvf[:], pm[:, 0:256], [tm], [tkvf])
                    P.dma("pool", d_out, o_kp[l], kvf[:, 0:128], reads=[tkvf])
                    P.dma("pool", d_out, o_vp[l], kvf[:, 128:256], reads=[tkvf])
        else:
            for s_ in range(16):
                pm, tm = nextM()
                for k in range(8):
                    mm(pm[0:4, 0:256], hT[:, k, 4 * s_:4 * s_ + 4], wkv[:, k, :], k == 0, k == 7, th_ + [twkv], [tm])
                cp("act", kvn_f[:, s_ % 2, :], pm[0:4, 0:256], [tm], [tkvn[s_ % 2]])
                cp("dve", vn_b[:, s_, :], kvn_f[:, s_ % 2, 128:256], [tkvn[s_ % 2]], [tvn])
                P.dma("pool", d_kvn[s_ % 2], o_ks[l, s_, 124:128, :], kvn_f[:, s_ % 2, 0:128], reads=[tkvn[s_ % 2]])
                P.dma("pool", d_kvn[s_ % 2], o_vs[l, s_, 124:128, :], kvn_f[:, s_ % 2, 128:256], reads=[tkvn[s_ % 2]])
            P.dma("pool", d_out, o_ks[l, :, 0:124, :], ckr[l, :, 4:128, :])
            P.dma("pool", d_out, o_vs[l, :, 0:124, :], cvr[l, :, 4:128, :])
        attn_units = []
        if not sample:
            def mk_unit(i, kv):
                first = (bidx == 0 and i == 0)
                blks = [1] if first else [0, 1]

                def p1():
                    for blk in blks:
                        kcol = i * 128 + blk * 128
                        mm(psS[blk][:, :], kT[l][:, kv, kcol:kcol + 128], qT[:, 4 * kv:4 * kv + 4, i * 128:(i + 1) * 128],
                           True, False, [tk[l], tq], [tS[blk]])
                        mm(psS[blk][:, :], ident_b[:], biasP[:, blk, kv, :], False, True, [tconst], [tS[blk]])
                        act(PT[:, blk, :], psS[blk][:, :], AF.Exp, [tS[blk]], [tPT])
                    for j, blk in enumerate(blks):
                        mm(psO[0:64, :], vtok[l][:, i + blk, kv * 64:(kv + 1) * 64], PT[:, blk, :], j == 0, j == len(blks) - 1,
                           [tv[l], tPT], [tO])
                    for j, blk in enumerate(blks):
                        mm(psD[0:64, :], ones_b[:, 0:64], PT[:, blk, :], j == 0, j == len(blks) - 1, [tconst, tPT], [tD])

                def p2():
                    for g_ in range(4):
                        ts("dve", den_sb[:, g_ * 128:(g_ + 1) * 128], psD[0:64, g_ * 128:(g_ + 1) * 128], esP[:, l, 4 * kv + g_:4 * kv + g_ + 1], None,
                           ALU.add, ALU.bypass, [tD, tconst], [tden])
                    recip(den_sb[:, :], den_sb[:, :], [tden], [tden])
                    tt("dve", attnT[:, 4 * kv:4 * kv + 4, i * 128:(i + 1) * 128], psO[0:64, :].rearrange("p (g q) -> p g q", g=4),
                       den_sb[:, :].rearrange("p (g q) -> p g q", g=4), ALU.mult, [tO, tden], [tat])
                return p1, p2
            for i in range(NTI):
                for kv in range(2):
                    attn_units.append(mk_unit(i, kv))
        if sample:
            pass
        else:
            pass
        if not sample:
            pass
        else:
            P.dma("sp", d_kc, kc_f[:], ck[l], writes=[tkc, ttq[0], ttq[1]])
            P.dma("pool", d_vc, vc_b[:], cv[l], writes=[tvc])
            for kv in range(2):
                for s_ in range(16):
                    pm, tm = nextM()
                    P.op("pe", (lambda pm_, s2, kv2: lambda e: e.transpose(pm_[0:64, 0:128], kc_f[:, s2, kv2 * 64:(kv2 + 1) * 64], ident_f[:]))(pm, s_, kv),
                         [tkc, tconst], [tm])
                    cp("act", kcT[:, s_, :], pm[0:64, 0:128], [tm], [tkcT])
                mm(psS[0][:, 0:256], ident_b[:], biasSc[:, kv, :], True, False, [tconst], [tS[0]])
                for s_ in range(16):
                    mm(psS[0][:, 16 * s_:16 * s_ + 16], kcT[:, s_, :], qT[:, 4 * kv:4 * kv + 4, 4 * s_:4 * s_ + 4],
                       False, s_ == 15, [tkcT, tq], [tS[0]])
                mm(psS[1][0:4, 0:256], ident_b[0:4, 0:4], biasSn[:, kv, :], True, False, [tconst], [tS[1]])
                for s_ in range(16):
                    mm(psS[1][0:4, 16 * s_:16 * s_ + 16], kT[l][:, kv, 4 * s_:4 * s_ + 4], qT[:, 4 * kv:4 * kv + 4, 4 * s_:4 * s_ + 4],
                       False, s_ == 15, [tk[l], tq], [tS[1]])
                act(PT[:, 0, 0:256], psS[0][:, 0:256], AF.Exp, [tS[0]], [tPT])
                act(PT[0:4, 1, 0:256], psS[1][0:4, 0:256], AF.Exp, [tS[1]], [tPT])
                for s_ in range(16):
                    c = slice(16 * s_, 16 * s_ + 16)
                    mm(psO[0:64, c], vc_b[:, s_, kv * 64:(kv + 1) * 64], PT[:, 0, c], True, False, [tvc, tPT], [tO])
                    mm(psO[0:64, c], vn_b[:, s_, kv * 64:(kv + 1) * 64], PT[0:4, 1, c], False, True, [tvn, tPT], [tO])
                for s_ in range(16):
                    c = slice(16 * s_, 16 * s_ + 16)
                    mm(psD[0:64, c], ones_b[:, 0:64], PT[:, 0, c], True, False, [tconst, tPT], [tD])
                    mm(psD[0:64, c], ones_b[0:4, 0:64], PT[0:4, 1, c], False, True, [tconst, tPT], [tD])
                for g_ in range(4):
                    ts("dve", den_sb[:, 0:256].rearrange("p (s g t) -> p g s t", g=4, t=4)[:, g_],
                       psD[0:64, 0:256].rearrange("p (s g t) -> p g s t", g=4, t=4)[:, g_], esP[:, l, 4 * kv + g_:4 * kv + g_ + 1], None,
                       ALU.add, ALU.bypass, [tD, tconst], [tden])
                recip(den_sb[:, 0:256], den_sb[:, 0:256], [tden], [tden])
                tt("dve", attnT[:, 4 * kv:4 * kv + 4, 0:64].rearrange("p g (s t) -> p s g t", t=4),
                   psO[0:64, 0:256].rearrange("p (s g t) -> p s g t", g=4, t=4),
                   den_sb[:, 0:256].rearrange("p (s g t) -> p s g t", g=4, t=4), ALU.mult, [tO, tden], [tat])
        NTL = NS * TL
        ntile = ntok // NTL

        def s5_bu(it):
            tok0 = it * NTL
            kk = 0
            for ri in range(2):
                for qd in range(4):
                    ps, tps = psA[kk % 2], tA[kk % 2]
                    kk += 1
                    for r4 in range(4):
                        gp = 4 * qd + r4
                        mm(ps[:, r4 * NTL:(r4 + 1) * NTL], tb["BT"][ri][:, gp, :], uT[:, qd, tok0:tok0 + NTL], True, True,
                           [tab, tu], [tps])
                    cp("act", bu[ri][:, 4 * qd:4 * qd + 4, 0:NTL], ps[:, 0:4 * NTL].rearrange("p (g t) -> p g t", t=NTL),
                       [tps], [tbu[ri]])

        def s5_y(it):
            tok0 = it * NTL
            for qd in range(4):
                pm, tm = nextM()
                for r4 in range(4):
                    gp = 4 * qd + r4
                    mm(pm[:, 0:NTL], tb["CT"][0][:, gp, :], xs5[0][:, gp, 0:NTL], r4 == 0, False, [tab, txs[0]], [tm])
                    mm(pm[:, 0:NTL], tb["CT"][1][:, gp, :], xs5[1][:, gp, 0:NTL], False, r4 == 3, [tab, txs[1]], [tm])
                stt(yT[:, qd, tok0:tok0 + NTL], uT[:, qd, tok0:tok0 + NTL], dsk[:, l, qd:qd + 1], pm[:, 0:NTL], ALU.mult, ALU.add,
                    [tu, tconst, tm], [ty])

        if sample:
            it = 0
            tok0 = 0
            s5_bu(0)
            ARt = tb["AR16"][:, :, 0:NS]; AIt = tb["AI16"][:, :, 0:NS]
            for t in range(TL):
                stp = grp["step"]
                cur, nxt = stp % 2, 1 - stp % 2
                grp["step"] += 1
                Xr_c, Xi_c = Xst[l][0][cur][:, :, 0:NS], Xst[l][1][cur][:, :, 0:NS]
                Xr_n, Xi_n = Xst[l][0][nxt][:, :, 0:NS], Xst[l][1][nxt][:, :, 0:NS]
                tXr_c, tXi_c, tXr_n, tXi_n = tX[l][0][cur], tX[l][1][cur], tX[l][0][nxt], tX[l][1][nxt]
                bur = bu[0][:, :, 0:NTL].rearrange("p g (s t) -> p g s t", t=TL)[:, :, :, t]
                bui = bu[1][:, :, 0:NTL].rearrange("p g (s t) -> p g s t", t=TL)[:, :, :, t]
                a0, a1, a2, a3 = [s[:, :, 0:NS] for s in stmp]
                tt("dve", a0, Xr_c, ARt, ALU.mult, [tXr_c, tab], [tst[0]])
                tt("dve", a1, Xi_c, AIt, ALU.mult, [tXi_c, tab], [tst[1]])
                tt("dve", a0, a0, a1, ALU.subtract, [tst[0], tst[1]], [tst[0]])
                tt("dve", Xr_n, a0, bur, ALU.add, [tst[0], tbu[0]], [tXr_n])
                tt("dve", a2, Xr_c, AIt, ALU.mult, [tXr_c, tab], [tst[2]])
                tt("dve", a3, Xi_c, ARt, ALU.mult, [tXi_c, tab], [tst[3]])
                tt("dve", a2, a2, a3, ALU.add, [tst[2], tst[3]], [tst[2]])
                tt("dve", Xi_n, a2, bui, ALU.add, [tst[2], tbu[1]], [tXi_n])
                xr_o = xs5[0][:, :, 0:NTL].rearrange("p g (s t) -> p g s t", t=TL)[:, :, :, t]
                xi_o = xs5[1][:, :, 0:NTL].rearrange("p g (s t) -> p g s t", t=TL)[:, :, :, t]
                cp("act", xr_o, Xr_n, [tXr_n], [txs[0]])
                cp("act", xi_o, Xi_n, [tXi_n], [txs[1]])
            s5_y(0)
        else:
            A_, B_, C_, D_ = S5tmp[:, 0], S5tmp[:, 1], S5tmp[:, 2], S5tmp[:, 3]
            tA_, tB_, tC_, tD_ = ttq
            c1 = tb["C1"][:]; s1 = tb["S1"][:]
            bur = bu[0][:, :, 0:TL]; bui = bu[1][:, :, 0:TL]
            Xcr = Xp[l][0]; Xci = Xp[l][1]; tXcr = tXp[l][0]; tXci = tXp[l][1]
            for it in range(ntile):
                s5_bu(it)
                if it < len(attn_units):
                    attn_units[it][0]()
                tt("dve", A_, c1, bur, ALU.mult, [tab, tbu[0], ttmp], [tA_])
                tt("dve", B_, s1, bui, ALU.mult, [tab, tbu[1], ttmp], [tB_])
                tt("dve", A_, A_, B_, ALU.add, [tB_], [tA_])
                tt("dve", B_, c1, bui, ALU.mult, [tab, tbu[1]], [tB_])
                tt("dve", C_, s1, bur, ALU.mult, [tab, tbu[0]], [tC_])
                tt("dve", B_, B_, C_, ALU.subtract, [tC_], [tB_])
                for gp in range(16):
                    P.op("dve", (lambda gp: lambda e: e.tensor_tensor_scan(
                        A_[:, gp, :], tb["RR"][:, gp:gp + 1].to_broadcast([128, TL]), A_[:, gp, :],
                        Xcr[:, gp:gp + 1], ALU.mult, ALU.add))(gp), [tab, tXcr], [tA_])
                    P.op("dve", (lambda gp: lambda e: e.tensor_tensor_scan(
                        B_[:, gp, :], tb["RR"][:, gp:gp + 1].to_broadcast([128, TL]), B_[:, gp, :],
                        Xci[:, gp:gp + 1], ALU.mult, ALU.add))(gp), [tab, tXci], [tB_])
                if it < len(attn_units):
                    attn_units[it][1]()
                if it > 0:
                    s5_y(it - 1)
                tt("dve", C_, c1, A_, ALU.mult, [tab, tA_], [tC_])
                tt("dve", D_, s1, B_, ALU.mult, [tab, tB_], [tD_])
                tt("dve", C_, C_, D_, ALU.subtract, [tD_], [tC_])
                tt("dve", D_, s1, A_, ALU.mult, [tab, tA_, tC_], [tD_])
                tt("dve", A_, c1, B_, ALU.mult, [tab, tB_], [tA_])
                tt("dve", D_, D_, A_, ALU.add, [tA_], [tD_])
                cp("act", xs5[0][:, :, 0:TL], C_, [tC_], [txs[0]])
                cp("act", xs5[1][:, :, 0:TL], D_, [tD_], [txs[1]])
                cp("act", Xcr[:], C_[:, :, TL - 1], [tC_], [tXcr])
                cp("act", Xci[:], D_[:, :, TL - 1], [tD_], [tXci])
            s5_y(ntile - 1)
        if not sample:
            for u_ in attn_units[ntile:]:
                u_[0](); u_[1]()
            cp("dve", kT[l][:, :, 0:128], kT[l][:, :, NT:NT + 128], [tk[l]], [tk[l]])
            cp("dve", vtok[l][:, 0, :], vtok[l][:, NTI, :], [tv[l]], [tv[l]])
        fin = grp["step"] % 2 if sample else 0
        if sample:
            P.dma("pool", d_so[l], o_srs[l], Xst[l][0][fin][:], reads=[tX[l][0][fin]])
            P.dma("pool", d_so[l], o_sis[l], Xst[l][1][fin][:], reads=[tX[l][1][fin]])
        elif bidx == n_batches - 1:
            P.dma("pool", d_out, o_srp[l], Xp[l][0][:], reads=[tXp[l][0]])
            P.dma("pool", d_out, o_sip[l], Xp[l][1][:], reads=[tXp[l][1]])
        Y = yT[:, :, 0:ntok]; G1 = g1[:, :, 0:ntok]; G2 = g2[:, :, 0:ntok]
        tt("dve", G1, Y, Y, ALU.mult, [ty], [tg1])
        ts("dve", G1, G1, 0.044715, 1.0, ALU.mult, ALU.add, [tg1], [tg1])
        tt("dve", G1, G1, Y, ALU.mult, [tg1, ty], [tg1])
        act(G1, G1, AF.Tanh, [tg1], [tg1], scale=0.7978845608028654)
        ts("dve", G2, Y, 0.5, None, ALU.mult, ALU.bypass, [ty], [tg2])
        stt(G1, G1, 1.0, G2, ALU.add, ALU.mult, [tg1, tg2], [tg1])
        cp("act", sbf[:, :, 0:ntok], G1, [tg1], [tsb])
        ts("dve", G2, G1, 0.5, None, ALU.mult, ALU.bypass, [tg1], [tg2])

        def glu_load(b, slot):
            return [(slot[:, 0:4, :], wb_glu[l, :, :, b * 128:(b + 1) * 128])]

        def glu_mm(b, slot):
            return [(slot[:, k, :], sbf[:, k, 0:ntok], (lambda ps: ps[:, 0:ntok])) for k in range(4)]

        def glu_cb(b, ps, tps):
            act(g1[:, b, 0:ntok], ps[:, 0:ntok], AF.Tanh, [tps, tconst], [tg1], bias=hbglu[:, l, b:b + 1], scale=0.5)
            stt(yT[:, b, 0:ntok], g1[:, b, 0:ntok], 1.0, g2[:, b, 0:ntok], ALU.add, ALU.mult, [tg1, tg2], [ty])
        linear(4, glu_load, glu_mm, glu_cb, [tsb], wtrk=[twbs[("glu", l)]])
        rmsnorm(attnT, [tat], 8, 64, lambda k: gattn[:, l, k:k + 1], ntok, lambda k: attnB[:, k, 0:ntok], [tatb], 512)
        rmsnorm(yT, [ty], 4, 128, lambda k: gssm[:, l, k:k + 1], ntok, lambda k: ssmB[:, k, 0:ntok], [tsmb], 512)

        def wo_load(b, slot):
            return [(slot[0:64, 0:8, :], wb_outa[l, :, :, b * 128:(b + 1) * 128]),
                    (slot[:, 8:12, :], wb_outb[l, :, :, b * 128:(b + 1) * 128])]

        def wo_mm(b, slot):
            o = (lambda ps: ps[:, 0:ntok])
            return [(slot[0:64, k, :], attnB[:, k, 0:ntok], o) for k in range(8)] + \
                   [(slot[:, 8 + k, :], ssmB[:, k, 0:ntok], o) for k in range(4)]

        def wo_cb(b, ps, tps):
            tt("dve", xT[:, b, 0:ntok], ps[:, 0:ntok], xT[:, b, 0:ntok], ALU.add, [tps, tx[b]], [tx[b]])
        linear(8, wo_load, wo_mm, wo_cb, [tatb, tsmb], wtrk=[twbs[("outa", l)], twbs[("outb", l)]])
        rmsnorm(xT, tx, 8, 128, lambda k: gffn[:, l, k:k + 1], ntok, lambda k: hT[:, k, 0:ntok], th_, D)
        car = scarry[l] if sample else carry[l]
        CTL = ntok // NS
        W2 = CTL + 2

        def up_load(b, slot):
            m = (b // 2) + 21 * (b % 2)
            return [(slot[:, 0:8, :], wb_up[l, :, :, m * 128:(m + 1) * 128])]

        def up_mm(b, slot):
            return [(slot[:, k, :], hT[:, k, 0:ntok], (lambda ps: ps[:, 0:ntok])) for k in range(8)]

        def up_cb(b, ps, tps):
            w = b % 2
            m = (b // 2) + 21 * w
            pp = (b // 2) % 2
            upb = upb2[pp]; cs_ = cs2[pp]; tup = tup2[pp]; tcs = tcs2[pp]
            ub = upb[:, w, 0:NS * W2].rearrange("p (s t) -> p s t", t=W2)
            cv_ = car[:, m, 0:NS * 2].rearrange("p (s t) -> p s t", t=2)
            cp("act", ub[:, :, 0:2], cv_, [tcar[l][m]], [tup[w]])
            cp("act", ub[:, :, 2:W2], ps[:, 0:ntok].rearrange("p (s t) -> p s t", t=CTL), [tps], [tup[w]])
            cp("dve", cv_, ub[:, :, CTL:CTL + 2], [tup[w]], [tcar[l][m]])
            cv3 = cs_[:, w, 0:ntok].rearrange("p (s t) -> p s t", t=CTL)
            act(cv3, ub[:, :, 2:W2], AF.Identity, [tup[w], tconst], [tcs[w]], bias=cbs[:, l, m:m + 1], scale=cws[:, l, m, 2:3])
            stt(cv3, ub[:, :, 1:1 + CTL], cws[:, l, m, 1:2], cv3, ALU.mult, ALU.add, [tup[w], tconst, tcs[w]], [tcs[w]])
            stt(cv3, ub[:, :, 0:CTL], cws[:, l, m, 0:1], cv3, ALU.mult, ALU.add, [tup[w], tconst, tcs[w]], [tcs[w]])
            if w == 1:
                mh = b // 2
                A_ = cs_[:, 0, 0:ntok]; G_ = cs_[:, 1, 0:ntok]
                act(g1[:, pp, 0:ntok], G_, AF.Tanh, [tcs[1], tg1], [tgs[pp]], scale=0.5)
                stt(g1[:, pp, 0:ntok], g1[:, pp, 0:ntok], 1.0, G_, ALU.add, ALU.mult, [tgs[pp], tcs[1]], [tgs[pp]])
                stt(hid[:, mh, 0:ntok], g1[:, pp, 0:ntok], 0.5, A_, ALU.mult, ALU.mult, [tgs[pp], tcs[0]], [thid])
        linear(42, up_load, up_mm, up_cb, th_, wtrk=[twbs[("up", l)]])
        if sample:
            P.dma("pool", d_out, o_cs[l], car[:, :, 0:32].rearrange("p m (s t) -> p m s t", t=2), reads=tcar[l])
        elif bidx == n_batches - 1:
            P.dma("pool", d_out, o_cp[l], car[:, :, 0:2], reads=tcar[l])

        def dn_load(b, slot):
            k0, nk = (0, 11) if b % 2 == 0 else (11, 10)
            return [(slot[:, 0:nk, :], wb_down[l, :, k0:k0 + nk, (b // 2) * 128:(b // 2 + 1) * 128])]

        def dn_mm(b, slot):
            k0, nk = (0, 11) if b % 2 == 0 else (11, 10)
            return [(slot[:, k, :], hid[:, k0 + k, 0:ntok], (lambda ps: ps[:, 0:ntok])) for k in range(nk)]

        def dn_cb(b, ps, tps):
            tt("dve", xT[:, b, 0:ntok], ps[:, 0:ntok], xT[:, b, 0:ntok], ALU.add, [tps, tx[b]], [tx[b]])
        linear(16, dn_load, dn_mm, dn_cb, [thid], group=2, wtrk=[twbs[("down", l)]])

    d_x = P.new_dsem("d_x"); d_y = P.new_dsem("d_y")
    for l in range(L):
        mset("dve", carry[l][:], 0.0, tcar[l])
        for ri in range(2):
            mset("dve", Xp[l][ri][:], 0.0, [tXp[l][ri]])
    pstep = [0, 0]
    for b in range(n_batches):
        P.dma("pool", d_x, xT[:], xpT[:, :, b * NT:(b + 1) * NT], writes=tx)
        for l in range(L):
            grp = dict(sample=False, ntok=NT, NS=1, TL=TLP, b=b, step=pstep[l])
            run_layer(l, grp)
            pstep[l] = grp["step"]
        rmsnorm(xT, tx, 8, 128, lambda k: gfin[:, k:k + 1], NT, lambda k: yout[:, k, :], [tyo, tg1, tg2] + tgs, D)
        P.dma("pool", d_y, o_ypT[:, :, b * NT:(b + 1) * NT], yout[:], reads=[tyo, tg1, tg2] + tgs)
    if do_sample:
        P.dma("pool", d_x, xT[:, :, 0:64], xsT, writes=tx)
        for l in range(L):
            cur = pstep[l] % 2
            d_s1 = P.new_dsem("d_s1_%d" % l); d_s2 = P.new_dsem("d_s2_%d" % l); d_s3 = P.new_dsem("d_s3_%d" % l)
            P.dma("pool", d_s1, Xst[l][0][cur][:], sre[l], writes=[tX[l][0][cur]])
            P.dma("pool", d_s2, Xst[l][1][cur][:], sim[l], writes=[tX[l][1][cur]])
            P.dma("pool", d_s3, scarry[l][:, :, 0:32].rearrange("p m (s t) -> p m s t", t=2), sconv[l], writes=tcar[l])
            grp = dict(sample=True, ntok=64, NS=16, TL=4, step=pstep[l])
            run_layer(l, grp)
        rmsnorm(xT, tx, 8, 128, lambda k: gfin[:, k:k + 1], 64, lambda k: yout[:, k, 0:64], [tyo, tg1, tg2] + tgs, D)
        P.dma("pool", d_y, o_ysT, yout[:, :, 0:64], reads=[tyo, tg1, tg2] + tgs)
    P.finish()
    return nc


def _consts():
    ident = np.eye(128, dtype=np.float32)
    ones = np.ones((128, 128), np.float32)
    slopes = np.array([2.0 ** -(h + 1) for h in range(8)], np.float32).reshape(2, 4)
    k = np.arange(128)[:, None]; q = np.arange(128)[None, :]
    biasP = np.full((128, 2, 2, 4, 128), NEG, np.float32)
    for kv in range(2):
        for g in range(4):
            sl = slopes[kv, g]
            dcur = q - k
            biasP[:, 1, kv, g, :] = np.where(dcur >= 0, -sl * dcur, NEG)
            dprev = q + 128 - k
            biasP[:, 0, kv, g, :] = np.where(dprev < 128, -sl * dprev, NEG)
    biasP = biasP.reshape(128, 2, 2, 512)
    j = np.arange(128)[:, None]; tq = np.arange(4)[None, :]
    bSc = np.full((128, 2, 16, 4, 4), NEG, np.float32)
    bSn = np.full((4, 2, 16, 4, 4), NEG, np.float32)
    tk_ = np.arange(4)[:, None]
    for kv in range(2):
        for g in range(4):
            sl = slopes[kv, g]
            dist = tq + 128 - j
            bSc[:, kv, :, g, :] = np.where(dist < 128, -sl * dist, NEG)[:, None, :]
            dn = tq - tk_
            bSn[:, kv, :, g, :] = np.where(dn >= 0, -sl * dn, NEG)[:, None, :]
    return dict(c_ident=ident, c_ones=ones, c_biasP=biasP, c_biasSc=bSc.reshape(128, 2, 256), c_biasSn=bSn.reshape(4, 2, 256))


def _ktile(w, kt):
    Lw, K, N = w.shape
    return np.ascontiguousarray(w.reshape(Lw, kt, K // kt, N).transpose(0, 2, 1, 3))


def _vec(v, kt):
    Lw, F = v.shape
    return np.ascontiguousarray(v.reshape(Lw, kt, F // kt).transpose(2, 0, 1))


def _l0(a):
    sh = a.shape
    a = a.reshape(sh[0], 16, 2, 64, *sh[3:])
    perm = (2, 3, 0, 1) + tuple(range(4, a.ndim))
    a = a.transpose(perm)
    return np.ascontiguousarray(a.reshape(128, sh[0], 16, *sh[3:]))


_PROG = {}


def kernel(x_prompt, x_sample, cache_k_win, cache_v_win, state_ssm_re, state_ssm_im, state_conv,
           norm_mix, w_in, sinks, lam_re, lam_im, log_step, b_re, b_im, c_re, c_im, d_skip,
           w_glu, b_glu, g_attn, g_ssm, w_out, norm_ffn, w_up, conv_w, conv_b, w_down, norm_final):
    f = lambda a: np.ascontiguousarray(np.asarray(a, dtype=np.float32))
    (x_prompt, x_sample, cache_k_win, cache_v_win, state_ssm_re, state_ssm_im, state_conv, norm_mix, w_in, sinks,
     lam_re, lam_im, log_step, b_re, b_im, c_re, c_im, d_skip, w_glu, b_glu, g_attn, g_ssm, w_out, norm_ffn, w_up,
     conv_w, conv_b, w_down, norm_final) = [f(a) for a in (
        x_prompt, x_sample, cache_k_win, cache_v_win, state_ssm_re, state_ssm_im, state_conv, norm_mix, w_in, sinks,
        lam_re, lam_im, log_step, b_re, b_im, c_re, c_im, d_skip, w_glu, b_glu, g_attn, g_ssm, w_out, norm_ffn, w_up,
        conv_w, conv_b, w_down, norm_final)]
    if "nc" not in _PROG:
        _PROG["nc"] = build_program()
    nc = _PROG["nc"]
    shared = dict(_consts())
    shared.update(
        w_in=_ktile(w_in, 8), w_outa=_ktile(w_out[:, :512], 8), w_outb=_ktile(w_out[:, 512:], 4),
        w_glu=_ktile(w_glu, 4), w_up=_ktile(w_up, 8), w_down=_ktile(w_down, 21),
        g_mix=_vec(norm_mix, 8), g_ffn=_vec(norm_ffn, 8), g_fin=_vec(norm_final[None], 8)[:, 0],
        g_attn=_vec(g_attn, 8), g_ssm=_vec(g_ssm, 4), b_glu=_vec(b_glu, 4), d_skip=_vec(d_skip, 4),
        cw=np.ascontiguousarray(conv_w.reshape(L, 3, MT_UP, 128).transpose(3, 0, 2, 1)),
        cb=_vec(conv_b, MT_UP),
        sinkP=np.ascontiguousarray(np.broadcast_to(sinks[None], (64, L, 8))),
        lamre=_l0(lam_re), lamim=_l0(lam_im),
        lstep=_l0(np.broadcast_to(log_step[:, :, None], (L, 32, 64))),
        bre=_l0(b_re), bim=_l0(b_im),
        cre=_l0(c_re.transpose(0, 1, 3, 2)), cim=_l0(c_im.transpose(0, 1, 3, 2)),
    )
    shared = {k: np.ascontiguousarray(v, dtype=np.float32) for k, v in shared.items()}
    in_maps = []
    for c in range(8):
        s0 = 16 * c
        m = dict(shared)
        m["xpT"] = np.ascontiguousarray(x_prompt[c].T.reshape(8, 128, SEQ).transpose(1, 0, 2))
        m["xsT"] = np.ascontiguousarray(x_sample[s0:s0 + 16].reshape(64, D).T.reshape(8, 128, 64).transpose(1, 0, 2))
        ckc = cache_k_win[:, s0:s0 + 16].reshape(L, 16, 128, 128); cvc = cache_v_win[:, s0:s0 + 16].reshape(L, 16, 128, 128)
        m["ck"] = np.ascontiguousarray(ckc.transpose(0, 2, 1, 3)); m["cv"] = np.ascontiguousarray(cvc.transpose(0, 2, 1, 3))
        m["ckr"] = np.ascontiguousarray(ckc); m["cvr"] = np.ascontiguousarray(cvc)
        def st(a):
            a = a[:, s0:s0 + 16].reshape(L, 16, 16, 2, 64).transpose(0, 3, 4, 2, 1)
            return np.ascontiguousarray(a.reshape(L, 128, 16, 16))
        m["sre"] = st(state_ssm_re); m["sim"] = st(state_ssm_im)
        m["sconv"] = np.ascontiguousarray(state_conv[:, s0:s0 + 16].reshape(L, 16, 2, MT_UP, 128).transpose(0, 4, 3, 1, 2))
        in_maps.append(m)
    res = run_bass_kernel_spmd(nc, in_maps, core_ids=list(range(8)))
    R = res.results
    B = 8
    y_p = np.stack([R[c]["o_ypT"].transpose(1, 0, 2).reshape(D, SEQ).T for c in range(B)])
    y_s = np.concatenate([R[c]["o_ysT"].transpose(1, 0, 2).reshape(D, 64).T.reshape(16, 4, D) for c in range(B)])
    k_p = np.stack([R[c]["o_kp"] for c in range(B)], 1).reshape(L, B, 128, 2, 64)
    v_p = np.stack([R[c]["o_vp"] for c in range(B)], 1).reshape(L, B, 128, 2, 64)

    def unst_p(key):
        a = np.stack([R[c][key] for c in range(B)], 1)
        a = a.reshape(L, B, 2, 64, 16).transpose(0, 1, 4, 2, 3)
        return np.ascontiguousarray(a.reshape(L, B, 32, 64))
    sr_p = unst_p("o_srp"); si_p = unst_p("o_sip")
    c_p = np.stack([R[c]["o_cp"] for c in range(B)], 1)
    c_p = np.ascontiguousarray(c_p.transpose(0, 1, 4, 3, 2).reshape(L, B, 2, 2 * DFF))
    k_s = np.concatenate([R[c]["o_ks"] for c in range(B)], 1).reshape(L, 128, 128, 2, 64)
    v_s = np.concatenate([R[c]["o_vs"] for c in range(B)], 1).reshape(L, 128, 128, 2, 64)

    def unst_s(key):
        a = np.stack([R[c][key] for c in range(B)], 1)
        a = a.reshape(L, B, 2, 64, 16, 16).transpose(0, 1, 5, 4, 2, 3)
        return np.ascontiguousarray(a.reshape(L, B * 16, 32, 64))
    sr_s = unst_s("o_srs"); si_s = unst_s("o_sis")
    c_s = np.stack([R[c]["o_cs"] for c in range(B)], 1)
    c_s = np.ascontiguousarray(c_s.transpose(0, 1, 4, 5, 3, 2).reshape(L, B * 16, 2, 2 * DFF))
    outs = (y_p, y_s, k_p, v_p, sr_p, si_p, c_p, k_s, v_s, sr_s, si_s, c_s)
    return tuple(np.ascontiguousarray(o, dtype=np.float32) for o in outs)
```

```python
import numpy as np
import concourse.bass as bass
import concourse.mybir as mybir

F32 = mybir.dt.float32
BF16 = mybir.dt.bfloat16
I32 = mybir.dt.int32
ALU = mybir.AluOpType
AF = mybir.ActivationFunctionType
AX = mybir.AxisListType

ENGS = ("pe", "act", "dve", "pool", "sp")


class Trk:
    __slots__ = ("name", "w", "rs", "excl")

    def __init__(self, name="", excl=False):
        self.name = name
        self.excl = excl
        self.w = None
        self.rs = []


class Op:
    __slots__ = ("eng", "idx", "fn", "deps", "flag", "dma", "val")

    def __init__(self, eng, idx, fn, deps, dma):
        self.eng, self.idx, self.fn, self.deps, self.dma = eng, idx, fn, deps, dma
        self.flag = False
        self.val = None


class Prog:
    def __init__(self, nc):
        self.nc = nc
        self.ops = {e: [] for e in ENGS}
        self.dsems = []
        self._ctx = []

    def enter(self, cm):
        v = cm.__enter__()
        self._ctx.append(cm)
        return v

    def sbuf(self, name, shape, dt):
        return self.enter(self.nc.sbuf_tensor(name, list(shape), dt))

    def psum(self, name, shape, dt=F32):
        return self.enter(self.nc.psum_tensor(name, list(shape), dt))

    def new_dsem(self, name):
        s = self.enter(self.nc.semaphore(name))
        d = {"sem": s, "cnt": 0}
        self.dsems.append(d)
        return d

    def _deps(self, eng, reads, writes):
        deps = []
        for t in reads:
            if t.w is not None:
                deps.append(t.w)
        for t in writes:
            if t.w is not None:
                deps.append(t.w)
            deps.extend(t.rs)
        return deps

    def op(self, eng, fn, reads=(), writes=()):
        ex = [t for t in reads if t.excl]
        if ex:
            reads = [t for t in reads if not t.excl]
            writes = list(writes) + ex
        deps = self._deps(eng, reads, writes)
        o = Op(eng, len(self.ops[eng]), fn, deps, None)
        self.ops[eng].append(o)
        for t in reads:
            t.rs.append(o)
        for t in writes:
            t.w = o
            t.rs = []
        return o

    def dma(self, q, dsem, out, in_, reads=(), writes=(), **kw):
        deps = self._deps(q, reads, writes)

        def fn(e):
            return e.dma_start(out=out, in_=in_, **kw)
        o = Op(q, len(self.ops[q]), fn, deps, dsem)
        dsem["cnt"] += 16
        o.val = dsem["cnt"]
        o.flag = True
        self.ops[q].append(o)
        for t in reads:
            t.rs.append(o)
        for t in writes:
            t.w = o
            t.rs = []
        return o

    def barrier(self, eng, dsem, fn, trks):
        dep = Op("sp", -1, None, [], dsem)
        dep.val = dsem["cnt"]
        dep.flag = True
        deps = [dep] + [t.w for t in trks if t.w is not None]
        o = Op(eng, len(self.ops[eng]), fn, deps, None)
        self.ops[eng].append(o)
        for t in trks:
            t.w = o
            t.rs = []
        return o

    def finish(self, final_waits=()):
        nc = self.nc
        for e in ENGS:
            for o in self.ops[e]:
                for d in o.deps:
                    if d.dma is None:
                        if d.eng == "pe" and e == "pe":
                            continue
                        d.flag = True
        esem = {}
        for e in ENGS:
            esem[e] = self.enter(nc.semaphore("esem_" + e))
            c = 0
            for o in self.ops[e]:
                if o.dma is None:
                    if o.flag:
                        c += 1
                        o.val = c
                    else:
                        o.val = None
        nxt = {}
        for e in ENGS:
            arr = [None] * len(self.ops[e])
            cur = None
            for i in range(len(self.ops[e]) - 1, -1, -1):
                o = self.ops[e][i]
                if o.dma is None and o.flag:
                    cur = o.val
                arr[i] = cur
            nxt[e] = arr
        self.nwaits = 0
        prog = self

        def emit(ename):
            def body(eh):
                seen = {}
                for o in prog.ops[ename]:
                    need = {}
                    for d in o.deps:
                        if d.dma is not None:
                            key = ("d", id(d.dma))
                            sem, val = d.dma["sem"], d.val
                        else:
                            if d.eng == "pe" and ename == "pe":
                                continue
                            key = ("e", d.eng)
                            sem, val = esem[d.eng], d.val
                            assert val is not None
                        if seen.get(key, 0) >= val:
                            continue
                        if key not in need or need[key][1] < val:
                            need[key] = (sem, val)
                    for key, (sem, val) in need.items():
                        eh.wait_ge(sem, val)
                        seen[key] = val
                        prog.nwaits += 1
                    if o.fn is None:
                        continue
                    ins = o.fn(eh)
                    if o.dma is not None:
                        ins.then_inc(o.dma["sem"], 16)
                    elif o.flag:
                        ins.then_inc(esem[ename], 1)
                if ename == "sp":
                    for d in prog.dsems:
                        if d["cnt"] > 0:
                            eh.wait_ge(d["sem"], d["cnt"])
            return body

        with nc.Block() as block:
            block.tensor(emit("pe"))
            block.scalar(emit("act"))
            block.vector(emit("dve"))
            block.gpsimd(emit("pool"))
            block.sync(emit("sp"))
        for cm in reversed(self._ctx):
            cm.__exit__(None, None, None)
        self._ctx = []

import math
from concourse.bass_utils import run_bass_kernel_spmd

D = 1024; L = 2; SEQ = 4096; NB = 16; NT = 256; TLP = 64; NTI = 2; DFF = 2688; MT_UP = 42
NEG = -30000.0
EPS = 1e-5
PI = math.pi


def build_program(n_batches=NB, do_sample=True):
    nc = bass.Bass("TRN2", target_bir_lowering=False)
    P = Prog(nc)

    def din(name, shape):
        return nc.dram_tensor(name, list(shape), F32, kind="ExternalInput").ap()

    def dout(name, shape):
        return nc.dram_tensor(name, list(shape), F32, kind="ExternalOutput").ap()

    xpT = din("xpT", [128, 8, SEQ]); xsT = din("xsT", [128, 8, 64])
    ck = din("ck", [L, 128, 16, 128]); cv = din("cv", [L, 128, 16, 128])
    ckr = din("ckr", [L, 16, 128, 128]); cvr = din("cvr", [L, 16, 128, 128])
    sre = din("sre", [L, 128, 16, 16]); sim = din("sim", [L, 128, 16, 16])
    sconv = din("sconv", [L, 128, MT_UP, 16, 2])
    w_in = din("w_in", [L, 128, 8, 1280])
    w_outa = din("w_outa", [L, 64, 8, 1024]); w_outb = din("w_outb", [L, 128, 4, 1024])
    w_glu = din("w_glu", [L, 128, 4, 512])
    w_up = din("w_up", [L, 128, 8, 2 * DFF]); w_down = din("w_down", [L, 128, 21, 1024])
    g_mix = din("g_mix", [128, L, 8]); g_ffn = din("g_ffn", [128, L, 8]); g_fin = din("g_fin", [128, 8])
    g_attn = din("g_attn", [64, L, 8]); g_ssm = din("g_ssm", [128, L, 4])
    b_glu = din("b_glu", [128, L, 4]); d_skip = din("d_skip", [128, L, 4])
    cw = din("cw", [128, L, MT_UP, 3]); cb = din("cb", [128, L, MT_UP])
    sinkP = din("sinkP", [64, L, 8])
    lamre = din("lamre", [128, L, 16]); lamim = din("lamim", [128, L, 16]); lstep = din("lstep", [128, L, 16])
    bre = din("bre", [128, L, 16, 16]); bim = din("bim", [128, L, 16, 16])
    cre = din("cre", [128, L, 16, 16]); cim = din("cim", [128, L, 16, 16])
    c_ident = din("c_ident", [128, 128]); c_ones = din("c_ones", [128, 128])
    c_biasP = din("c_biasP", [128, 2, 2, 512]); c_biasSc = din("c_biasSc", [128, 2, 256]); c_biasSn = din("c_biasSn", [4, 2, 256])

    o_ypT = dout("o_ypT", [128, 8, SEQ]); o_ysT = dout("o_ysT", [128, 8, 64])
    o_kp = dout("o_kp", [L, 128, 128]); o_vp = dout("o_vp", [L, 128, 128])
    o_srp = dout("o_srp", [L, 128, 16]); o_sip = dout("o_sip", [L, 128, 16])
    o_cp = dout("o_cp", [L, 128, MT_UP, 2])
    o_ks = dout("o_ks", [L, 16, 128, 128]); o_vs = dout("o_vs", [L, 16, 128, 128])
    o_srs = dout("o_srs", [L, 128, 16, 16]); o_sis = dout("o_sis", [L, 128, 16, 16])
    o_cs = dout("o_cs", [L, 128, MT_UP, 16, 2])

    def dscr(name, shape):
        return nc.dram_tensor(name, list(shape), BF16, kind="Internal").ap()
    wb_in = dscr("wb_in", [L, 128, 8, 1280]); wb_outa = dscr("wb_outa", [L, 64, 8, 1024]); wb_outb = dscr("wb_outb", [L, 128, 4, 1024])
    wb_glu = dscr("wb_glu", [L, 128, 4, 512]); wb_up = dscr("wb_up", [L, 128, 8, 2 * DFF]); wb_down = dscr("wb_down", [L, 128, 21, 1024])

    def T(n=1):
        return [Trk() for _ in range(n)] if n > 1 else Trk()

    def act(out, in_, func, reads, writes, bias=None, scale=None):
        kw = {}
        if bias is not None:
            kw["bias"] = bias
        if scale is not None:
            kw["scale"] = scale
        return P.op("act", lambda e: e.activation(out, in_, func, **kw), reads, writes)

    def tt(eng, out, a, b, op, reads, writes):
        return P.op(eng, lambda e: e.tensor_tensor(out, a, b, op), reads, writes)

    def ts(eng, out, a, s1, s2, op0, op1, reads, writes):
        return P.op(eng, lambda e: e.tensor_scalar(out, a, s1, s2, op0, op1), reads, writes)

    def stt(out, a, s, b, op0, op1, reads, writes):
        return P.op("dve", lambda e: e.scalar_tensor_tensor(out, a, s, b, op0, op1), reads, writes)

    def cp(eng, out, in_, reads, writes):
        if eng == "act":
            return P.op("act", lambda e: e.activation(out, in_, AF.Copy), reads, writes)
        return P.op(eng, lambda e: e.tensor_copy(out, in_), reads, writes)

    def mm(out, lhsT, rhs, start, stop, reads, writes):
        return P.op("pe", lambda e: e.matmul(out, lhsT, rhs, start=start, stop=stop), reads, writes)

    def mset(eng, ap, val, writes):
        return P.op(eng, lambda e: e.memset(ap, val), (), writes)

    def recip(out, in_, reads, writes):
        return P.op("dve", lambda e: e.reciprocal(out, in_), reads, writes)

    d_pre = P.new_dsem("d_pre"); d_kc = P.new_dsem("d_kc"); d_vc = P.new_dsem("d_vc")
    d_kvn = [P.new_dsem("d_kvn0"), P.new_dsem("d_kvn1")]; d_so = [P.new_dsem("d_so0"), P.new_dsem("d_so1")]
    d_out = P.new_dsem("d_out")

    d_pre2 = P.new_dsem("d_pre2")

    def load(dst, src, trk, q="sp"):
        return P.dma(q, d_pre if q == "sp" else d_pre2, dst, src, writes=[trk])

    ident_f = P.sbuf("ident_f", [128, 128], F32); ident_b = P.sbuf("ident_b", [128, 128], BF16)
    ones_b = P.sbuf("ones_b", [128, 128], BF16)
    biasP = P.sbuf("biasP", [128, 2, 2, 512], BF16)
    biasSc = P.sbuf("biasSc", [128, 2, 256], BF16); biasSn = P.sbuf("biasSn", [4, 2, 256], BF16)
    tconst = T()
    load(ident_f[:], c_ident, tconst)
    load(ident_b[:], c_ident, tconst, "pool"); load(ones_b[:], c_ones, tconst, "pool")
    load(biasP[:], c_biasP, tconst, "pool"); load(biasSc[:], c_biasSc, tconst, "pool"); load(biasSn[:], c_biasSn, tconst, "pool")
    gmix = P.sbuf("gmix", [128, L, 8], F32); gffn = P.sbuf("gffn", [128, L, 8], F32); gfin = P.sbuf("gfin", [128, 8], F32)
    gattn = P.sbuf("gattn", [64, L, 8], F32); gssm = P.sbuf("gssm", [128, L, 4], F32)
    bglu = P.sbuf("bglu", [128, L, 4], F32); hbglu = P.sbuf("hbglu", [128, L, 4], F32); dsk = P.sbuf("dsk", [128, L, 4], F32)
    cws = P.sbuf("cws", [128, L, MT_UP, 3], F32); cbs = P.sbuf("cbs", [128, L, MT_UP], F32)
    esP = P.sbuf("esP", [64, L, 8], F32)
    for dst, src in ((gmix, g_mix), (gffn, g_ffn), (gfin, g_fin), (gattn, g_attn), (gssm, g_ssm), (bglu, b_glu),
                     (dsk, d_skip), (cws, cw), (cbs, cb), (esP, sinkP)):
        load(dst[:], src, tconst)
    epsT = P.sbuf("epsT", [128, 1], F32); hpiT = P.sbuf("hpiT", [128, 1], F32)
    P.barrier("dve", d_pre, lambda e: e.memset(epsT[:], EPS), [tconst])
    P.barrier("dve", d_pre2, lambda e: e.memset(hpiT[:], PI / 2), [tconst])
    mset("dve", epsT[:], EPS, [tconst]); mset("dve", hpiT[:], PI / 2, [tconst])
    act(esP[:], esP[:], AF.Exp, [tconst], [tconst])
    ts("dve", hbglu[:], bglu[:], 0.5, None, ALU.mult, ALU.bypass, [tconst], [tconst])

    twbs = {}
    for l_ in range(L):
        for nm_, dst_, src_, kt_ in (("in", wb_in, w_in, 8), ("outa", wb_outa, w_outa, 8), ("outb", wb_outb, w_outb, 4),
                                     ("glu", wb_glu, w_glu, 4), ("up", wb_up, w_up, 8), ("down", wb_down, w_down, 21)):
            d_cvt = P.new_dsem("d_cvt_%s%d" % (nm_, l_)); t_ = Trk()
            twbs[(nm_, l_)] = t_
            for k_ in range(kt_):
                P.dma("pool", d_cvt, dst_[l_, :, k_, :], src_[l_, :, k_, :], writes=[t_])


    XT = lambda n=1: [Trk(excl=True) for _ in range(n)] if n > 1 else Trk(excl=True)
    psA = [P.psum("psA%d" % i, [128, 512]) for i in range(2)]; tA = XT(2)
    psS = [P.psum("psS%d" % i, [128, 512]) for i in range(2)]; tS = XT(2)
    psO = P.psum("psO", [128, 512]); tO = XT()
    psD = P.psum("psD", [128, 512]); tD = XT()
    psM = [P.psum("psM%d" % i, [128, 512]) for i in range(2)]; tM = XT(2)
    cntM = [0]

    def nextM():
        i = cntM[0] % 2
        cntM[0] += 1
        return psM[i], tM[i]

    s5 = []
    scr = [P.sbuf("s5scr%d" % i, [128, 16], F32) for i in range(8)]
    G12 = P.sbuf("G12", [128, 8, NT], F32)
    bmf = G12[:].rearrange("p a (b c) -> p (a b) c", c=128)
    S5tmp = P.sbuf("S5tmp", [128, 4, 16, TLP], F32); ttmp = T(); ttq = T(4)
    braw = S5tmp[:, 0].rearrange("p g t -> p (g t)")[:, 0:512].rearrange("p (a g h) -> p a g h", a=2, g=16)
    bbar = S5tmp[:, 1].rearrange("p g t -> p (g t)")[:, 0:512].rearrange("p (a g h) -> p a g h", a=2, g=16)
    craw = S5tmp[:, 2].rearrange("p g t -> p (g t)")[:, 0:512].rearrange("p (a g h) -> p a g h", a=2, g=16)
    tab = T()
    for l in range(L):
        lr = P.sbuf("lr%d" % l, [128, 16], F32); li = P.sbuf("li%d" % l, [128, 16], F32); ls = P.sbuf("ls%d" % l, [128, 16], F32)
        d_tab = P.new_dsem("d_tab%d" % l)
        for dst_, src_ in ((lr[:], lamre[:, l, :]), (li[:], lamim[:, l, :]), (ls[:], lstep[:, l, :]), (braw[:, 0], bre[:, l]),
                           (braw[:, 1], bim[:, l]), (craw[:, 0], cre[:, l]), (craw[:, 1], cim[:, l])):
            P.dma("sp", d_tab, dst_, src_, writes=[tab])
        P.barrier("dve", d_tab, (lambda l_: lambda e: e.memset(scr[0][:], 0.0))(l), [tab])
        AR = P.sbuf("AR%d" % l, [128, 16], F32); AI = P.sbuf("AI%d" % l, [128, 16], F32)
        AR16 = P.sbuf("AR16_%d" % l, [128, 16, 16], F32); AI16 = P.sbuf("AI16_%d" % l, [128, 16, 16], F32)
        BT = [P.sbuf("BT%d_%d" % (l, ri), [128, 16, 128], BF16) for ri in range(2)]
        CT = [P.sbuf("CT%d_%d" % (l, ri), [128, 16, 128], BF16) for ri in range(2)]
        dt_, zr, th, rr, cc, ss, t0, t1 = [s[:] for s in scr]
        R, W = [tab], [tab]
        act(dt_, ls[:], AF.Exp, R, W)
        tt("dve", zr, lr[:], dt_, ALU.mult, R, W)
        tt("dve", th, li[:], dt_, ALU.mult, R, W)
        act(rr, zr, AF.Exp, R, W)
        act(cc, th, AF.Sin, R, W, bias=hpiT[:], scale=1.0 / 32)
        act(ss, th, AF.Sin, R, W, scale=1.0 / 32)
        for _ in range(5):
            tt("dve", t0, cc, cc, ALU.mult, R, W)
            tt("dve", t1, ss, ss, ALU.mult, R, W)
            tt("dve", ss, ss, cc, ALU.mult, R, W)
            ts("dve", ss, ss, 2.0, None, ALU.mult, ALU.bypass, R, W)
            tt("dve", cc, t0, t1, ALU.subtract, R, W)
        tt("dve", AR[:], rr, cc, ALU.mult, R, W)
        tt("dve", AI[:], rr, ss, ALU.mult, R, W)
        for s_ in range(16):
            cp("dve", AR16[:, :, s_], AR[:], R, W); cp("dve", AI16[:, :, s_], AI[:], R, W)
        RR = P.sbuf("RR%d" % l, [128, 16], F32)
        cp("dve", RR[:], rr, R, W)
        C1 = P.sbuf("C1_%d" % l, [128, 16, TLP], F32); S1 = P.sbuf("S1_%d" % l, [128, 16, TLP], F32)
        cp("dve", C1[:, :, 0], cc, R, W); cp("dve", S1[:, :, 0], ss, R, W)
        kk_ = 1
        Dt_ = S5tmp[:, 3]
        while kk_ < TLP:
            cp("dve", t0, C1[:, :, kk_ - 1], R, W); cp("dve", t1, S1[:, :, kk_ - 1], R, W)
            cKb = t0.unsqueeze(2).to_broadcast([128, 16, kk_]); sKb = t1.unsqueeze(2).to_broadcast([128, 16, kk_])
            tA2 = Dt_[:, :, 0:kk_]; tB2 = Dt_[:, :, 32:32 + kk_]
            tt("dve", tA2, C1[:, :, 0:kk_], cKb, ALU.mult, R, W)
            tt("dve", tB2, S1[:, :, 0:kk_], sKb, ALU.mult, R, W)
            tt("dve", C1[:, :, kk_:2 * kk_], tA2, tB2, ALU.subtract, R, W)
            tt("dve", tA2, S1[:, :, 0:kk_], cKb, ALU.mult, R, W)
            tt("dve", tB2, C1[:, :, 0:kk_], sKb, ALU.mult, R, W)
            tt("dve", S1[:, :, kk_:2 * kk_], tA2, tB2, ALU.add, R, W)
            kk_ *= 2
        nr, den, cr, ci = dt_, zr, th, rr
        ts("dve", nr, AR[:], -1.0, None, ALU.add, ALU.bypass, R, W)
        tt("dve", t0, lr[:], lr[:], ALU.mult, R, W)
        tt("dve", t1, li[:], li[:], ALU.mult, R, W)
        tt("dve", den, t0, t1, ALU.add, R, W)
        recip(den, den, R, W)
        tt("dve", t0, nr, lr[:], ALU.mult, R, W)
        tt("dve", t1, AI[:], li[:], ALU.mult, R, W)
        tt("dve", t0, t0, t1, ALU.add, R, W)
        tt("dve", cr, t0, den, ALU.mult, R, W)
        tt("dve", t0, AI[:], lr[:], ALU.mult, R, W)
        tt("dve", t1, nr, li[:], ALU.mult, R, W)
        tt("dve", t0, t0, t1, ALU.subtract, R, W)
        tt("dve", ci, t0, den, ALU.mult, R, W)
        nci = cc
        ts("dve", nci, ci, -1.0, None, ALU.mult, ALU.bypass, R, W)
        for gp in range(16):
            ts("dve", bbar[:, 0, gp, :], braw[:, 0, gp, :], cr[:, gp:gp + 1], None, ALU.mult, ALU.bypass, R, W)
            stt(bbar[:, 0, gp, :], braw[:, 1, gp, :], nci[:, gp:gp + 1], bbar[:, 0, gp, :], ALU.mult, ALU.add, R, W)
            ts("dve", bbar[:, 1, gp, :], braw[:, 1, gp, :], cr[:, gp:gp + 1], None, ALU.mult, ALU.bypass, R, W)
            stt(bbar[:, 1, gp, :], braw[:, 0, gp, :], ci[:, gp:gp + 1], bbar[:, 1, gp, :], ALU.mult, ALU.add, R, W)
        for ri in range(2):
            mset("dve", bmf[:], 0.0, W)
            for gp in range(16):
                c0 = 32 * (gp % 4)
                cp("dve", bmf[0:64, gp, c0:c0 + 16], bbar[0:64, ri, gp, :], R, W)
                cp("dve", bmf[64:128, gp, c0 + 16:c0 + 32], bbar[64:128, ri, gp, :], R, W)
            for gp in range(16):
                pm, tm = nextM()
                P.op("pe", (lambda pm_, gp_: lambda e: e.transpose(pm_[:, 0:128], bmf[:, gp_, :], ident_f[:]))(pm, gp),
                     [tab, tconst], [tm])
                cp("act", BT[ri][:, gp, :], pm[:, 0:128], [tm], [tab])
            mset("dve", CT[ri][:], 0.0, W)
            for gp in range(16):
                c0 = 32 * (gp % 4)
                sc = 1.0 if ri == 0 else -1.0
                ts("dve", CT[ri][0:64, gp, c0:c0 + 16], craw[0:64, ri, gp, :], sc, None, ALU.mult, ALU.bypass, R, W)
                ts("dve", CT[ri][64:128, gp, c0 + 16:c0 + 32], craw[64:128, ri, gp, :], sc, None, ALU.mult, ALU.bypass, R, W)
        s5.append(dict(AR=AR, AI=AI, AR16=AR16, AI16=AI16, BT=BT, CT=CT, RR=RR, C1=C1, S1=S1))

    xT = P.sbuf("xT", [128, 8, NT], F32); tx = T(8)
    hT = P.sbuf("hT", [128, 8, NT], BF16); th_ = T(8)
    rstd = P.sbuf("rstd", [128, NT], F32); trs = T()
    qT = P.sbuf("qT", [64, 8, NT], BF16); tq = T()
    kT = [P.sbuf("kT%d" % l, [64, 2, 128 + NT], BF16) for l in range(L)]; tk = T(2)
    vtok = [P.sbuf("vtok%d" % l, [128, NTI + 1, 128], BF16) for l in range(L)]; tv = T(2)
    kvf = P.sbuf("kvf", [128, 256], F32); tkvf = T()
    wkv = P.sbuf("wkv", [128, 8, 256], BF16); twkv = T(); d_wkv = P.new_dsem("d_wkv")
    uT = P.sbuf("uT", [128, 4, NT], BF16); tu = T()
    PT = P.sbuf("PT", [128, 2, 512], BF16); tPT = T()
    den_sb = P.sbuf("den_sb", [64, 512], F32); tden = T()
    attnT = P.sbuf("attnT", [64, 8, NT], F32); tat = T()
    attnB = P.sbuf("attnB", [64, 8, NT], BF16); tatb = T()
    bu = [P.sbuf("bu%d" % ri, [128, 16, 64], F32) for ri in range(2)]; tbu = T(2)
    xs5 = [P.sbuf("xs5_%d" % ri, [128, 16, 64], BF16) for ri in range(2)]; txs = T(2)
    Xs_ = [[P.sbuf("Xs_%d_%d" % (ri, pp), [128, 16, 16], F32) for pp in range(2)] for ri in range(2)]
    tXs_ = [[T() for pp in range(2)] for ri in range(2)]
    Xst = [Xs_, Xs_]; tX = [tXs_, tXs_]
    Xp = [[P.sbuf("Xp%d_%d" % (l, ri), [128, 16], F32) for ri in range(2)] for l in range(L)]
    tXp = [[T() for ri in range(2)] for l in range(L)]
    stmp = [P.sbuf("stmp%d" % i, [128, 16, 16], F32) for i in range(4)]; tst = T(4)
    yT = P.sbuf("yT", [128, 4, NT], F32); ty = T()
    tg1 = T(); tg2 = T()
    g1 = G12[:, 0:4, :]; g2 = G12[:, 4:8, :]
    sbf = P.sbuf("sbf", [128, 4, NT], BF16); tsb = T()
    ssmB = P.sbuf("ssmB", [128, 4, NT], BF16); tsmb = T()
    upb2 = [P.sbuf("upb%d" % i, [128, 2, NT + 32], F32) for i in range(2)]; tup2 = [T(2), T(2)]
    cs2 = [P.sbuf("cs_%d" % i, [128, 2, NT], F32) for i in range(2)]; tcs2 = [T(2), T(2)]; tgs = T(2)
    hid = P.sbuf("hid", [128, 21, NT], BF16); thid = T()
    sq = hid[:, 0:8, :]; tsq = thid
    carry = [P.sbuf("carry%d" % l, [128, MT_UP, 2], F32) for l in range(L)]; tcar = [T(MT_UP), T(MT_UP)]
    scarry1 = P.sbuf("scarry", [128, MT_UP, 32], F32); scarry = [scarry1, scarry1]
    kc_f = S5tmp[:, 0:2].rearrange("p a g t -> p (a g t)").rearrange("p (s c) -> p s c", c=128); tkc = ttmp
    kcT = P.sbuf("kcT", [64, 16, 128], BF16); tkcT = T()
    vc_b = P.sbuf("vc_b", [128, 16, 128], BF16); tvc = T()
    kvn_f = P.sbuf("kvn_f", [4, 2, 256], F32); tkvn = T(2)
    vn_b = P.sbuf("vn_b", [4, 16, 128], BF16); tvn = T()
    yout = G12; tyo = T()

    NSLOT = 3
    wsl = [P.sbuf("wsl%d" % i, [128, 12, 128], BF16) for i in range(NSLOT)]
    twsl = T(NSLOT); dwsl = [P.new_dsem("dw%d" % i) for i in range(NSLOT)]
    wcnt = [0]

    def linear(nblk, load_fn, mm_fn, cb_fn, rtrks, group=1, wtrk=()):
        base = wcnt[0]
        wcnt[0] += nblk

        def issue(b):
            si = (base + b) % NSLOT
            for dst, src in load_fn(b, wsl[si]):
                P.dma("sp", dwsl[si], dst, src, reads=list(wtrk), writes=[twsl[si]])
        for b in range(min(NSLOT - 1, nblk)):
            issue(b)
        for b in range(nblk):
            if b + NSLOT - 1 < nblk:
                issue(b + NSLOT - 1)
            si = (base + b) % NSLOT
            ps, tps = psA[(b // group) % 2], tA[(b // group) % 2]
            pairs = mm_fn(b, wsl[si])
            out_ap = pairs[0][2]
            for i, (lt, rh, _) in enumerate(pairs):
                mm(out_ap(ps), lt, rh, i == 0 and b % group == 0, i == len(pairs) - 1 and b % group == group - 1,
                   [twsl[si]] + rtrks, [tps])
            if b % group == group - 1:
                cb_fn(b // group, ps, tps)

    def rmsnorm(src, tsrc, nk, npart, gain_fn, ntok, dst_fn, tdst, nfeat):
        act(sq[0:npart, 0:nk, 0:ntok], src[0:npart, 0:nk, 0:ntok], AF.Square, tsrc, [tsq])
        pm, tm = nextM()
        for k in range(nk):
            mm(pm[:, 0:ntok], ones_b[0:npart, :], sq[0:npart, k, 0:ntok], k == 0, k == nk - 1, [tsq, tconst], [tm])
        act(rstd[:, 0:ntok], pm[:, 0:ntok], AF.Sqrt, [tm, tconst], [trs], bias=epsT[:], scale=1.0 / nfeat)
        recip(rstd[:, 0:ntok], rstd[:, 0:ntok], [trs], [trs])
        for k in range(nk):
            stt(dst_fn(k), src[0:npart, k, 0:ntok], gain_fn(k), rstd[0:npart, 0:ntok], ALU.mult, ALU.mult,
                tsrc + [trs, tconst], tdst)

    def run_layer(l, grp):
        sample = grp["sample"]
        ntok = grp["ntok"]; NS = grp["NS"]; TL = grp["TL"]
        bidx = grp.get("b", 0)
        tb = s5[l]
        rmsnorm(xT, tx, 8, 128, lambda k: gmix[:, l, k:k + 1], ntok, lambda k: hT[:, k, 0:ntok], th_, D)
        blocks = [(h * 64, 64) for h in range(8)] + [(512 + kv * 64, 64) for kv in range(2)] + [(768 + q * 128, 128) for q in range(4)]
        koff = 0 if sample else 128

        def w_in_load(b, slot):
            c0, msz = blocks[b]
            return [(slot[:, 0:8, 0:msz], wb_in[l, :, :, c0:c0 + msz])]

        def w_in_mm(b, slot):
            c0, msz = blocks[b]
            return [(slot[:, k, 0:msz], hT[:, k, 0:ntok], (lambda ps, msz=msz: ps[0:msz, 0:ntok])) for k in range(8)]

        def w_in_cb(b, ps, tps):
            if b < 8:
                act(qT[:, b, 0:ntok], ps[0:64, 0:ntok], AF.Copy, [tps], [tq], scale=0.125)
            elif b < 10:
                cp("act", kT[l][:, b - 8, koff:koff + ntok], ps[0:64, 0:ntok], [tps], [tk[l]])
            else:
                cp("act", uT[:, b - 10, 0:ntok], ps[:, 0:ntok], [tps], [tu])
        linear(14, w_in_load, w_in_mm, w_in_cb, th_, wtrk=[twbs[("in", l)]])
        P.dma("sp", d_wkv, wkv[:], wb_in[l, :, :, 512:768], reads=[twbs[("in", l)]], writes=[twkv])
        if not sample:
            for i in range(NTI):
                pm, tm = nextM()
                for k in range(8):
                    mm(pm[:, 0:256], hT[:, k, i * 128:(i + 1) * 128], wkv[:, k, :], k == 0, k == 7, th_ + [twkv], [tm])
                cp("act", vtok[l][:, i + 1, :], pm[:, 128:256], [tm], [tv[l]])
                if bidx == n_batches - 1 and i == NTI - 1:
                    cp("dve", kvf[:], pm[:, 0:256], [tm], [tkvf])
                    P.dma("pool", d_out, o_kp[l], kvf[:, 0:128], reads=[tkvf])
                    P.dma("pool", d_out, o_vp[l], kvf[:, 128:256], reads=[tkvf])
        else:
            for s_ in range(16):
                pm, tm = nextM()
                for k in range(8):
                    mm(pm[0:4, 0:256], hT[:, k, 4 * s_:4 * s_ + 4], wkv[:, k, :], k == 0, k == 7, th_ + [twkv], [tm])
                cp("act", kvn_f[:, s_ % 2, :], pm[0:4, 0:256], [tm], [tkvn[s_ % 2]])
                cp("dve", vn_b[:, s_, :], kvn_f[:, s_ % 2, 128:256], [tkvn[s_ % 2]], [tvn])
                P.dma("pool", d_kvn[s_ % 2], o_ks[l, s_, 124:128, :], kvn_f[:, s_ % 2, 0:128], reads=[tkvn[s_ % 2]])
                P.dma("pool", d_kvn[s_ % 2], o_vs[l, s_, 124:128, :], kvn_f[:, s_ % 2, 128:256], reads=[tkvn[s_ % 2]])
            P.dma("pool", d_out, o_ks[l, :, 0:124, :], ckr[l, :, 4:128, :])
            P.dma("pool", d_out, o_vs[l, :, 0:124, :], cvr[l, :, 4:128, :])
        attn_units = []
        if not sample:
            def mk_unit(i, kv):
                first = (bidx == 0 and i == 0)
                blks = [1] if first else [0, 1]

                def p1():
                    for blk in blks:
                        kcol = i * 128 + blk * 128
                        mm(psS[blk][:, :], kT[l][:, kv, kcol:kcol + 128], qT[:, 4 * kv:4 * kv + 4, i * 128:(i + 1) * 128],
                           True, False, [tk[l], tq], [tS[blk]])
                        mm(psS[blk][:, :], ident_b[:], biasP[:, blk, kv, :], False, True, [tconst], [tS[blk]])
                        act(PT[:, blk, :], psS[blk][:, :], AF.Exp, [tS[blk]], [tPT])
                    for j, blk in enumerate(blks):
                        mm(psO[0:64, :], vtok[l][:, i + blk, kv * 64:(kv + 1) * 64], PT[:, blk, :], j == 0, j == len(blks) - 1,
                           [tv[l], tPT], [tO])
                    for j, blk in enumerate(blks):
                        mm(psD[0:64, :], ones_b[:, 0:64], PT[:, blk, :], j == 0, j == len(blks) - 1, [tconst, tPT], [tD])

                def p2():
                    for g_ in range(4):
                        ts("dve", den_sb[:, g_ * 128:(g_ + 1) * 128], psD[0:64, g_ * 128:(g_ + 1) * 128], esP[:, l, 4 * kv + g_:4 * kv + g_ + 1], None,
                           ALU.add, ALU.bypass, [tD, tconst], [tden])
                    recip(den_sb[:, :], den_sb[:, :], [tden], [tden])
                    tt("dve", attnT[:, 4 * kv:4 * kv + 4, i * 128:(i + 1) * 128], psO[0:64, :].rearrange("p (g q) -> p g q", g=4),
                       den_sb[:, :].rearrange("p (g q) -> p g q", g=4), ALU.mult, [tO, tden], [tat])
                return p1, p2
            for i in range(NTI):
                for kv in range(2):
                    attn_units.append(mk_unit(i, kv))
        if sample:
            pass
        else:
            pass
        if not sample:
            pass
        else:
            P.dma("sp", d_kc, kc_f[:], ck[l], writes=[tkc, ttq[0], ttq[1]])
            P.dma("pool", d_vc, vc_b[:], cv[l], writes=[tvc])
            for kv in range(2):
                for s_ in range(16):
                    pm, tm = nextM()
                    P.op("pe", (lambda pm_, s2, kv2: lambda e: e.transpose(pm_[0:64, 0:128], kc_f[:, s2, kv2 * 64:(kv2 + 1) * 64], ident_f[:]))(pm, s_, kv),
                         [tkc, tconst], [tm])
                    cp("act", kcT[:, s_, :], pm[0:64, 0:128], [tm], [tkcT])
                mm(psS[0][:, 0:256], ident_b[:], biasSc[:, kv, :], True, False, [tconst], [tS[0]])
                for s_ in range(16):
                    mm(psS[0][:, 16 * s_:16 * s_ + 16], kcT[:, s_, :], qT[:, 4 * kv:4 * kv + 4, 4 * s_:4 * s_ + 4],
                       False, s_ == 15, [tkcT, tq], [tS[0]])
                mm(psS[1][0:4, 0:256], ident_b[0:4, 0:4], biasSn[:, kv, :], True, False, [tconst], [tS[1]])
                for s_ in range(16):
                    mm(psS[1][0:4, 16 * s_:16 * s_ + 16], kT[l][:, kv, 4 * s_:4 * s_ + 4], qT[:, 4 * kv:4 * kv + 4, 4 * s_:4 * s_ + 4],
                       False, s_ == 15, [tk[l], tq], [tS[1]])
                act(PT[:, 0, 0:256], psS[0][:, 0:256], AF.Exp, [tS[0]], [tPT])
                act(PT[0:4, 1, 0:256], psS[1][0:4, 0:256], AF.Exp, [tS[1]], [tPT])
                for s_ in range(16):
                    c = slice(16 * s_, 16 * s_ + 16)
                    mm(psO[0:64, c], vc_b[:, s_, kv * 64:(kv + 1) * 64], PT[:, 0, c], True, False, [tvc, tPT], [tO])
                    mm(psO[0:64, c], vn_b[:, s_, kv * 64:(kv + 1) * 64], PT[0:4, 1, c], False, True, [tvn, tPT], [tO])
                for s_ in range(16):
                    c = slice(16 * s_, 16 * s_ + 16)
                    mm(psD[0:64, c], ones_b[:, 0:64], PT[:, 0, c], True, False, [tconst, tPT], [tD])
                    mm(psD[0:64, c], ones_b[0:4, 0:64], PT[0:4, 1, c], False, True, [tconst, tPT], [tD])
                for g_ in range(4):
                    ts("dve", den_sb[:, 0:256].rearrange("p (s g t) -> p g s t", g=4, t=4)[:, g_],
                       psD[0:64, 0:256].rearrange("p (s g t) -> p g s t", g=4, t=4)[:, g_], esP[:, l, 4 * kv + g_:4 * kv + g_ + 1], None,
                       ALU.add, ALU.bypass, [tD, tconst], [tden])
                recip(den_sb[:, 0:256], den_sb[:, 0:256], [tden], [tden])
                tt("dve", attnT[:, 4 * kv:4 * kv + 4, 0:64].rearrange("p g (s t) -> p s g t", t=4),
                   psO[0:64, 0:256].rearrange("p (s g t) -> p s g t", g=4, t=4),
                   den_sb[:, 0:256].rearrange("p (s g t) -> p s g t", g=4, t=4), ALU.mult, [tO, tden], [tat])
        NTL = NS * TL
        ntile = ntok // NTL

        def s5_bu(it):
            tok0 = it * NTL
            kk = 0
            for ri in range(2):
                for qd in range(4):
                    ps, tps = psA[kk % 2], tA[kk % 2]
                    kk += 1
                    for r4 in range(4):
                        gp = 4 * qd + r4
                        mm(ps[:, r4 * NTL:(r4 + 1) * NTL], tb["BT"][ri][:, gp, :], uT[:, qd, tok0:tok0 + NTL], True, True,
                           [tab, tu], [tps])
                    cp("act", bu[ri][:, 4 * qd:4 * qd + 4, 0:NTL], ps[:, 0:4 * NTL].rearrange("p (g t) -> p g t", t=NTL),
                       [tps], [tbu[ri]])

        def s5_y(it):
            tok0 = it * NTL
            for qd in range(4):
                pm, tm = nextM()
                for r4 in range(4):
                    gp = 4 * qd + r4
                    mm(pm[:, 0:NTL], tb["CT"][0][:, gp, :], xs5[0][:, gp, 0:NTL], r4 == 0, False, [tab, txs[0]], [tm])
                    mm(pm[:, 0:NTL], tb["CT"][1][:, gp, :], xs5[1][:, gp, 0:NTL], False, r4 == 3, [tab, txs[1]], [tm])
                stt(yT[:, qd, tok0:tok0 + NTL], uT[:, qd, tok0:tok0 + NTL], dsk[:, l, qd:qd + 1], pm[:, 0:NTL], ALU.mult, ALU.add,
                    [tu, tconst, tm], [ty])

        if sample:
            it = 0
            tok0 = 0
            s5_bu(0)
            ARt = tb["AR16"][:, :, 0:NS]; AIt = tb["AI16"][:, :, 0:NS]
            for t in range(TL):
                stp = grp["step"]
                cur, nxt = stp % 2, 1 - stp % 2
                grp["step"] += 1
                Xr_c, Xi_c = Xst[l][0][cur][:, :, 0:NS], Xst[l][1][cur][:, :, 0:NS]
                Xr_n, Xi_n = Xst[l][0][nxt][:, :, 0:NS], Xst[l][1][nxt][:, :, 0:NS]
                tXr_c, tXi_c, tXr_n, tXi_n = tX[l][0][cur], tX[l][1][cur], tX[l][0][nxt], tX[l][1][nxt]
                bur = bu[0][:, :, 0:NTL].rearrange("p g (s t) -> p g s t", t=TL)[:, :, :, t]
                bui = bu[1][:, :, 0:NTL].rearrange("p g (s t) -> p g s t", t=TL)[:, :, :, t]
                a0, a1, a2, a3 = [s[:, :, 0:NS] for s in stmp]
                tt("dve", a0, Xr_c, ARt, ALU.mult, [tXr_c, tab], [tst[0]])
                tt("dve", a1, Xi_c, AIt, ALU.mult, [tXi_c, tab], [tst[1]])
                tt("dve", a0, a0, a1, ALU.subtract, [tst[0], tst[1]], [tst[0]])
                tt("dve", Xr_n, a0, bur, ALU.add, [tst[0], tbu[0]], [tXr_n])
                tt("dve", a2, Xr_c, AIt, ALU.mult, [tXr_c, tab], [tst[2]])
                tt("dve", a3, Xi_c, ARt, ALU.mult, [tXi_c, tab], [tst[3]])
                tt("dve", a2, a2, a3, ALU.add, [tst[2], tst[3]], [tst[2]])
                tt("dve", Xi_n, a2, bui, ALU.add, [tst[2], tbu[1]], [tXi_n])
                xr_o = xs5[0][:, :, 0:NTL].rearrange("p g (s t) -> p g s t", t=TL)[:, :, :, t]
                xi_o = xs5[1][:, :, 0:NTL].rearrange("p g (s t) -> p g s t", t=TL)[:, :, :, t]
                cp("act", xr_o, Xr_n, [tXr_n], [txs[0]])
                cp("act", xi_o, Xi_n, [tXi_n], [txs[1]])
            s5_y(0)
        else:
            A_, B_, C_, D_ = S5tmp[:, 0], S5tmp[:, 1], S5tmp[:, 2], S5tmp[:, 3]
            tA_, tB_, tC_, tD_ = ttq
            c1 = tb["C1"][:]; s1 = tb["S1"][:]
            bur = bu[0][:, :, 0:TL]; bui = bu[1][:, :, 0:TL]
            Xcr = Xp[l][0]; Xci = Xp[l][1]; tXcr = tXp[l][0]; tXci = tXp[l][1]
            for it in range(ntile):
                s5_bu(it)
                if it < len(attn_units):
                    attn_units[it][0]()
                tt("dve", A_, c1, bur, ALU.mult, [tab, tbu[0], ttmp], [tA_])
                tt("dve", B_, s1, bui, ALU.mult, [tab, tbu[1], ttmp], [tB_])
                tt("dve", A_, A_, B_, ALU.add, [tB_], [tA_])
                tt("dve", B_, c1, bui, ALU.mult, [tab, tbu[1]], [tB_])
                tt("dve", C_, s1, bur, ALU.mult, [tab, tbu[0]], [tC_])
                tt("dve", B_, B_, C_, ALU.subtract, [tC_], [tB_])
                for gp in range(16):
                    P.op("dve", (lambda gp: lambda e: e.tensor_tensor_scan(
                        A_[:, gp, :], tb["RR"][:, gp:gp + 1].to_broadcast([128, TL]), A_[:, gp, :],
                        Xcr[:, gp:gp + 1], ALU.mult, ALU.add))(gp), [tab, tXcr], [tA_])
                    P.op("dve", (lambda gp: lambda e: e.tensor_tensor_scan(
                        B_[:, gp, :], tb["RR"][:, gp:gp + 1].to_broadcast([128, TL]), B_[:, gp, :],
                        Xci[:, gp:gp + 1], ALU.mult, ALU.add))(gp), [tab, tXci], [tB_])
                if it < len(attn_units):
                    attn_units[it][1]()
                if it > 0:
                    s5_y(it - 1)
                tt("dve", C_, c1, A_, ALU.mult, [tab, tA_], [tC_])
                tt("dve", D_, s1, B_, ALU.mult, [tab, tB_], [tD_])
                tt("dve", C_, C_, D_, ALU.subtract, [tD_], [tC_])
                tt("dve", D_, s1, A_, ALU.mult, [tab, tA_, tC_], [tD_])
                tt("dve", A_, c1, B_, ALU.mult, [tab, tB_], [tA_])
                tt("dve", D_, D_, A_, ALU.add, [tA_], [tD_])
                cp("act", xs5[0][:, :, 0:TL], C_, [tC_], [txs[0]])
                cp("act", xs5[1][:, :, 0:TL], D_, [tD_], [txs[1]])
                cp("act", Xcr[:], C_[:, :, TL - 1], [tC_], [tXcr])
                cp("act", Xci[:], D_[:, :, TL - 1], [tD_], [tXci])
            s5_y(ntile - 1)
        if not sample:
            for u_ in attn_units[ntile:]:
                u_[0](); u_[1]()
            cp("dve", kT[l][:, :, 0:128], kT[l][:, :, NT:NT + 128], [tk[l]], [tk[l]])
            cp("dve", vtok[l][:, 0, :], vtok[l][:, NTI, :], [tv[l]], [tv[l]])
        fin = grp["step"] % 2 if sample else 0
        if sample:
            P.dma("pool", d_so[l], o_srs[l], Xst[l][0][fin][:], reads=[tX[l][0][fin]])
            P.dma("pool", d_so[l], o_sis[l], Xst[l][1][fin][:], reads=[tX[l][1][fin]])
        elif bidx == n_batches - 1:
            P.dma("pool", d_out, o_srp[l], Xp[l][0][:], reads=[tXp[l][0]])
            P.dma("pool", d_out, o_sip[l], Xp[l][1][:], reads=[tXp[l][1]])
        Y = yT[:, :, 0:ntok]; G1 = g1[:, :, 0:ntok]; G2 = g2[:, :, 0:ntok]
        tt("dve", G1, Y, Y, ALU.mult, [ty], [tg1])
        ts("dve", G1, G1, 0.044715, 1.0, ALU.mult, ALU.add, [tg1], [tg1])
        tt("dve", G1, G1, Y, ALU.mult, [tg1, ty], [tg1])
        act(G1, G1, AF.Tanh, [tg1], [tg1], scale=0.7978845608028654)
        ts("dve", G2, Y, 0.5, None, ALU.mult, ALU.bypass, [ty], [tg2])
        stt(G1, G1, 1.0, G2, ALU.add, ALU.mult, [tg1, tg2], [tg1])
        cp("act", sbf[:, :, 0:ntok], G1, [tg1], [tsb])
        ts("dve", G2, G1, 0.5, None, ALU.mult, ALU.bypass, [tg1], [tg2])

        def glu_load(b, slot):
            return [(slot[:, 0:4, :], wb_glu[l, :, :, b * 128:(b + 1) * 128])]

        def glu_mm(b, slot):
            return [(slot[:, k, :], sbf[:, k, 0:ntok], (lambda ps: ps[:, 0:ntok])) for k in range(4)]

        def glu_cb(b, ps, tps):
            act(g1[:, b, 0:ntok], ps[:, 0:ntok], AF.Tanh, [tps, tconst], [tg1], bias=hbglu[:, l, b:b + 1], scale=0.5)
            stt(yT[:, b, 0:ntok], g1[:, b, 0:ntok], 1.0, g2[:, b, 0:ntok], ALU.add, ALU.mult, [tg1, tg2], [ty])
        linear(4, glu_load, glu_mm, glu_cb, [tsb], wtrk=[twbs[("glu", l)]])
        rmsnorm(attnT, [tat], 8, 64, lambda k: gattn[:, l, k:k + 1], ntok, lambda k: attnB[:, k, 0:ntok], [tatb], 512)
        rmsnorm(yT, [ty], 4, 128, lambda k: gssm[:, l, k:k + 1], ntok, lambda k: ssmB[:, k, 0:ntok], [tsmb], 512)

        def wo_load(b, slot):
            return [(slot[0:64, 0:8, :], wb_outa[l, :, :, b * 128:(b + 1) * 128]),
                    (slot[:, 8:12, :], wb_outb[l, :, :, b * 128:(b + 1) * 128])]

        def wo_mm(b, slot):
            o = (lambda ps: ps[:, 0:ntok])
            return [(slot[0:64, k, :], attnB[:, k, 0:ntok], o) for k in range(8)] + \
                   [(slot[:, 8 + k, :], ssmB[:, k, 0:ntok], o) for k in range(4)]

        def wo_cb(b, ps, tps):
            tt("dve", xT[:, b, 0:ntok], ps[:, 0:ntok], xT[:, b, 0:ntok], ALU.add, [tps, tx[b]], [tx[b]])
        linear(8, wo_load, wo_mm, wo_cb, [tatb, tsmb], wtrk=[twbs[("outa", l)], twbs[("outb", l)]])
        rmsnorm(xT, tx, 8, 128, lambda k: gffn[:, l, k:k + 1], ntok, lambda k: hT[:, k, 0:ntok], th_, D)
        car = scarry[l] if sample else carry[l]
        CTL = ntok // NS
        W2 = CTL + 2

        def up_load(b, slot):
            m = (b // 2) + 21 * (b % 2)
            return [(slot[:, 0:8, :], wb_up[l, :, :, m * 128:(m + 1) * 128])]

        def up_mm(b, slot):
            return [(slot[:, k, :], hT[:, k, 0:ntok], (lambda ps: ps[:, 0:ntok])) for k in range(8)]

        def up_cb(b, ps, tps):
            w = b % 2
            m = (b // 2) + 21 * w
            pp = (b // 2) % 2
            upb = upb2[pp]; cs_ = cs2[pp]; tup = tup2[pp]; tcs = tcs2[pp]
            ub = upb[:, w, 0:NS * W2].rearrange("p (s t) -> p s t", t=W2)
            cv_ = car[:, m, 0:NS * 2].rearrange("p (s t) -> p s t", t=2)
            cp("act", ub[:, :, 0:2], cv_, [tcar[l][m]], [tup[w]])
            cp("act", ub[:, :, 2:W2], ps[:, 0:ntok].rearrange("p (s t) -> p s t", t=CTL), [tps], [tup[w]])
            cp("dve", cv_, ub[:, :, CTL:CTL + 2], [tup[w]], [tcar[l][m]])
            cv3 = cs_[:, w, 0:ntok].rearrange("p (s t) -> p s t", t=CTL)
            act(cv3, ub[:, :, 2:W2], AF.Identity, [tup[w], tconst], [tcs[w]], bias=cbs[:, l, m:m + 1], scale=cws[:, l, m, 2:3])
            stt(cv3, ub[:, :, 1:1 + CTL], cws[:, l, m, 1:2], cv3, ALU.mult, ALU.add, [tup[w], tconst, tcs[w]], [tcs[w]])
            stt(cv3, ub[:, :, 0:CTL], cws[:, l, m, 0:1], cv3, ALU.mult, ALU.add, [tup[w], tconst, tcs[w]], [tcs[w]])
            if w == 1:
                mh = b // 2
                A_ = cs_[:, 0, 0:ntok]; G_ = cs_[:, 1, 0:ntok]
                act(g1[:, pp, 0:ntok], G_, AF.Tanh, [tcs[1], tg1], [tgs[pp]], scale=0.5)
                stt(g1[:, pp, 0:ntok], g1[:, pp, 0:ntok], 1.0, G_, ALU.add, ALU.mult, [tgs[pp], tcs[1]], [tgs[pp]])
                stt(hid[:, mh, 0:ntok], g1[:, pp, 0:ntok], 0.5, A_, ALU.mult, ALU.mult, [tgs[pp], tcs[0]], [thid])
        linear(42, up_load, up_mm, up_cb, th_, wtrk=[twbs[("up", l)]])
        if sample:
            P.dma("pool", d_out, o_cs[l], car[:, :, 0:32].rearrange("p m (s t) -> p m s t", t=2), reads=tcar[l])
        elif bidx == n_batches - 1:
            P.dma("pool", d_out, o_cp[l], car[:, :, 0:2], reads=tcar[l])

        def dn_load(b, slot):
            k0, nk = (0, 11) if b % 2 == 0 else (11, 10)
            return [(slot[:, 0:nk, :], wb_down[l, :, k0:k0 + nk, (b // 2) * 128:(b // 2 + 1) * 128])]

        def dn_mm(b, slot):
            k0, nk = (0, 11) if b % 2 == 0 else (11, 10)
            return [(slot[:, k, :], hid[:, k0 + k, 0:ntok], (lambda ps: ps[:, 0:ntok])) for k in range(nk)]

        def dn_cb(b, ps, tps):
            tt("dve", xT[:, b, 0:ntok], ps[:, 0:ntok], xT[:, b, 0:ntok], ALU.add, [tps, tx[b]], [tx[b]])
        linear(16, dn_load, dn_mm, dn_cb, [thid], group=2, wtrk=[twbs[("down", l)]])

    d_x = P.new_dsem("d_x"); d_y = P.new_dsem("d_y")
    for l in range(L):
        mset("dve", carry[l][:], 0.0, tcar[l])
        for ri in range(2):
            mset("dve", Xp[l][ri][:], 0.0, [tXp[l][ri]])
    pstep = [0, 0]
    for b in range(n_batches):
        P.dma("pool", d_x, xT[:], xpT[:, :, b * NT:(b + 1) * NT], writes=tx)
        for l in range(L):
            grp = dict(sample=False, ntok=NT, NS=1, TL=TLP, b=b, step=pstep[l])
            run_layer(l, grp)
            pstep[l] = grp["step"]
        rmsnorm(xT, tx, 8, 128, lambda k: gfin[:, k:k + 1], NT, lambda k: yout[:, k, :], [tyo, tg1, tg2] + tgs, D)
        P.dma("pool", d_y, o_ypT[:, :, b * NT:(b + 1) * NT], yout[:], reads=[tyo, tg1, tg2] + tgs)
    if do_sample:
        P.dma("pool", d_x, xT[:, :, 0:64], xsT, writes=tx)
        for l in range(L):
            cur = pstep[l] % 2
            d_s1 = P.new_dsem("d_s1_%d" % l); d_s2 = P.new_dsem("d_s2_%d" % l); d_s3 = P.new_dsem("d_s3_%d" % l)
            P.dma("pool", d_s1, Xst[l][0][cur][:], sre[l], writes=[tX[l][0][cur]])
            P.dma("pool", d_s2, Xst[l][1][cur][:], sim[l], writes=[tX[l][1][cur]])
            P.dma("pool", d_s3, scarry[l][:, :, 0:32].rearrange("p m (s t) -> p m s t", t=2), sconv[l], writes=tcar[l])
            grp = dict(sample=True, ntok=64, NS=16, TL=4, step=pstep[l])
            run_layer(l, grp)
        rmsnorm(xT, tx, 8, 128, lambda k: gfin[:, k:k + 1], 64, lambda k: yout[:, k, 0:64], [tyo, tg1, tg2] + tgs, D)
        P.dma("pool", d_y, o_ysT, yout[:, :, 0:64], reads=[tyo, tg1, tg2] + tgs)
    P.finish()
    return nc


def _consts():
    ident = np.eye(128, dtype=np.float32)
    ones = np.ones((128, 128), np.float32)
    slopes = np.array([2.0 ** -(h + 1) for h in range(8)], np.float32).reshape(2, 4)
    k = np.arange(128)[:, None]; q = np.arange(128)[None, :]
    biasP = np.full((128, 2, 2, 4, 128), NEG, np.float32)
    for kv in range(2):
        for g in range(4):
            sl = slopes[kv, g]
            dcur = q - k
            biasP[:, 1, kv, g, :] = np.where(dcur >= 0, -sl * dcur, NEG)
            dprev = q + 128 - k
            biasP[:, 0, kv, g, :] = np.where(dprev < 128, -sl * dprev, NEG)
    biasP = biasP.reshape(128, 2, 2, 512)
    j = np.arange(128)[:, None]; tq = np.arange(4)[None, :]
    bSc = np.full((128, 2, 16, 4, 4), NEG, np.float32)
    bSn = np.full((4, 2, 16, 4, 4), NEG, np.float32)
    tk_ = np.arange(4)[:, None]
    for kv in range(2):
        for g in range(4):
            sl = slopes[kv, g]
            dist = tq + 128 - j
            bSc[:, kv, :, g, :] = np.where(dist < 128, -sl * dist, NEG)[:, None, :]
            dn = tq - tk_
            bSn[:, kv, :, g, :] = np.where(dn >= 0, -sl * dn, NEG)[:, None, :]
    return dict(c_ident=ident, c_ones=ones, c_biasP=biasP, c_biasSc=bSc.reshape(128, 2, 256), c_biasSn=bSn.reshape(4, 2, 256))


def _ktile(w, kt):
    Lw, K, N = w.shape
    return np.ascontiguousarray(w.reshape(Lw, kt, K // kt, N).transpose(0, 2, 1, 3))


def _vec(v, kt):
    Lw, F = v.shape
    return np.ascontiguousarray(v.reshape(Lw, kt, F // kt).transpose(2, 0, 1))


def _l0(a):
    sh = a.shape
    a = a.reshape(sh[0], 16, 2, 64, *sh[3:])
    perm = (2, 3, 0, 1) + tuple(range(4, a.ndim))
    a = a.transpose(perm)
    return np.ascontiguousarray(a.reshape(128, sh[0], 16, *sh[3:]))


_PROG = {}


def kernel(x_prompt, x_sample, cache_k_win, cache_v_win, state_ssm_re, state_ssm_im, state_conv,
           norm_mix, w_in, sinks, lam_re, lam_im, log_step, b_re, b_im, c_re, c_im, d_skip,
           w_glu, b_glu, g_attn, g_ssm, w_out, norm_ffn, w_up, conv_w, conv_b, w_down, norm_final):
    f = lambda a: np.ascontiguousarray(np.asarray(a, dtype=np.float32))
    (x_prompt, x_sample, cache_k_win, cache_v_win, state_ssm_re, state_ssm_im, state_conv, norm_mix, w_in, sinks,
     lam_re, lam_im, log_step, b_re, b_im, c_re, c_im, d_skip, w_glu, b_glu, g_attn, g_ssm, w_out, norm_ffn, w_up,
     conv_w, conv_b, w_down, norm_final) = [f(a) for a in (
        x_prompt, x_sample, cache_k_win, cache_v_win, state_ssm_re, state_ssm_im, state_conv, norm_mix, w_in, sinks,
        lam_re, lam_im, log_step, b_re, b_im, c_re, c_im, d_skip, w_glu, b_glu, g_attn, g_ssm, w_out, norm_ffn, w_up,
        conv_w, conv_b, w_down, norm_final)]
    if "nc" not in _PROG:
        _PROG["nc"] = build_program()
    nc = _PROG["nc"]
    shared = dict(_consts())
    shared.update(
        w_in=_ktile(w_in, 8), w_outa=_ktile(w_out[:, :512], 8), w_outb=_ktile(w_out[:, 512:], 4),
        w_glu=_ktile(w_glu, 4), w_up=_ktile(w_up, 8), w_down=_ktile(w_down, 21),
        g_mix=_vec(norm_mix, 8), g_ffn=_vec(norm_ffn, 8), g_fin=_vec(norm_final[None], 8)[:, 0],
        g_attn=_vec(g_attn, 8), g_ssm=_vec(g_ssm, 4), b_glu=_vec(b_glu, 4), d_skip=_vec(d_skip, 4),
        cw=np.ascontiguousarray(conv_w.reshape(L, 3, MT_UP, 128).transpose(3, 0, 2, 1)),
        cb=_vec(conv_b, MT_UP),
        sinkP=np.ascontiguousarray(np.broadcast_to(sinks[None], (64, L, 8))),
        lamre=_l0(lam_re), lamim=_l0(lam_im),
        lstep=_l0(np.broadcast_to(log_step[:, :, None], (L, 32, 64))),
        bre=_l0(b_re), bim=_l0(b_im),
        cre=_l0(c_re.transpose(0, 1, 3, 2)), cim=_l0(c_im.transpose(0, 1, 3, 2)),
    )
    shared = {k: np.ascontiguousarray(v, dtype=np.float32) for k, v in shared.items()}
    in_maps = []
    for c in range(8):
        s0 = 16 * c
        m = dict(shared)
        m["xpT"] = np.ascontiguousarray(x_prompt[c].T.reshape(8, 128, SEQ).transpose(1, 0, 2))
        m["xsT"] = np.ascontiguousarray(x_sample[s0:s0 + 16].reshape(64, D).T.reshape(8, 128, 64).transpose(1, 0, 2))
        ckc = cache_k_win[:, s0:s0 + 16].reshape(L, 16, 128, 128); cvc = cache_v_win[:, s0:s0 + 16].reshape(L, 16, 128, 128)
        m["ck"] = np.ascontiguousarray(ckc.transpose(0, 2, 1, 3)); m["cv"] = np.ascontiguousarray(cvc.transpose(0, 2, 1, 3))
        m["ckr"] = np.ascontiguousarray(ckc); m["cvr"] = np.ascontiguousarray(cvc)
        def st(a):
            a = a[:, s0:s0 + 16].reshape(L, 16, 16, 2, 64).transpose(0, 3, 4, 2, 1)
            return np.ascontiguousarray(a.reshape(L, 128, 16, 16))
        m["sre"] = st(state_ssm_re); m["sim"] = st(state_ssm_im)
        m["sconv"] = np.ascontiguousarray(state_conv[:, s0:s0 + 16].reshape(L, 16, 2, MT_UP, 128).transpose(0, 4, 3, 1, 2))
        in_maps.append(m)
    res = run_bass_kernel_spmd(nc, in_maps, core_ids=list(range(8)))
    R = res.results
    B = 8
    y_p = np.stack([R[c]["o_ypT"].transpose(1, 0, 2).reshape(D, SEQ).T for c in range(B)])
    y_s = np.concatenate([R[c]["o_ysT"].transpose(1, 0, 2).reshape(D, 64).T.reshape(16, 4, D) for c in range(B)])
    k_p = np.stack([R[c]["o_kp"] for c in range(B)], 1).reshape(L, B, 128, 2, 64)
    v_p = np.stack([R[c]["o_vp"] for c in range(B)], 1).reshape(L, B, 128, 2, 64)

    def unst_p(key):
        a = np.stack([R[c][key] for c in range(B)], 1)
        a = a.reshape(L, B, 2, 64, 16).transpose(0, 1, 4, 2, 3)
        return np.ascontiguousarray(a.reshape(L, B, 32, 64))
    sr_p = unst_p("o_srp"); si_p = unst_p("o_sip")
    c_p = np.stack([R[c]["o_cp"] for c in range(B)], 1)
    c_p = np.ascontiguousarray(c_p.transpose(0, 1, 4, 3, 2).reshape(L, B, 2, 2 * DFF))
    k_s = np.concatenate([R[c]["o_ks"] for c in range(B)], 1).reshape(L, 128, 128, 2, 64)
    v_s = np.concatenate([R[c]["o_vs"] for c in range(B)], 1).reshape(L, 128, 128, 2, 64)

    def unst_s(key):
        a = np.stack([R[c][key] for c in range(B)], 1)
        a = a.reshape(L, B, 2, 64, 16, 16).transpose(0, 1, 5, 4, 2, 3)
        return np.ascontiguousarray(a.reshape(L, B * 16, 32, 64))
    sr_s = unst_s("o_srs"); si_s = unst_s("o_sis")
    c_s = np.stack([R[c]["o_cs"] for c in range(B)], 1)
    c_s = np.ascontiguousarray(c_s.transpose(0, 1, 4, 5, 3, 2).reshape(L, B * 16, 2, 2 * DFF))
    outs = (y_p, y_s, k_p, v_p, sr_p, si_p, c_p, k_s, v_s, sr_s, si_s, c_s)
    return tuple(np.ascontiguousarray(o, dtype=np.float32) for o in outs)
```

```python
import numpy as np
import concourse.bass as bass
import concourse.mybir as mybir

F32 = mybir.dt.float32
BF16 = mybir.dt.bfloat16
I32 = mybir.dt.int32
ALU = mybir.AluOpType
AF = mybir.ActivationFunctionType
AX = mybir.AxisListType

ENGS = ("pe", "act", "dve", "pool", "sp")


class Trk:
    __slots__ = ("name", "w", "rs", "excl")

    def __init__(self, name="", excl=False):
        self.name = name
        self.excl = excl
        self.w = None
        self.rs = []


class Op:
    __slots__ = ("eng", "idx", "fn", "deps", "flag", "dma", "val")

    def __init__(self, eng, idx, fn, deps, dma):
        self.eng, self.idx, self.fn, self.deps, self.dma = eng, idx, fn, deps, dma
        self.flag = False
        self.val = None


class Prog:
    def __init__(self, nc):
        self.nc = nc
        self.ops = {e: [] for e in ENGS}
        self.dsems = []
        self._ctx = []

    def enter(self, cm):
        v = cm.__enter__()
        self._ctx.append(cm)
        return v

    def sbuf(self, name, shape, dt):
        return self.enter(self.nc.sbuf_tensor(name, list(shape), dt))

    def psum(self, name, shape, dt=F32):
        return self.enter(self.nc.psum_tensor(name, list(shape), dt))

    def new_dsem(self, name):
        s = self.enter(self.nc.semaphore(name))
        d = {"sem": s, "cnt": 0}
        self.dsems.append(d)
        return d

    def _deps(self, eng, reads, writes):
        deps = []
        for t in reads:
            if t.w is not None:
                deps.append(t.w)
        for t in writes:
            if t.w is not None:
                deps.append(t.w)
            deps.extend(t.rs)
        return deps

    def op(self, eng, fn, reads=(), writes=()):
        ex = [t for t in reads if t.excl]
        if ex:
            reads = [t for t in reads if not t.excl]
            writes = list(writes) + ex
        deps = self._deps(eng, reads, writes)
        o = Op(eng, len(self.ops[eng]), fn, deps, None)
        self.ops[eng].append(o)
        for t in reads:
            t.rs.append(o)
        for t in writes:
            t.w = o
            t.rs = []
        return o

    def dma(self, q, dsem, out, in_, reads=(), writes=(), **kw):
        deps = self._deps(q, reads, writes)

        def fn(e):
            return e.dma_start(out=out, in_=in_, **kw)
        o = Op(q, len(self.ops[q]), fn, deps, dsem)
        dsem["cnt"] += 16
        o.val = dsem["cnt"]
        o.flag = True
        self.ops[q].append(o)
        for t in reads:
            t.rs.append(o)
        for t in writes:
            t.w = o
            t.rs = []
        return o

    def barrier(self, eng, dsem, fn, trks):
        dep = Op("sp", -1, None, [], dsem)
        dep.val = dsem["cnt"]
        dep.flag = True
        deps = [dep] + [t.w for t in trks if t.w is not None]
        o = Op(eng, len(self.ops[eng]), fn, deps, None)
        self.ops[eng].append(o)
        for t in trks:
            t.w = o
            t.rs = []
        return o

    def finish(self, final_waits=()):
        nc = self.nc
        for e in ENGS:
            for o in self.ops[e]:
                for d in o.deps:
                    if d.dma is None:
                        if d.eng == "pe" and e == "pe":
                            continue
                        d.flag = True
        esem = {}
        for e in ENGS:
            esem[e] = self.enter(nc.semaphore("esem_" + e))
            c = 0
            for o in self.ops[e]:
                if o.dma is None:
                    if o.flag:
                        c += 1
                        o.val = c
                    else:
                        o.val = None
        nxt = {}
        for e in ENGS:
            arr = [None] * len(self.ops[e])
            cur = None
            for i in range(len(self.ops[e]) - 1, -1, -1):
                o = self.ops[e][i]
                if o.dma is None and o.flag:
                    cur = o.val
                arr[i] = cur
            nxt[e] = arr
        self.nwaits = 0
        prog = self

        def emit(ename):
            def body(eh):
                seen = {}
                for o in prog.ops[ename]:
                    need = {}
                    for d in o.deps:
                        if d.dma is not None:
                            key = ("d", id(d.dma))
                            sem, val = d.dma["sem"], d.val
                        else:
                            if d.eng == "pe" and ename == "pe":
                                continue
                            key = ("e", d.eng)
                            sem, val = esem[d.eng], d.val
                            assert val is not None
                        if seen.get(key, 0) >= val:
                            continue
                        if key not in need or need[key][1] < val:
                            need[key] = (sem, val)
                    for key, (sem, val) in need.items():
                        eh.wait_ge(sem, val)
                        seen[key] = val
                        prog.nwaits += 1
                    if o.fn is None:
                        continue
                    ins = o.fn(eh)
                    if o.dma is not None:
                        ins.then_inc(o.dma["sem"], 16)
                    elif o.flag:
                        ins.then_inc(esem[ename], 1)
                if ename == "sp":
                    for d in prog.dsems:
                        if d["cnt"] > 0:
                            eh.wait_ge(d["sem"], d["cnt"])
            return body

        with nc.Block() as block:
            block.tensor(emit("pe"))
            block.scalar(emit("act"))
            block.vector(emit("dve"))
            block.gpsimd(emit("pool"))
            block.sync(emit("sp"))
        for cm in reversed(self._ctx):
            cm.__exit__(None, None, None)
        self._ctx = []

import math
from concourse.bass_utils import run_bass_kernel_spmd

D = 1024; L = 2; SEQ = 4096; NB = 16; NT = 256; TLP = 64; NTI = 2; DFF = 2688; MT_UP = 42
NEG = -30000.0
EPS = 1e-5
PI = math.pi


def build_program(n_batches=NB, do_sample=True):
    nc = bass.Bass("TRN2", target_bir_lowering=False)
    P = Prog(nc)

    def din(name, shape):
        return nc.dram_tensor(name, list(shape), F32, kind="ExternalInput").ap()

    def dout(name, shape):
        return nc.dram_tensor(name, list(shape), F32, kind="ExternalOutput").ap()

    xpT = din("xpT", [128, 8, SEQ]); xsT = din("xsT", [128, 8, 64])
    ck = din("ck", [L, 128, 16, 128]); cv = din("cv", [L, 128, 16, 128])
    ckr = din("ckr", [L, 16, 128, 128]); cvr = din("cvr", [L, 16, 128, 128])
    sre = din("sre", [L, 128, 16, 16]); sim = din("sim", [L, 128, 16, 16])
    sconv = din("sconv", [L, 128, MT_UP, 16, 2])
    w_in = din("w_in", [L, 128, 8, 1280])
    w_outa = din("w_outa", [L, 64, 8, 1024]); w_outb = din("w_outb", [L, 128, 4, 1024])
    w_glu = din("w_glu", [L, 128, 4, 512])
    w_up = din("w_up", [L, 128, 8, 2 * DFF]); w_down = din("w_down", [L, 128, 21, 1024])
    g_mix = din("g_mix", [128, L, 8]); g_ffn = din("g_ffn", [128, L, 8]); g_fin = din("g_fin", [128, 8])
    g_attn = din("g_attn", [64, L, 8]); g_ssm = din("g_ssm", [128, L, 4])
    b_glu = din("b_glu", [128, L, 4]); d_skip = din("d_skip", [128, L, 4])
    cw = din("cw", [128, L, MT_UP, 3]); cb = din("cb", [128, L, MT_UP])
    sinkP = din("sinkP", [64, L, 8])
    lamre = din("lamre", [128, L, 16]); lamim = din("lamim", [128, L, 16]); lstep = din("lstep", [128, L, 16])
    bre = din("bre", [128, L, 16, 16]); bim = din("bim", [128, L, 16, 16])
    cre = din("cre", [128, L, 16, 16]); cim = din("cim", [128, L, 16, 16])
    c_ident = din("c_ident", [128, 128]); c_ones = din("c_ones", [128, 128])
    c_biasP = din("c_biasP", [128, 2, 2, 512]); c_biasSc = din("c_biasSc", [128, 2, 256]); c_biasSn = din("c_biasSn", [4, 2, 256])

    o_ypT = dout("o_ypT", [128, 8, SEQ]); o_ysT = dout("o_ysT", [128, 8, 64])
    o_kp = dout("o_kp", [L, 128, 128]); o_vp = dout("o_vp", [L, 128, 128])
    o_srp = dout("o_srp", [L, 128, 16]); o_sip = dout("o_sip", [L, 128, 16])
    o_cp = dout("o_cp", [L, 128, MT_UP, 2])
    o_ks = dout("o_ks", [L, 16, 128, 128]); o_vs = dout("o_vs", [L, 16, 128, 128])
    o_srs = dout("o_srs", [L, 128, 16, 16]); o_sis = dout("o_sis", [L, 128, 16, 16])
    o_cs = dout("o_cs", [L, 128, MT_UP, 16, 2])

    def dscr(name, shape):
        return nc.dram_tensor(name, list(shape), BF16, kind="Internal").ap()
    wb_in = dscr("wb_in", [L, 128, 8, 1280]); wb_outa = dscr("wb_outa", [L, 64, 8, 1024]); wb_outb = dscr("wb_outb", [L, 128, 4, 1024])
    wb_glu = dscr("wb_glu", [L, 128, 4, 512]); wb_up = dscr("wb_up", [L, 128, 8, 2 * DFF]); wb_down = dscr("wb_down", [L, 128, 21, 1024])

    def T(n=1):
        return [Trk() for _ in range(n)] if n > 1 else Trk()

    def act(out, in_, func, reads, writes, bias=None, scale=None):
        kw = {}
        if bias is not None:
            kw["bias"] = bias
        if scale is not None:
            kw["scale"] = scale
        return P.op("act", lambda e: e.activation(out, in_, func, **kw), reads, writes)

    def tt(eng, out, a, b, op, reads, writes):
        return P.op(eng, lambda e: e.tensor_tensor(out, a, b, op), reads, writes)

    def ts(eng, out, a, s1, s2, op0, op1, reads, writes):
        return P.op(eng, lambda e: e.tensor_scalar(out, a, s1, s2, op0, op1), reads, writes)

    def stt(out, a, s, b, op0, op1, reads, writes):
        return P.op("dve", lambda e: e.scalar_tensor_tensor(out, a, s, b, op0, op1), reads, writes)

    def cp(eng, out, in_, reads, writes):
        if eng == "act":
            return P.op("act", lambda e: e.activation(out, in_, AF.Copy), reads, writes)
        return P.op(eng, lambda e: e.tensor_copy(out, in_), reads, writes)

    def mm(out, lhsT, rhs, start, stop, reads, writes):
        return P.op("pe", lambda e: e.matmul(out, lhsT, rhs, start=start, stop=stop), reads, writes)

    def mset(eng, ap, val, writes):
        return P.op(eng, lambda e: e.memset(ap, val), (), writes)

    def recip(out, in_, reads, writes):
        return P.op("dve", lambda e: e.reciprocal(out, in_), reads, writes)

    d_pre = P.new_dsem("d_pre"); d_kc = P.new_dsem("d_kc"); d_vc = P.new_dsem("d_vc")
    d_kvn = [P.new_dsem("d_kvn0"), P.new_dsem("d_kvn1")]; d_so = [P.new_dsem("d_so0"), P.new_dsem("d_so1")]
    d_out = P.new_dsem("d_out")

    d_pre2 = P.new_dsem("d_pre2")

    def load(dst, src, trk, q="sp"):
        return P.dma(q, d_pre if q == "sp" else d_pre2, dst, src, writes=[trk])

    ident_f = P.sbuf("ident_f", [128, 128], F32); ident_b = P.sbuf("ident_b", [128, 128], BF16)
    ones_b = P.sbuf("ones_b", [128, 128], BF16)
    biasP = P.sbuf("biasP", [128, 2, 2, 512], BF16)
    biasSc = P.sbuf("biasSc", [128, 2, 256], BF16); biasSn = P.sbuf("biasSn", [4, 2, 256], BF16)
    tconst = T()
    load(ident_f[:], c_ident, tconst)
    load(ident_b[:], c_ident, tconst, "pool"); load(ones_b[:], c_ones, tconst, "pool")
    load(biasP[:], c_biasP, tconst, "pool"); load(biasSc[:], c_biasSc, tconst, "pool"); load(biasSn[:], c_biasSn, tconst, "pool")
    gmix = P.sbuf("gmix", [128, L, 8], F32); gffn = P.sbuf("gffn", [128, L, 8], F32); gfin = P.sbuf("gfin", [128, 8], F32)
    gattn = P.sbuf("gattn", [64, L, 8], F32); gssm = P.sbuf("gssm", [128, L, 4], F32)
    bglu = P.sbuf("bglu", [128, L, 4], F32); hbglu = P.sbuf("hbglu", [128, L, 4], F32); dsk = P.sbuf("dsk", [128, L, 4], F32)
    cws = P.sbuf("cws", [128, L, MT_UP, 3], F32); cbs = P.sbuf("cbs", [128, L, MT_UP], F32)
    esP = P.sbuf("esP", [64, L, 8], F32)
    for dst, src in ((gmix, g_mix), (gffn, g_ffn), (gfin, g_fin), (gattn, g_attn), (gssm, g_ssm), (bglu, b_glu),
                     (dsk, d_skip), (cws, cw), (cbs, cb), (esP, sinkP)):
        load(dst[:], src, tconst)
    epsT = P.sbuf("epsT", [128, 1], F32); hpiT = P.sbuf("hpiT", [128, 1], F32)
    P.barrier("dve", d_pre, lambda e: e.memset(epsT[:], EPS), [tconst])
    P.barrier("dve", d_pre2, lambda e: e.memset(hpiT[:], PI / 2), [tconst])
    mset("dve", epsT[:], EPS, [tconst]); mset("dve", hpiT[:], PI / 2, [tconst])
    act(esP[:], esP[:], AF.Exp, [tconst], [tconst])
    ts("dve", hbglu[:], bglu[:], 0.5, None, ALU.mult, ALU.bypass, [tconst], [tconst])

    twbs = {}
    for l_ in range(L):
        for nm_, dst_, src_, kt_ in (("in", wb_in, w_in, 8), ("outa", wb_outa, w_outa, 8), ("outb", wb_outb, w_outb, 4),
                                     ("glu", wb_glu, w_glu, 4), ("up", wb_up, w_up, 8), ("down", wb_down, w_down, 21)):
            d_cvt = P.new_dsem("d_cvt_%s%d" % (nm_, l_)); t_ = Trk()
            twbs[(nm_, l_)] = t_
            for k_ in range(kt_):
                P.dma("pool", d_cvt, dst_[l_, :, k_, :], src_[l_, :, k_, :], writes=[t_])


    XT = lambda n=1: [Trk(excl=True) for _ in range(n)] if n > 1 else Trk(excl=True)
    psA = [P.psum("psA%d" % i, [128, 512]) for i in range(2)]; tA = XT(2)
    psS = [P.psum("psS%d" % i, [128, 512]) for i in range(2)]; tS = XT(2)
    psO = P.psum("psO", [128, 512]); tO = XT()
    psD = P.psum("psD", [128, 512]); tD = XT()
    psM = [P.psum("psM%d" % i, [128, 512]) for i in range(2)]; tM = XT(2)
    cntM = [0]

    def nextM():
        i = cntM[0] % 2
        cntM[0] += 1
        return psM[i], tM[i]

    s5 = []
    scr = [P.sbuf("s5scr%d" % i, [128, 16], F32) for i in range(8)]
    G12 = P.sbuf("G12", [128, 8, NT], F32)
    bmf = G12[:].rearrange("p a (b c) -> p (a b) c", c=128)
    S5tmp = P.sbuf("S5tmp", [128, 4, 16, TLP], F32); ttmp = T(); ttq = T(4)
    braw = S5tmp[:, 0].rearrange("p g t -> p (g t)")[:, 0:512].rearrange("p (a g h) -> p a g h", a=2, g=16)
    bbar = S5tmp[:, 1].rearrange("p g t -> p (g t)")[:, 0:512].rearrange("p (a g h) -> p a g h", a=2, g=16)
    craw = S5tmp[:, 2].rearrange("p g t -> p (g t)")[:, 0:512].rearrange("p (a g h) -> p a g h", a=2, g=16)
    tab = T()
    for l in range(L):
        lr = P.sbuf("lr%d" % l, [128, 16], F32); li = P.sbuf("li%d" % l, [128, 16], F32); ls = P.sbuf("ls%d" % l, [128, 16], F32)
        d_tab = P.new_dsem("d_tab%d" % l)
        for dst_, src_ in ((lr[:], lamre[:, l, :]), (li[:], lamim[:, l, :]), (ls[:], lstep[:, l, :]), (braw[:, 0], bre[:, l]),
                           (braw[:, 1], bim[:, l]), (craw[:, 0], cre[:, l]), (craw[:, 1], cim[:, l])):
            P.dma("sp", d_tab, dst_, src_, writes=[tab])
        P.barrier("dve", d_tab, (lambda l_: lambda e: e.memset(scr[0][:], 0.0))(l), [tab])
        AR = P.sbuf("AR%d" % l, [128, 16], F32); AI = P.sbuf("AI%d" % l, [128, 16], F32)
        AR16 = P.sbuf("AR16_%d" % l, [128, 16, 16], F32); AI16 = P.sbuf("AI16_%d" % l, [128, 16, 16], F32)
        BT = [P.sbuf("BT%d_%d" % (l, ri), [128, 16, 128], BF16) for ri in range(2)]
        CT = [P.sbuf("CT%d_%d" % (l, ri), [128, 16, 128], BF16) for ri in range(2)]
        dt_, zr, th, rr, cc, ss, t0, t1 = [s[:] for s in scr]
        R, W = [tab], [tab]
        act(dt_, ls[:], AF.Exp, R, W)
        tt("dve", zr, lr[:], dt_, ALU.mult, R, W)
        tt("dve", th, li[:], dt_, ALU.mult, R, W)
        act(rr, zr, AF.Exp, R, W)
        act(cc, th, AF.Sin, R, W, bias=hpiT[:], scale=1.0 / 32)
        act(ss, th, AF.Sin, R, W, scale=1.0 / 32)
        for _ in range(5):
            tt("dve", t0, cc, cc, ALU.mult, R, W)
            tt("dve", t1, ss, ss, ALU.mult, R, W)
            tt("dve", ss, ss, cc, ALU.mult, R, W)
            ts("dve", ss, ss, 2.0, None, ALU.mult, ALU.bypass, R, W)
            tt("dve", cc, t0, t1, ALU.subtract, R, W)
        tt("dve", AR[:], rr, cc, ALU.mult, R, W)
        tt("dve", AI[:], rr, ss, ALU.mult, R, W)
        for s_ in range(16):
            cp("dve", AR16[:, :, s_], AR[:], R, W); cp("dve", AI16[:, :, s_], AI[:], R, W)
        RR = P.sbuf("RR%d" % l, [128, 16], F32)
        cp("dve", RR[:], rr, R, W)
        C1 = P.sbuf("C1_%d" % l, [128, 16, TLP], F32); S1 = P.sbuf("S1_%d" % l, [128, 16, TLP], F32)
        cp("dve", C1[:, :, 0], cc, R, W); cp("dve", S1[:, :, 0], ss, R, W)
        kk_ = 1
        while kk_ < TLP:
            cp("dve", t0, C1[:, :, kk_ - 1], R, W); cp("dve", t1, S1[:, :, kk_ - 1], R, W)
            ts("dve", zr, t1, -1.0, None, ALU.mult, ALU.bypass, R, W)
            for gp in range(16):
                ts("dve", C1[:, gp, kk_:2 * kk_], C1[:, gp, 0:kk_], t0[:, gp:gp + 1], None, ALU.mult, ALU.bypass, R, W)
                stt(C1[:, gp, kk_:2 * kk_], S1[:, gp, 0:kk_], zr[:, gp:gp + 1], C1[:, gp, kk_:2 * kk_], ALU.mult, ALU.add, R, W)
                ts("dve", S1[:, gp, kk_:2 * kk_], S1[:, gp, 0:kk_], t0[:, gp:gp + 1], None, ALU.mult, ALU.bypass, R, W)
                stt(S1[:, gp, kk_:2 * kk_], C1[:, gp, 0:kk_], t1[:, gp:gp + 1], S1[:, gp, kk_:2 * kk_], ALU.mult, ALU.add, R, W)
            kk_ *= 2
        nr, den, cr, ci = dt_, zr, th, rr
        ts("dve", nr, AR[:], -1.0, None, ALU.add, ALU.bypass, R, W)
        tt("dve", t0, lr[:], lr[:], ALU.mult, R, W)
        tt("dve", t1, li[:], li[:], ALU.mult, R, W)
        tt("dve", den, t0, t1, ALU.add, R, W)
        recip(den, den, R, W)
        tt("dve", t0, nr, lr[:], ALU.mult, R, W)
        tt("dve", t1, AI[:], li[:], ALU.mult, R, W)
        tt("dve", t0, t0, t1, ALU.add, R, W)
        tt("dve", cr, t0, den, ALU.mult, R, W)
        tt("dve", t0, AI[:], lr[:], ALU.mult, R, W)
        tt("dve", t1, nr, li[:], ALU.mult, R, W)
        tt("dve", t0, t0, t1, ALU.subtract, R, W)
        tt("dve", ci, t0, den, ALU.mult, R, W)
        nci = cc
        ts("dve", nci, ci, -1.0, None, ALU.mult, ALU.bypass, R, W)
        for gp in range(16):
            ts("dve", bbar[:, 0, gp, :], braw[:, 0, gp, :], cr[:, gp:gp + 1], None, ALU.mult, ALU.bypass, R, W)
            stt(bbar[:, 0, gp, :], braw[:, 1, gp, :], nci[:, gp:gp + 1], bbar[:, 0, gp, :], ALU.mult, ALU.add, R, W)
            ts("dve", bbar[:, 1, gp, :], braw[:, 1, gp, :], cr[:, gp:gp + 1], None, ALU.mult, ALU.bypass, R, W)
            stt(bbar[:, 1, gp, :], braw[:, 0, gp, :], ci[:, gp:gp + 1], bbar[:, 1, gp, :], ALU.mult, ALU.add, R, W)
        for ri in range(2):
            mset("dve", bmf[:], 0.0, W)
            for gp in range(16):
                c0 = 32 * (gp % 4)
                cp("dve", bmf[0:64, gp, c0:c0 + 16], bbar[0:64, ri, gp, :], R, W)
                cp("dve", bmf[64:128, gp, c0 + 16:c0 + 32], bbar[64:128, ri, gp, :], R, W)
            for gp in range(16):
                pm, tm = nextM()
                P.op("pe", (lambda pm_, gp_: lambda e: e.transpose(pm_[:, 0:128], bmf[:, gp_, :], ident_f[:]))(pm, gp),
                     [tab, tconst], [tm])
                cp("act", BT[ri][:, gp, :], pm[:, 0:128], [tm], [tab])
            mset("dve", CT[ri][:], 0.0, W)
            for gp in range(16):
                c0 = 32 * (gp % 4)
                sc = 1.0 if ri == 0 else -1.0
                ts("dve", CT[ri][0:64, gp, c0:c0 + 16], craw[0:64, ri, gp, :], sc, None, ALU.mult, ALU.bypass, R, W)
                ts("dve", CT[ri][64:128, gp, c0 + 16:c0 + 32], craw[64:128, ri, gp, :], sc, None, ALU.mult, ALU.bypass, R, W)
        s5.append(dict(AR=AR, AI=AI, AR16=AR16, AI16=AI16, BT=BT, CT=CT, RR=RR, C1=C1, S1=S1))

    xT = P.sbuf("xT", [128, 8, NT], F32); tx = T(8)
    hT = P.sbuf("hT", [128, 8, NT], BF16); th_ = T(8)
    rstd = P.sbuf("rstd", [128, NT], F32); trs = T()
    qT = P.sbuf("qT", [64, 8, NT], BF16); tq = T()
    kT = [P.sbuf("kT%d" % l, [64, 2, 128 + NT], BF16) for l in range(L)]; tk = T(2)
    vtok = [P.sbuf("vtok%d" % l, [128, NTI + 1, 128], BF16) for l in range(L)]; tv = T(2)
    kvf = P.sbuf("kvf", [128, 256], F32); tkvf = T()
    wkv = P.sbuf("wkv", [128, 8, 256], BF16); twkv = T(); d_wkv = P.new_dsem("d_wkv")
    uT = P.sbuf("uT", [128, 4, NT], BF16); tu = T()
    PT = P.sbuf("PT", [128, 2, 512], BF16); tPT = T()
    den_sb = P.sbuf("den_sb", [64, 512], F32); tden = T()
    attnT = P.sbuf("attnT", [64, 8, NT], F32); tat = T()
    attnB = P.sbuf("attnB", [64, 8, NT], BF16); tatb = T()
    bu = [P.sbuf("bu%d" % ri, [128, 16, 64], F32) for ri in range(2)]; tbu = T(2)
    xs5 = [P.sbuf("xs5_%d" % ri, [128, 16, 64], BF16) for ri in range(2)]; txs = T(2)
    Xs_ = [[P.sbuf("Xs_%d_%d" % (ri, pp), [128, 16, 16], F32) for pp in range(2)] for ri in range(2)]
    tXs_ = [[T() for pp in range(2)] for ri in range(2)]
    Xst = [Xs_, Xs_]; tX = [tXs_, tXs_]
    Xp = [[P.sbuf("Xp%d_%d" % (l, ri), [128, 16], F32) for ri in range(2)] for l in range(L)]
    tXp = [[T() for ri in range(2)] for l in range(L)]
    stmp = [P.sbuf("stmp%d" % i, [128, 16, 16], F32) for i in range(4)]; tst = T(4)
    yT = P.sbuf("yT", [128, 4, NT], F32); ty = T()
    tg1 = T(); tg2 = T()
    g1 = G12[:, 0:4, :]; g2 = G12[:, 4:8, :]
    sbf = P.sbuf("sbf", [128, 4, NT], BF16); tsb = T()
    ssmB = P.sbuf("ssmB", [128, 4, NT], BF16); tsmb = T()
    upb2 = [P.sbuf("upb%d" % i, [128, 2, NT + 32], F32) for i in range(2)]; tup2 = [T(2), T(2)]
    cs2 = [P.sbuf("cs_%d" % i, [128, 2, NT], F32) for i in range(2)]; tcs2 = [T(2), T(2)]; tgs = T(2)
    hid = P.sbuf("hid", [128, 21, NT], BF16); thid = T()
    sq = hid[:, 0:8, :]; tsq = thid
    carry = [P.sbuf("carry%d" % l, [128, MT_UP, 2], F32) for l in range(L)]; tcar = [T(MT_UP), T(MT_UP)]
    scarry1 = P.sbuf("scarry", [128, MT_UP, 32], F32); scarry = [scarry1, scarry1]
    kc_f = S5tmp[:, 0:2].rearrange("p a g t -> p (a g t)").rearrange("p (s c) -> p s c", c=128); tkc = ttmp
    kcT = P.sbuf("kcT", [64, 16, 128], BF16); tkcT = T()
    vc_b = P.sbuf("vc_b", [128, 16, 128], BF16); tvc = T()
    kvn_f = P.sbuf("kvn_f", [4, 2, 256], F32); tkvn = T(2)
    vn_b = P.sbuf("vn_b", [4, 16, 128], BF16); tvn = T()
    yout = G12; tyo = T()

    NSLOT = 4
    wsl = [P.sbuf("wsl%d" % i, [128, 8, 128], BF16) for i in range(NSLOT)]
    twsl = T(NSLOT); dwsl = [P.new_dsem("dw%d" % i) for i in range(NSLOT)]
    wcnt = [0]

    def linear(nblk, load_fn, mm_fn, cb_fn, rtrks, group=1, wtrk=()):
        base = wcnt[0]
        wcnt[0] += nblk

        def issue(b):
            si = (base + b) % NSLOT
            for dst, src in load_fn(b, wsl[si]):
                P.dma("sp", dwsl[si], dst, src, reads=list(wtrk), writes=[twsl[si]])
        for b in range(min(NSLOT - 1, nblk)):
            issue(b)
        for b in range(nblk):
            if b + NSLOT - 1 < nblk:
                issue(b + NSLOT - 1)
            si = (base + b) % NSLOT
            ps, tps = psA[(b // group) % 2], tA[(b // group) % 2]
            pairs = mm_fn(b, wsl[si])
            out_ap = pairs[0][2]
            for i, (lt, rh, _) in enumerate(pairs):
                mm(out_ap(ps), lt, rh, i == 0 and b % group == 0, i == len(pairs) - 1 and b % group == group - 1,
                   [twsl[si]] + rtrks, [tps])
            if b % group == group - 1:
                cb_fn(b // group, ps, tps)

    def rmsnorm(src, tsrc, nk, npart, gain_fn, ntok, dst_fn, tdst, nfeat):
        act(sq[0:npart, 0:nk, 0:ntok], src[0:npart, 0:nk, 0:ntok], AF.Square, tsrc, [tsq])
        pm, tm = nextM()
        for k in range(nk):
            mm(pm[:, 0:ntok], ones_b[0:npart, :], sq[0:npart, k, 0:ntok], k == 0, k == nk - 1, [tsq, tconst], [tm])
        act(rstd[:, 0:ntok], pm[:, 0:ntok], AF.Sqrt, [tm, tconst], [trs], bias=epsT[:], scale=1.0 / nfeat)
        recip(rstd[:, 0:ntok], rstd[:, 0:ntok], [trs], [trs])
        for k in range(nk):
            stt(dst_fn(k), src[0:npart, k, 0:ntok], gain_fn(k), rstd[0:npart, 0:ntok], ALU.mult, ALU.mult,
                tsrc + [trs, tconst], tdst)

    def run_layer(l, grp):
        sample = grp["sample"]
        ntok = grp["ntok"]; NS = grp["NS"]; TL = grp["TL"]
        bidx = grp.get("b", 0)
        tb = s5[l]
        rmsnorm(xT, tx, 8, 128, lambda k: gmix[:, l, k:k + 1], ntok, lambda k: hT[:, k, 0:ntok], th_, D)
        blocks = [(h * 64, 64) for h in range(8)] + [(512 + kv * 64, 64) for kv in range(2)] + [(768 + q * 128, 128) for q in range(4)]
        koff = 0 if sample else 128

        def w_in_load(b, slot):
            c0, msz = blocks[b]
            return [(slot[:, 0:8, 0:msz], wb_in[l, :, :, c0:c0 + msz])]

        def w_in_mm(b, slot):
            c0, msz = blocks[b]
            return [(slot[:, k, 0:msz], hT[:, k, 0:ntok], (lambda ps, msz=msz: ps[0:msz, 0:ntok])) for k in range(8)]

        def w_in_cb(b, ps, tps):
            if b < 8:
                act(qT[:, b, 0:ntok], ps[0:64, 0:ntok], AF.Copy, [tps], [tq], scale=0.125)
            elif b < 10:
                cp("act", kT[l][:, b - 8, koff:koff + ntok], ps[0:64, 0:ntok], [tps], [tk[l]])
            else:
                cp("act", uT[:, b - 10, 0:ntok], ps[:, 0:ntok], [tps], [tu])
        linear(14, w_in_load, w_in_mm, w_in_cb, th_, wtrk=[twbs[("in", l)]])
        P.dma("sp", d_wkv, wkv[:], wb_in[l, :, :, 512:768], reads=[twbs[("in", l)]], writes=[twkv])
        if not sample:
            for i in range(NTI):
                pm, tm = nextM()
                for k in range(8):
                    mm(pm[:, 0:256], hT[:, k, i * 128:(i + 1) * 128], wkv[:, k, :], k == 0, k == 7, th_ + [twkv], [tm])
                cp("act", vtok[l][:, i + 1, :], pm[:, 128:256], [tm], [tv[l]])
                if bidx == n_batches - 1 and i == NTI - 1:
                    cp("dve", kvf[:], pm[:, 0:256], [tm], [tkvf])
                    P.dma("pool", d_out, o_kp[l], kvf[:, 0:128], reads=[tkvf])
                    P.dma("pool", d_out, o_vp[l], kvf[:, 128:256], reads=[tkvf])
        else:
            for s_ in range(16):
                pm, tm = nextM()
                for k in range(8):
                    mm(pm[0:4, 0:256], hT[:, k, 4 * s_:4 * s_ + 4], wkv[:, k, :], k == 0, k == 7, th_ + [twkv], [tm])
                cp("act", kvn_f[:, s_ % 2, :], pm[0:4, 0:256], [tm], [tkvn[s_ % 2]])
                cp("dve", vn_b[:, s_, :], kvn_f[:, s_ % 2, 128:256], [tkvn[s_ % 2]], [tvn])
                P.dma("pool", d_kvn[s_ % 2], o_ks[l, s_, 124:128, :], kvn_f[:, s_ % 2, 0:128], reads=[tkvn[s_ % 2]])
                P.dma("pool", d_kvn[s_ % 2], o_vs[l, s_, 124:128, :], kvn_f[:, s_ % 2, 128:256], reads=[tkvn[s_ % 2]])
            P.dma("pool", d_out, o_ks[l, :, 0:124, :], ckr[l, :, 4:128, :])
            P.dma("pool", d_out, o_vs[l, :, 0:124, :], cvr[l, :, 4:128, :])
        attn_units = []
        if not sample:
            def mk_unit(i, kv):
                first = (bidx == 0 and i == 0)
                blks = [1] if first else [0, 1]

                def p1():
                    for blk in blks:
                        kcol = i * 128 + blk * 128
                        mm(psS[blk][:, :], kT[l][:, kv, kcol:kcol + 128], qT[:, 4 * kv:4 * kv + 4, i * 128:(i + 1) * 128],
                           True, False, [tk[l], tq], [tS[blk]])
                        mm(psS[blk][:, :], ident_b[:], biasP[:, blk, kv, :], False, True, [tconst], [tS[blk]])
                        act(PT[:, blk, :], psS[blk][:, :], AF.Exp, [tS[blk]], [tPT])
                    for j, blk in enumerate(blks):
                        mm(psO[0:64, :], vtok[l][:, i + blk, kv * 64:(kv + 1) * 64], PT[:, blk, :], j == 0, j == len(blks) - 1,
                           [tv[l], tPT], [tO])
                    for j, blk in enumerate(blks):
                        mm(psD[0:64, :], ones_b[:, 0:64], PT[:, blk, :], j == 0, j == len(blks) - 1, [tconst, tPT], [tD])

                def p2():
                    for g_ in range(4):
                        ts("dve", den_sb[:, g_ * 128:(g_ + 1) * 128], psD[0:64, g_ * 128:(g_ + 1) * 128], esP[:, l, 4 * kv + g_:4 * kv + g_ + 1], None,
                           ALU.add, ALU.bypass, [tD, tconst], [tden])
                    recip(den_sb[:, :], den_sb[:, :], [tden], [tden])
                    tt("dve", attnT[:, 4 * kv:4 * kv + 4, i * 128:(i + 1) * 128], psO[0:64, :].rearrange("p (g q) -> p g q", g=4),
                       den_sb[:, :].rearrange("p (g q) -> p g q", g=4), ALU.mult, [tO, tden], [tat])
                return p1, p2
            for i in range(NTI):
                for kv in range(2):
                    attn_units.append(mk_unit(i, kv))
        if sample:
            pass
        else:
            pass
        if not sample:
            pass
        else:
            P.dma("sp", d_kc, kc_f[:], ck[l], writes=[tkc, ttq[0], ttq[1]])
            P.dma("pool", d_vc, vc_b[:], cv[l], writes=[tvc])
            for kv in range(2):
                for s_ in range(16):
                    pm, tm = nextM()
                    P.op("pe", (lambda pm_, s2, kv2: lambda e: e.transpose(pm_[0:64, 0:128], kc_f[:, s2, kv2 * 64:(kv2 + 1) * 64], ident_f[:]))(pm, s_, kv),
                         [tkc, tconst], [tm])
                    cp("act", kcT[:, s_, :], pm[0:64, 0:128], [tm], [tkcT])
                mm(psS[0][:, 0:256], ident_b[:], biasSc[:, kv, :], True, False, [tconst], [tS[0]])
                for s_ in range(16):
                    mm(psS[0][:, 16 * s_:16 * s_ + 16], kcT[:, s_, :], qT[:, 4 * kv:4 * kv + 4, 4 * s_:4 * s_ + 4],
                       False, s_ == 15, [tkcT, tq], [tS[0]])
                mm(psS[1][0:4, 0:256], ident_b[0:4, 0:4], biasSn[:, kv, :], True, False, [tconst], [tS[1]])
                for s_ in range(16):
                    mm(psS[1][0:4, 16 * s_:16 * s_ + 16], kT[l][:, kv, 4 * s_:4 * s_ + 4], qT[:, 4 * kv:4 * kv + 4, 4 * s_:4 * s_ + 4],
                       False, s_ == 15, [tk[l], tq], [tS[1]])
                act(PT[:, 0, 0:256], psS[0][:, 0:256], AF.Exp, [tS[0]], [tPT])
                act(PT[0:4, 1, 0:256], psS[1][0:4, 0:256], AF.Exp, [tS[1]], [tPT])
                for s_ in range(16):
                    c = slice(16 * s_, 16 * s_ + 16)
                    mm(psO[0:64, c], vc_b[:, s_, kv * 64:(kv + 1) * 64], PT[:, 0, c], True, False, [tvc, tPT], [tO])
                    mm(psO[0:64, c], vn_b[:, s_, kv * 64:(kv + 1) * 64], PT[0:4, 1, c], False, True, [tvn, tPT], [tO])
                for s_ in range(16):
                    c = slice(16 * s_, 16 * s_ + 16)
                    mm(psD[0:64, c], ones_b[:, 0:64], PT[:, 0, c], True, False, [tconst, tPT], [tD])
                    mm(psD[0:64, c], ones_b[0:4, 0:64], PT[0:4, 1, c], False, True, [tconst, tPT], [tD])
                for g_ in range(4):
                    ts("dve", den_sb[:, 0:256].rearrange("p (s g t) -> p g s t", g=4, t=4)[:, g_],
                       psD[0:64, 0:256].rearrange("p (s g t) -> p g s t", g=4, t=4)[:, g_], esP[:, l, 4 * kv + g_:4 * kv + g_ + 1], None,
                       ALU.add, ALU.bypass, [tD, tconst], [tden])
                recip(den_sb[:, 0:256], den_sb[:, 0:256], [tden], [tden])
                tt("dve", attnT[:, 4 * kv:4 * kv + 4, 0:64].rearrange("p g (s t) -> p s g t", t=4),
                   psO[0:64, 0:256].rearrange("p (s g t) -> p s g t", g=4, t=4),
                   den_sb[:, 0:256].rearrange("p (s g t) -> p s g t", g=4, t=4), ALU.mult, [tO, tden], [tat])
        NTL = NS * TL
        ntile = ntok // NTL

        def s5_bu(it):
            tok0 = it * NTL
            kk = 0
            for ri in range(2):
                for qd in range(4):
                    ps, tps = psA[kk % 2], tA[kk % 2]
                    kk += 1
                    for r4 in range(4):
                        gp = 4 * qd + r4
                        mm(ps[:, r4 * NTL:(r4 + 1) * NTL], tb["BT"][ri][:, gp, :], uT[:, qd, tok0:tok0 + NTL], True, True,
                           [tab, tu], [tps])
                    cp("act", bu[ri][:, 4 * qd:4 * qd + 4, 0:NTL], ps[:, 0:4 * NTL].rearrange("p (g t) -> p g t", t=NTL),
                       [tps], [tbu[ri]])

        def s5_y(it):
            tok0 = it * NTL
            for qd in range(4):
                pm, tm = nextM()
                for r4 in range(4):
                    gp = 4 * qd + r4
                    mm(pm[:, 0:NTL], tb["CT"][0][:, gp, :], xs5[0][:, gp, 0:NTL], r4 == 0, False, [tab, txs[0]], [tm])
                    mm(pm[:, 0:NTL], tb["CT"][1][:, gp, :], xs5[1][:, gp, 0:NTL], False, r4 == 3, [tab, txs[1]], [tm])
                stt(yT[:, qd, tok0:tok0 + NTL], uT[:, qd, tok0:tok0 + NTL], dsk[:, l, qd:qd + 1], pm[:, 0:NTL], ALU.mult, ALU.add,
                    [tu, tconst, tm], [ty])

        if sample:
            it = 0
            tok0 = 0
            s5_bu(0)
            ARt = tb["AR16"][:, :, 0:NS]; AIt = tb["AI16"][:, :, 0:NS]
            for t in range(TL):
                stp = grp["step"]
                cur, nxt = stp % 2, 1 - stp % 2
                grp["step"] += 1
                Xr_c, Xi_c = Xst[l][0][cur][:, :, 0:NS], Xst[l][1][cur][:, :, 0:NS]
                Xr_n, Xi_n = Xst[l][0][nxt][:, :, 0:NS], Xst[l][1][nxt][:, :, 0:NS]
                tXr_c, tXi_c, tXr_n, tXi_n = tX[l][0][cur], tX[l][1][cur], tX[l][0][nxt], tX[l][1][nxt]
                bur = bu[0][:, :, 0:NTL].rearrange("p g (s t) -> p g s t", t=TL)[:, :, :, t]
                bui = bu[1][:, :, 0:NTL].rearrange("p g (s t) -> p g s t", t=TL)[:, :, :, t]
                a0, a1, a2, a3 = [s[:, :, 0:NS] for s in stmp]
                tt("dve", a0, Xr_c, ARt, ALU.mult, [tXr_c, tab], [tst[0]])
                tt("dve", a1, Xi_c, AIt, ALU.mult, [tXi_c, tab], [tst[1]])
                tt("dve", a0, a0, a1, ALU.subtract, [tst[0], tst[1]], [tst[0]])
                tt("dve", Xr_n, a0, bur, ALU.add, [tst[0], tbu[0]], [tXr_n])
                tt("dve", a2, Xr_c, AIt, ALU.mult, [tXr_c, tab], [tst[2]])
                tt("dve", a3, Xi_c, ARt, ALU.mult, [tXi_c, tab], [tst[3]])
                tt("dve", a2, a2, a3, ALU.add, [tst[2], tst[3]], [tst[2]])
                tt("dve", Xi_n, a2, bui, ALU.add, [tst[2], tbu[1]], [tXi_n])
                xr_o = xs5[0][:, :, 0:NTL].rearrange("p g (s t) -> p g s t", t=TL)[:, :, :, t]
                xi_o = xs5[1][:, :, 0:NTL].rearrange("p g (s t) -> p g s t", t=TL)[:, :, :, t]
                cp("act", xr_o, Xr_n, [tXr_n], [txs[0]])
                cp("act", xi_o, Xi_n, [tXi_n], [txs[1]])
            s5_y(0)
        else:
            A_, B_, C_, D_ = S5tmp[:, 0], S5tmp[:, 1], S5tmp[:, 2], S5tmp[:, 3]
            tA_, tB_, tC_, tD_ = ttq
            c1 = tb["C1"][:]; s1 = tb["S1"][:]
            bur = bu[0][:, :, 0:TL]; bui = bu[1][:, :, 0:TL]
            Xcr = Xp[l][0]; Xci = Xp[l][1]; tXcr = tXp[l][0]; tXci = tXp[l][1]
            for it in range(ntile):
                s5_bu(it)
                if it < len(attn_units):
                    attn_units[it][0]()
                tt("dve", A_, c1, bur, ALU.mult, [tab, tbu[0], ttmp], [tA_])
                tt("dve", B_, s1, bui, ALU.mult, [tab, tbu[1], ttmp], [tB_])
                tt("dve", A_, A_, B_, ALU.add, [tB_], [tA_])
                tt("dve", B_, c1, bui, ALU.mult, [tab, tbu[1]], [tB_])
                tt("dve", C_, s1, bur, ALU.mult, [tab, tbu[0]], [tC_])
                tt("dve", B_, B_, C_, ALU.subtract, [tC_], [tB_])
                for gp in range(16):
                    P.op("dve", (lambda gp: lambda e: e.tensor_tensor_scan(
                        A_[:, gp, :], tb["RR"][:, gp:gp + 1].to_broadcast([128, TL]), A_[:, gp, :],
                        Xcr[:, gp:gp + 1], ALU.mult, ALU.add))(gp), [tab, tXcr], [tA_])
                    P.op("dve", (lambda gp: lambda e: e.tensor_tensor_scan(
                        B_[:, gp, :], tb["RR"][:, gp:gp + 1].to_broadcast([128, TL]), B_[:, gp, :],
                        Xci[:, gp:gp + 1], ALU.mult, ALU.add))(gp), [tab, tXci], [tB_])
                if it < len(attn_units):
                    attn_units[it][1]()
                if it > 0:
                    s5_y(it - 1)
                tt("dve", C_, c1, A_, ALU.mult, [tab, tA_], [tC_])
                tt("dve", D_, s1, B_, ALU.mult, [tab, tB_], [tD_])
                tt("dve", C_, C_, D_, ALU.subtract, [tD_], [tC_])
                tt("dve", D_, s1, A_, ALU.mult, [tab, tA_, tC_], [tD_])
                tt("dve", A_, c1, B_, ALU.mult, [tab, tB_], [tA_])
                tt("dve", D_, D_, A_, ALU.add, [tA_], [tD_])
                cp("act", xs5[0][:, :, 0:TL], C_, [tC_], [txs[0]])
                cp("act", xs5[1][:, :, 0:TL], D_, [tD_], [txs[1]])
                cp("act", Xcr[:], C_[:, :, TL - 1], [tC_], [tXcr])
                cp("act", Xci[:], D_[:, :, TL - 1], [tD_], [tXci])
            s5_y(ntile - 1)
        if not sample:
            for u_ in attn_units[ntile:]:
                u_[0](); u_[1]()
            cp("dve", kT[l][:, :, 0:128], kT[l][:, :, NT:NT + 128], [tk[l]], [tk[l]])
            cp("dve", vtok[l][:, 0, :], vtok[l][:, NTI, :], [tv[l]], [tv[l]])
        fin = grp["step"] % 2 if sample else 0
        if sample:
            P.dma("pool", d_so[l], o_srs[l], Xst[l][0][fin][:], reads=[tX[l][0][fin]])
            P.dma("pool", d_so[l], o_sis[l], Xst[l][1][fin][:], reads=[tX[l][1][fin]])
        elif bidx == n_batches - 1:
            P.dma("pool", d_out, o_srp[l], Xp[l][0][:], reads=[tXp[l][0]])
            P.dma("pool", d_out, o_sip[l], Xp[l][1][:], reads=[tXp[l][1]])
        Y = yT[:, :, 0:ntok]; G1 = g1[:, :, 0:ntok]; G2 = g2[:, :, 0:ntok]
        tt("dve", G1, Y, Y, ALU.mult, [ty], [tg1])
        ts("dve", G1, G1, 0.044715, 1.0, ALU.mult, ALU.add, [tg1], [tg1])
        tt("dve", G1, G1, Y, ALU.mult, [tg1, ty], [tg1])
        act(G1, G1, AF.Tanh, [tg1], [tg1], scale=0.7978845608028654)
        ts("dve", G2, Y, 0.5, None, ALU.mult, ALU.bypass, [ty], [tg2])
        stt(G1, G1, 1.0, G2, ALU.add, ALU.mult, [tg1, tg2], [tg1])
        cp("act", sbf[:, :, 0:ntok], G1, [tg1], [tsb])
        ts("dve", G2, G1, 0.5, None, ALU.mult, ALU.bypass, [tg1], [tg2])

        def glu_load(b, slot):
            return [(slot[:, 0:4, :], wb_glu[l, :, :, b * 128:(b + 1) * 128])]

        def glu_mm(b, slot):
            return [(slot[:, k, :], sbf[:, k, 0:ntok], (lambda ps: ps[:, 0:ntok])) for k in range(4)]

        def glu_cb(b, ps, tps):
            act(g1[:, b, 0:ntok], ps[:, 0:ntok], AF.Tanh, [tps, tconst], [tg1], bias=hbglu[:, l, b:b + 1], scale=0.5)
            stt(yT[:, b, 0:ntok], g1[:, b, 0:ntok], 1.0, g2[:, b, 0:ntok], ALU.add, ALU.mult, [tg1, tg2], [ty])
        linear(4, glu_load, glu_mm, glu_cb, [tsb], wtrk=[twbs[("glu", l)]])
        rmsnorm(attnT, [tat], 8, 64, lambda k: gattn[:, l, k:k + 1], ntok, lambda k: attnB[:, k, 0:ntok], [tatb], 512)
        rmsnorm(yT, [ty], 4, 128, lambda k: gssm[:, l, k:k + 1], ntok, lambda k: ssmB[:, k, 0:ntok], [tsmb], 512)

        def wo_load(b, slot):
            blk = b // 2
            if b % 2 == 0:
                return [(slot[0:64, 0:8, :], wb_outa[l, :, :, blk * 128:(blk + 1) * 128])]
            return [(slot[:, 0:4, :], wb_outb[l, :, :, blk * 128:(blk + 1) * 128])]

        def wo_mm(b, slot):
            o = (lambda ps: ps[:, 0:ntok])
            if b % 2 == 0:
                return [(slot[0:64, k, :], attnB[:, k, 0:ntok], o) for k in range(8)]
            return [(slot[:, k, :], ssmB[:, k, 0:ntok], o) for k in range(4)]

        def wo_cb(b, ps, tps):
            tt("dve", xT[:, b, 0:ntok], ps[:, 0:ntok], xT[:, b, 0:ntok], ALU.add, [tps, tx[b]], [tx[b]])
        linear(16, wo_load, wo_mm, wo_cb, [tatb, tsmb], group=2, wtrk=[twbs[("outa", l)], twbs[("outb", l)]])
        rmsnorm(xT, tx, 8, 128, lambda k: gffn[:, l, k:k + 1], ntok, lambda k: hT[:, k, 0:ntok], th_, D)
        car = scarry[l] if sample else carry[l]
        CTL = ntok // NS
        W2 = CTL + 2

        def up_load(b, slot):
            m = (b // 2) + 21 * (b % 2)
            return [(slot[:, 0:8, :], wb_up[l, :, :, m * 128:(m + 1) * 128])]

        def up_mm(b, slot):
            return [(slot[:, k, :], hT[:, k, 0:ntok], (lambda ps: ps[:, 0:ntok])) for k in range(8)]

        def up_cb(b, ps, tps):
            w = b % 2
            m = (b // 2) + 21 * w
            pp = (b // 2) % 2
            upb = upb2[pp]; cs_ = cs2[pp]; tup = tup2[pp]; tcs = tcs2[pp]
            ub = upb[:, w, 0:NS * W2].rearrange("p (s t) -> p s t", t=W2)
            cv_ = car[:, m, 0:NS * 2].rearrange("p (s t) -> p s t", t=2)
            cp("act", ub[:, :, 0:2], cv_, [tcar[l][m]], [tup[w]])
            cp("act", ub[:, :, 2:W2], ps[:, 0:ntok].rearrange("p (s t) -> p s t", t=CTL), [tps], [tup[w]])
            cp("dve", cv_, ub[:, :, CTL:CTL + 2], [tup[w]], [tcar[l][m]])
            cv3 = cs_[:, w, 0:ntok].rearrange("p (s t) -> p s t", t=CTL)
            act(cv3, ub[:, :, 2:W2], AF.Identity, [tup[w], tconst], [tcs[w]], bias=cbs[:, l, m:m + 1], scale=cws[:, l, m, 2:3])
            stt(cv3, ub[:, :, 1:1 + CTL], cws[:, l, m, 1:2], cv3, ALU.mult, ALU.add, [tup[w], tconst, tcs[w]], [tcs[w]])
            stt(cv3, ub[:, :, 0:CTL], cws[:, l, m, 0:1], cv3, ALU.mult, ALU.add, [tup[w], tconst, tcs[w]], [tcs[w]])
            if w == 1:
                mh = b // 2
                A_ = cs_[:, 0, 0:ntok]; G_ = cs_[:, 1, 0:ntok]
                act(g1[:, pp, 0:ntok], G_, AF.Tanh, [tcs[1], tg1], [tgs[pp]], scale=0.5)
                stt(g1[:, pp, 0:ntok], g1[:, pp, 0:ntok], 1.0, G_, ALU.add, ALU.mult, [tgs[pp], tcs[1]], [tgs[pp]])
                stt(hid[:, mh, 0:ntok], g1[:, pp, 0:ntok], 0.5, A_, ALU.mult, ALU.mult, [tgs[pp], tcs[0]], [thid])
        linear(42, up_load, up_mm, up_cb, th_, wtrk=[twbs[("up", l)]])
        if sample:
            P.dma("pool", d_out, o_cs[l], car[:, :, 0:32].rearrange("p m (s t) -> p m s t", t=2), reads=tcar[l])
        elif bidx == n_batches - 1:
            P.dma("pool", d_out, o_cp[l], car[:, :, 0:2], reads=tcar[l])

        def dn_load(b, slot):
            k0, nk = 7 * (b % 3), 7
            return [(slot[:, 0:nk, :], wb_down[l, :, k0:k0 + nk, (b // 3) * 128:(b // 3 + 1) * 128])]

        def dn_mm(b, slot):
            k0, nk = 7 * (b % 3), 7
            return [(slot[:, k, :], hid[:, k0 + k, 0:ntok], (lambda ps: ps[:, 0:ntok])) for k in range(nk)]

        def dn_cb(b, ps, tps):
            tt("dve", xT[:, b, 0:ntok], ps[:, 0:ntok], xT[:, b, 0:ntok], ALU.add, [tps, tx[b]], [tx[b]])
        linear(24, dn_load, dn_mm, dn_cb, [thid], group=3, wtrk=[twbs[("down", l)]])

    d_x = P.new_dsem("d_x"); d_y = P.new_dsem("d_y")
    for l in range(L):
        mset("dve", carry[l][:], 0.0, tcar[l])
        for ri in range(2):
            mset("dve", Xp[l][ri][:], 0.0, [tXp[l][ri]])
    pstep = [0, 0]
    for b in range(n_batches):
        P.dma("pool", d_x, xT[:], xpT[:, :, b * NT:(b + 1) * NT], writes=tx)
        for l in range(L):
            grp = dict(sample=False, ntok=NT, NS=1, TL=TLP, b=b, step=pstep[l])
            run_layer(l, grp)
            pstep[l] = grp["step"]
        rmsnorm(xT, tx, 8, 128, lambda k: gfin[:, k:k + 1], NT, lambda k: yout[:, k, :], [tyo, tg1, tg2] + tgs, D)
        P.dma("pool", d_y, o_ypT[:, :, b * NT:(b + 1) * NT], yout[:], reads=[tyo, tg1, tg2] + tgs)
    if do_sample:
        P.dma("pool", d_x, xT[:, :, 0:64], xsT, writes=tx)
        for l in range(L):
            cur = pstep[l] % 2
            d_s1 = P.new_dsem("d_s1_%d" % l); d_s2 = P.new_dsem("d_s2_%d" % l); d_s3 = P.new_dsem("d_s3_%d" % l)
            P.dma("pool", d_s1, Xst[l][0][cur][:], sre[l], writes=[tX[l][0][cur]])
            P.dma("pool", d_s2, Xst[l][1][cur][:], sim[l], writes=[tX[l][1][cur]])
            P.dma("pool", d_s3, scarry[l][:, :, 0:32].rearrange("p m (s t) -> p m s t", t=2), sconv[l], writes=tcar[l])
            grp = dict(sample=True, ntok=64, NS=16, TL=4, step=pstep[l])
            run_layer(l, grp)
        rmsnorm(xT, tx, 8, 128, lambda k: gfin[:, k:k + 1], 64, lambda k: yout[:, k, 0:64], [tyo, tg1, tg2] + tgs, D)
        P.dma("pool", d_y, o_ysT, yout[:, :, 0:64], reads=[tyo, tg1, tg2] + tgs)
    P.finish()
    return nc


def _consts():
    ident = np.eye(128, dtype=np.float32)
    ones = np.ones((128, 128), np.float32)
    slopes = np.array([2.0 ** -(h + 1) for h in range(8)], np.float32).reshape(2, 4)
    k = np.arange(128)[:, None]; q = np.arange(128)[None, :]
    biasP = np.full((128, 2, 2, 4, 128), NEG, np.float32)
    for kv in range(2):
        for g in range(4):
            sl = slopes[kv, g]
            dcur = q - k
            biasP[:, 1, kv, g, :] = np.where(dcur >= 0, -sl * dcur, NEG)
            dprev = q + 128 - k
            biasP[:, 0, kv, g, :] = np.where(dprev < 128, -sl * dprev, NEG)
    biasP = biasP.reshape(128, 2, 2, 512)
    j = np.arange(128)[:, None]; tq = np.arange(4)[None, :]
    bSc = np.full((128, 2, 16, 4, 4), NEG, np.float32)
    bSn = np.full((4, 2, 16, 4, 4), NEG, np.float32)
    tk_ = np.arange(4)[:, None]
    for kv in range(2):
        for g in range(4):
            sl = slopes[kv, g]
            dist = tq + 128 - j
            bSc[:, kv, :, g, :] = np.where(dist < 128, -sl * dist, NEG)[:, None, :]
            dn = tq - tk_
            bSn[:, kv, :, g, :] = np.where(dn >= 0, -sl * dn, NEG)[:, None, :]
    return dict(c_ident=ident, c_ones=ones, c_biasP=biasP, c_biasSc=bSc.reshape(128, 2, 256), c_biasSn=bSn.reshape(4, 2, 256))


def _ktile(w, kt):
    Lw, K, N = w.shape
    return np.ascontiguousarray(w.reshape(Lw, kt, K // kt, N).transpose(0, 2, 1, 3))


def _vec(v, kt):
    Lw, F = v.shape
    return np.ascontiguousarray(v.reshape(Lw, kt, F // kt).transpose(2, 0, 1))


def _l0(a):
    sh = a.shape
    a = a.reshape(sh[0], 16, 2, 64, *sh[3:])
    perm = (2, 3, 0, 1) + tuple(range(4, a.ndim))
    a = a.transpose(perm)
    return np.ascontiguousarray(a.reshape(128, sh[0], 16, *sh[3:]))


_PROG = {}


def kernel(x_prompt, x_sample, cache_k_win, cache_v_win, state_ssm_re, state_ssm_im, state_conv,
           norm_mix, w_in, sinks, lam_re, lam_im, log_step, b_re, b_im, c_re, c_im, d_skip,
           w_glu, b_glu, g_attn, g_ssm, w_out, norm_ffn, w_up, conv_w, conv_b, w_down, norm_final):
    f = lambda a: np.ascontiguousarray(np.asarray(a, dtype=np.float32))
    (x_prompt, x_sample, cache_k_win, cache_v_win, state_ssm_re, state_ssm_im, state_conv, norm_mix, w_in, sinks,
     lam_re, lam_im, log_step, b_re, b_im, c_re, c_im, d_skip, w_glu, b_glu, g_attn, g_ssm, w_out, norm_ffn, w_up,
     conv_w, conv_b, w_down, norm_final) = [f(a) for a in (
        x_prompt, x_sample, cache_k_win, cache_v_win, state_ssm_re, state_ssm_im, state_conv, norm_mix, w_in, sinks,
        lam_re, lam_im, log_step, b_re, b_im, c_re, c_im, d_skip, w_glu, b_glu, g_attn, g_ssm, w_out, norm_ffn, w_up,
        conv_w, conv_b, w_down, norm_final)]
    if "nc" not in _PROG:
        _PROG["nc"] = build_program()
    nc = _PROG["nc"]
    shared = dict(_consts())
    shared.update(
        w_in=_ktile(w_in, 8), w_outa=_ktile(w_out[:, :512], 8), w_outb=_ktile(w_out[:, 512:], 4),
        w_glu=_ktile(w_glu, 4), w_up=_ktile(w_up, 8), w_down=_ktile(w_down, 21),
        g_mix=_vec(norm_mix, 8), g_ffn=_vec(norm_ffn, 8), g_fin=_vec(norm_final[None], 8)[:, 0],
        g_attn=_vec(g_attn, 8), g_ssm=_vec(g_ssm, 4), b_glu=_vec(b_glu, 4), d_skip=_vec(d_skip, 4),
        cw=np.ascontiguousarray(conv_w.reshape(L, 3, MT_UP, 128).transpose(3, 0, 2, 1)),
        cb=_vec(conv_b, MT_UP),
        sinkP=np.ascontiguousarray(np.broadcast_to(sinks[None], (64, L, 8))),
        lamre=_l0(lam_re), lamim=_l0(lam_im),
        lstep=_l0(np.broadcast_to(log_step[:, :, None], (L, 32, 64))),
        bre=_l0(b_re), bim=_l0(b_im),
        cre=_l0(c_re.transpose(0, 1, 3, 2)), cim=_l0(c_im.transpose(0, 1, 3, 2)),
    )
    shared = {k: np.ascontiguousarray(v, dtype=np.float32) for k, v in shared.items()}
    in_maps = []
    for c in range(8):
        s0 = 16 * c
        m = dict(shared)
        m["xpT"] = np.ascontiguousarray(x_prompt[c].T.reshape(8, 128, SEQ).transpose(1, 0, 2))
        m["xsT"] = np.ascontiguousarray(x_sample[s0:s0 + 16].reshape(64, D).T.reshape(8, 128, 64).transpose(1, 0, 2))
        ckc = cache_k_win[:, s0:s0 + 16].reshape(L, 16, 128, 128); cvc = cache_v_win[:, s0:s0 + 16].reshape(L, 16, 128, 128)
        m["ck"] = np.ascontiguousarray(ckc.transpose(0, 2, 1, 3)); m["cv"] = np.ascontiguousarray(cvc.transpose(0, 2, 1, 3))
        m["ckr"] = np.ascontiguousarray(ckc); m["cvr"] = np.ascontiguousarray(cvc)
        def st(a):
            a = a[:, s0:s0 + 16].reshape(L, 16, 16, 2, 64).transpose(0, 3, 4, 2, 1)
            return np.ascontiguousarray(a.reshape(L, 128, 16, 16))
        m["sre"] = st(state_ssm_re); m["sim"] = st(state_ssm_im)
        m["sconv"] = np.ascontiguousarray(state_conv[:, s0:s0 + 16].reshape(L, 16, 2, MT_UP, 128).transpose(0, 4, 3, 1, 2))
        in_maps.append(m)
    res = run_bass_kernel_spmd(nc, in_maps, core_ids=list(range(8)))
    R = res.results
    B = 8
    y_p = np.stack([R[c]["o_ypT"].transpose(1, 0, 2).reshape(D, SEQ).T for c in range(B)])
    y_s = np.concatenate([R[c]["o_ysT"].transpose(1, 0, 2).reshape(D, 64).T.reshape(16, 4, D) for c in range(B)])
    k_p = np.stack([R[c]["o_kp"] for c in range(B)], 1).reshape(L, B, 128, 2, 64)
    v_p = np.stack([R[c]["o_vp"] for c in range(B)], 1).reshape(L, B, 128, 2, 64)

    def unst_p(key):
        a = np.stack([R[c][key] for c in range(B)], 1)
        a = a.reshape(L, B, 2, 64, 16).transpose(0, 1, 4, 2, 3)
        return np.ascontiguousarray(a.reshape(L, B, 32, 64))
    sr_p = unst_p("o_srp"); si_p = unst_p("o_sip")
    c_p = np.stack([R[c]["o_cp"] for c in range(B)], 1)
    c_p = np.ascontiguousarray(c_p.transpose(0, 1, 4, 3, 2).reshape(L, B, 2, 2 * DFF))
    k_s = np.concatenate([R[c]["o_ks"] for c in range(B)], 1).reshape(L, 128, 128, 2, 64)
    v_s = np.concatenate([R[c]["o_vs"] for c in range(B)], 1).reshape(L, 128, 128, 2, 64)

    def unst_s(key):
        a = np.stack([R[c][key] for c in range(B)], 1)
        a = a.reshape(L, B, 2, 64, 16, 16).transpose(0, 1, 5, 4, 2, 3)
        return np.ascontiguousarray(a.reshape(L, B * 16, 32, 64))
    sr_s = unst_s("o_srs"); si_s = unst_s("o_sis")
    c_s = np.stack([R[c]["o_cs"] for c in range(B)], 1)
    c_s = np.ascontiguousarray(c_s.transpose(0, 1, 4, 5, 3, 2).reshape(L, B * 16, 2, 2 * DFF))
    outs = (y_p, y_s, k_p, v_p, sr_p, si_p, c_p, k_s, v_s, sr_s, si_s, c_s)
    return tuple(np.ascontiguousarray(o, dtype=np.float32) for o in outs)
```

```python
import numpy as np
import concourse.bass as bass
import concourse.mybir as mybir

F32 = mybir.dt.float32
BF16 = mybir.dt.bfloat16
I32 = mybir.dt.int32
ALU = mybir.AluOpType
AF = mybir.ActivationFunctionType
AX = mybir.AxisListType

ENGS = ("pe", "act", "dve", "pool", "sp")


class Trk:
    __slots__ = ("name", "w", "rs", "excl")

    def __init__(self, name="", excl=False):
        self.name = name
        self.excl = excl
        self.w = None
        self.rs = []


class Op:
    __slots__ = ("eng", "idx", "fn", "deps", "flag", "dma", "val")

    def __init__(self, eng, idx, fn, deps, dma):
        self.eng, self.idx, self.fn, self.deps, self.dma = eng, idx, fn, deps, dma
        self.flag = False
        self.val = None


class Prog:
    def __init__(self, nc):
        self.nc = nc
        self.ops = {e: [] for e in ENGS}
        self.dsems = []
        self._ctx = []

    def enter(self, cm):
        v = cm.__enter__()
        self._ctx.append(cm)
        return v

    def sbuf(self, name, shape, dt):
        return self.enter(self.nc.sbuf_tensor(name, list(shape), dt))

    def psum(self, name, shape, dt=F32):
        return self.enter(self.nc.psum_tensor(name, list(shape), dt))

    def new_dsem(self, name):
        s = self.enter(self.nc.semaphore(name))
        d = {"sem": s, "cnt": 0}
        self.dsems.append(d)
        return d

    def _deps(self, eng, reads, writes):
        deps = []
        for t in reads:
            if t.w is not None:
                deps.append(t.w)
        for t in writes:
            if t.w is not None:
                deps.append(t.w)
            deps.extend(t.rs)
        return deps

    def op(self, eng, fn, reads=(), writes=()):
        ex = [t for t in reads if t.excl]
        if ex:
            reads = [t for t in reads if not t.excl]
            writes = list(writes) + ex
        deps = self._deps(eng, reads, writes)
        o = Op(eng, len(self.ops[eng]), fn, deps, None)
        self.ops[eng].append(o)
        for t in reads:
            t.rs.append(o)
        for t in writes:
            t.w = o
            t.rs = []
        return o

    def dma(self, q, dsem, out, in_, reads=(), writes=(), **kw):
        deps = self._deps(q, reads, writes)

        def fn(e):
            return e.dma_start(out=out, in_=in_, **kw)
        o = Op(q, len(self.ops[q]), fn, deps, dsem)
        dsem["cnt"] += 16
        o.val = dsem["cnt"]
        o.flag = True
        self.ops[q].append(o)
        for t in reads:
            t.rs.append(o)
        for t in writes:
            t.w = o
            t.rs = []
        return o

    def barrier(self, eng, dsem, fn, trks):
        dep = Op("sp", -1, None, [], dsem)
        dep.val = dsem["cnt"]
        dep.flag = True
        deps = [dep] + [t.w for t in trks if t.w is not None]
        o = Op(eng, len(self.ops[eng]), fn, deps, None)
        self.ops[eng].append(o)
        for t in trks:
            t.w = o
            t.rs = []
        return o

    def finish(self, final_waits=()):
        nc = self.nc
        for e in ENGS:
            for o in self.ops[e]:
                for d in o.deps:
                    if d.dma is None:
                        if d.eng == "pe" and e == "pe":
                            continue
                        d.flag = True
        esem = {}
        for e in ENGS:
            esem[e] = self.enter(nc.semaphore("esem_" + e))
            c = 0
            for o in self.ops[e]:
                if o.dma is None:
                    if o.flag:
                        c += 1
                        o.val = c
                    else:
                        o.val = None
        nxt = {}
        for e in ENGS:
            arr = [None] * len(self.ops[e])
            cur = None
            for i in range(len(self.ops[e]) - 1, -1, -1):
                o = self.ops[e][i]
                if o.dma is None and o.flag:
                    cur = o.val
                arr[i] = cur
            nxt[e] = arr
        self.nwaits = 0
        prog = self

        def emit(ename):
            def body(eh):
                seen = {}
                for o in prog.ops[ename]:
                    need = {}
                    for d in o.deps:
                        if d.dma is not None:
                            key = ("d", id(d.dma))
                            sem, val = d.dma["sem"], d.val
                        else:
                            if d.eng == "pe" and ename == "pe":
                                continue
                            key = ("e", d.eng)
                            sem, val = esem[d.eng], d.val
                            assert val is not None
                        if seen.get(key, 0) >= val:
                            continue
                        if key not in need or need[key][1] < val:
                            need[key] = (sem, val)
                    for key, (sem, val) in need.items():
                        eh.wait_ge(sem, val)
                        seen[key] = val
                        prog.nwaits += 1
                    if o.fn is None:
                        continue
                    ins = o.fn(eh)
                    if o.dma is not None:
                        ins.then_inc(o.dma["sem"], 16)
                    elif o.flag:
                        ins.then_inc(esem[ename], 1)
                if ename == "sp":
                    for d in prog.dsems:
                        if d["cnt"] > 0:
                            eh.wait_ge(d["sem"], d["cnt"])
            return body

        with nc.Block() as block:
            block.tensor(emit("pe"))
            block.scalar(emit("act"))
            block.vector(emit("dve"))
            block.gpsimd(emit("pool"))
            block.sync(emit("sp"))
        for cm in reversed(self._ctx):
            cm.__exit__(None, None, None)
        self._ctx = []

import math
from concourse.bass_utils import run_bass_kernel_spmd

D = 1024; L = 2; SEQ = 4096; NB = 16; NT = 256; TLP = 64; NTI = 2; DFF = 2688; MT_UP = 42
NEG = -30000.0
EPS = 1e-5
PI = math.pi


def build_program(n_batches=NB, do_sample=True):
    nc = bass.Bass("TRN2", target_bir_lowering=False)
    P = Prog(nc)

    def din(name, shape):
        return nc.dram_tensor(name, list(shape), F32, kind="ExternalInput").ap()

    def dout(name, shape):
        return nc.dram_tensor(name, list(shape), F32, kind="ExternalOutput").ap()

    xpT = din("xpT", [128, 8, SEQ]); xsT = din("xsT", [128, 8, 64])
    ck = din("ck", [L, 128, 16, 128]); cv = din("cv", [L, 128, 16, 128])
    ckr = din("ckr", [L, 16, 128, 128]); cvr = din("cvr", [L, 16, 128, 128])
    sre = din("sre", [L, 128, 16, 16]); sim = din("sim", [L, 128, 16, 16])
    sconv = din("sconv", [L, 128, MT_UP, 16, 2])
    w_in = din("w_in", [L, 128, 8, 1280])
    w_outa = din("w_outa", [L, 64, 8, 1024]); w_outb = din("w_outb", [L, 128, 4, 1024])
    w_glu = din("w_glu", [L, 128, 4, 512])
    w_up = din("w_up", [L, 128, 8, 2 * DFF]); w_down = din("w_down", [L, 128, 21, 1024])
    g_mix = din("g_mix", [128, L, 8]); g_ffn = din("g_ffn", [128, L, 8]); g_fin = din("g_fin", [128, 8])
    g_attn = din("g_attn", [64, L, 8]); g_ssm = din("g_ssm", [128, L, 4])
    b_glu = din("b_glu", [128, L, 4]); d_skip = din("d_skip", [128, L, 4])
    cw = din("cw", [128, L, MT_UP, 3]); cb = din("cb", [128, L, MT_UP])
    sinkP = din("sinkP", [64, L, 8])
    lamre = din("lamre", [128, L, 16]); lamim = din("lamim", [128, L, 16]); lstep = din("lstep", [128, L, 16])
    bre = din("bre", [128, L, 16, 16]); bim = din("bim", [128, L, 16, 16])
    cre = din("cre", [128, L, 16, 16]); cim = din("cim", [128, L, 16, 16])
    c_ident = din("c_ident", [128, 128]); c_ones = din("c_ones", [128, 128])
    c_biasP = din("c_biasP", [128, 2, 2, 512]); c_biasSc = din("c_biasSc", [128, 2, 256]); c_biasSn = din("c_biasSn", [4, 2, 256])

    o_ypT = dout("o_ypT", [128, 8, SEQ]); o_ysT = dout("o_ysT", [128, 8, 64])
    o_kp = dout("o_kp", [L, 128, 128]); o_vp = dout("o_vp", [L, 128, 128])
    o_srp = dout("o_srp", [L, 128, 16]); o_sip = dout("o_sip", [L, 128, 16])
    o_cp = dout("o_cp", [L, 128, MT_UP, 2])
    o_ks = dout("o_ks", [L, 16, 128, 128]); o_vs = dout("o_vs", [L, 16, 128, 128])
    o_srs = dout("o_srs", [L, 128, 16, 16]); o_sis = dout("o_sis", [L, 128, 16, 16])
    o_cs = dout("o_cs", [L, 128, MT_UP, 16, 2])

    def dscr(name, shape):
        return nc.dram_tensor(name, list(shape), BF16, kind="Internal").ap()
    wb_in = dscr("wb_in", [L, 128, 8, 1280]); wb_outa = dscr("wb_outa", [L, 64, 8, 1024]); wb_outb = dscr("wb_outb", [L, 128, 4, 1024])
    wb_glu = dscr("wb_glu", [L, 128, 4, 512]); wb_up = dscr("wb_up", [L, 128, 8, 2 * DFF]); wb_down = dscr("wb_down", [L, 128, 21, 1024])

    def T(n=1):
        return [Trk() for _ in range(n)] if n > 1 else Trk()

    def act(out, in_, func, reads, writes, bias=None, scale=None):
        kw = {}
        if bias is not None:
            kw["bias"] = bias
        if scale is not None:
            kw["scale"] = scale
        return P.op("act", lambda e: e.activation(out, in_, func, **kw), reads, writes)

    def tt(eng, out, a, b, op, reads, writes):
        return P.op(eng, lambda e: e.tensor_tensor(out, a, b, op), reads, writes)

    def ts(eng, out, a, s1, s2, op0, op1, reads, writes):
        return P.op(eng, lambda e: e.tensor_scalar(out, a, s1, s2, op0, op1), reads, writes)

    def stt(out, a, s, b, op0, op1, reads, writes):
        return P.op("dve", lambda e: e.scalar_tensor_tensor(out, a, s, b, op0, op1), reads, writes)

    def cp(eng, out, in_, reads, writes):
        if eng == "act":
            return P.op("act", lambda e: e.activation(out, in_, AF.Copy), reads, writes)
        return P.op(eng, lambda e: e.tensor_copy(out, in_), reads, writes)

    def mm(out, lhsT, rhs, start, stop, reads, writes):
        return P.op("pe", lambda e: e.matmul(out, lhsT, rhs, start=start, stop=stop), reads, writes)

    def mset(eng, ap, val, writes):
        return P.op(eng, lambda e: e.memset(ap, val), (), writes)

    def recip(out, in_, reads, writes):
        return P.op("dve", lambda e: e.reciprocal(out, in_), reads, writes)

    d_pre = P.new_dsem("d_pre"); d_kc = P.new_dsem("d_kc"); d_vc = P.new_dsem("d_vc")
    d_kvn = [P.new_dsem("d_kvn0"), P.new_dsem("d_kvn1")]; d_so = [P.new_dsem("d_so0"), P.new_dsem("d_so1")]
    d_out = P.new_dsem("d_out")

    d_pre2 = P.new_dsem("d_pre2")

    def load(dst, src, trk, q="sp"):
        return P.dma(q, d_pre if q == "sp" else d_pre2, dst, src, writes=[trk])

    ident_f = P.sbuf("ident_f", [128, 128], F32); ident_b = P.sbuf("ident_b", [128, 128], BF16)
    ones_b = P.sbuf("ones_b", [128, 128], BF16)
    biasP = P.sbuf("biasP", [128, 2, 2, 512], BF16)
    biasSc = P.sbuf("biasSc", [128, 2, 256], BF16); biasSn = P.sbuf("biasSn", [4, 2, 256], BF16)
    tconst = T()
    load(ident_f[:], c_ident, tconst)
    load(ident_b[:], c_ident, tconst, "pool"); load(ones_b[:], c_ones, tconst, "pool")
    load(biasP[:], c_biasP, tconst, "pool"); load(biasSc[:], c_biasSc, tconst, "pool"); load(biasSn[:], c_biasSn, tconst, "pool")
    gmix = P.sbuf("gmix", [128, L, 8], F32); gffn = P.sbuf("gffn", [128, L, 8], F32); gfin = P.sbuf("gfin", [128, 8], F32)
    gattn = P.sbuf("gattn", [64, L, 8], F32); gssm = P.sbuf("gssm", [128, L, 4], F32)
    bglu = P.sbuf("bglu", [128, L, 4], F32); hbglu = P.sbuf("hbglu", [128, L, 4], F32); dsk = P.sbuf("dsk", [128, L, 4], F32)
    cws = P.sbuf("cws", [128, L, MT_UP, 3], F32); cbs = P.sbuf("cbs", [128, L, MT_UP], F32)
    esP = P.sbuf("esP", [64, L, 8], F32)
    for dst, src in ((gmix, g_mix), (gffn, g_ffn), (gfin, g_fin), (gattn, g_attn), (gssm, g_ssm), (bglu, b_glu),
                     (dsk, d_skip), (cws, cw), (cbs, cb), (esP, sinkP)):
        load(dst[:], src, tconst)
    epsT = P.sbuf("epsT", [128, 1], F32); hpiT = P.sbuf("hpiT", [128, 1], F32)
    P.barrier("dve", d_pre, lambda e: e.memset(epsT[:], EPS), [tconst])
    P.barrier("dve", d_pre2, lambda e: e.memset(hpiT[:], PI / 2), [tconst])
    mset("dve", epsT[:], EPS, [tconst]); mset("dve", hpiT[:], PI / 2, [tconst])
    act(esP[:], esP[:], AF.Exp, [tconst], [tconst])
    ts("dve", hbglu[:], bglu[:], 0.5, None, ALU.mult, ALU.bypass, [tconst], [tconst])

    twbs = {}
    for l_ in range(L):
        for nm_, dst_, src_, kt_ in (("in", wb_in, w_in, 8), ("outa", wb_outa, w_outa, 8), ("outb", wb_outb, w_outb, 4),
                                     ("glu", wb_glu, w_glu, 4), ("up", wb_up, w_up, 8), ("down", wb_down, w_down, 21)):
            d_cvt = P.new_dsem("d_cvt_%s%d" % (nm_, l_)); t_ = Trk()
            twbs[(nm_, l_)] = t_
            for k_ in range(kt_):
                P.dma("pool", d_cvt, dst_[l_, :, k_, :], src_[l_, :, k_, :], writes=[t_])


    XT = lambda n=1: [Trk(excl=True) for _ in range(n)] if n > 1 else Trk(excl=True)
    psA = [P.psum("psA%d" % i, [128, 512]) for i in range(2)]; tA = XT(2)
    psS = [P.psum("psS%d" % i, [128, 512]) for i in range(2)]; tS = XT(2)
    psO = P.psum("psO", [128, 512]); tO = XT()
    psD = P.psum("psD", [128, 512]); tD = XT()
    psM = [P.psum("psM%d" % i, [128, 512]) for i in range(2)]; tM = XT(2)
    cntM = [0]

    def nextM():
        i = cntM[0] % 2
        cntM[0] += 1
        return psM[i], tM[i]

    s5 = []
    scr = [P.sbuf("s5scr%d" % i, [128, 16], F32) for i in range(8)]
    G12 = P.sbuf("G12", [128, 8, NT], F32)
    bmf = G12[:].rearrange("p a (b c) -> p (a b) c", c=128)
    S5tmp = P.sbuf("S5tmp", [128, 4, 16, TLP], F32); ttmp = T(); ttq = T(4)
    braw = S5tmp[:, 0].rearrange("p g t -> p (g t)")[:, 0:512].rearrange("p (a g h) -> p a g h", a=2, g=16)
    bbar = S5tmp[:, 1].rearrange("p g t -> p (g t)")[:, 0:512].rearrange("p (a g h) -> p a g h", a=2, g=16)
    craw = S5tmp[:, 2].rearrange("p g t -> p (g t)")[:, 0:512].rearrange("p (a g h) -> p a g h", a=2, g=16)
    tab = T()
    for l in range(L):
        lr = P.sbuf("lr%d" % l, [128, 16], F32); li = P.sbuf("li%d" % l, [128, 16], F32); ls = P.sbuf("ls%d" % l, [128, 16], F32)
        d_tab = P.new_dsem("d_tab%d" % l)
        for dst_, src_ in ((lr[:], lamre[:, l, :]), (li[:], lamim[:, l, :]), (ls[:], lstep[:, l, :]), (braw[:, 0], bre[:, l]),
                           (braw[:, 1], bim[:, l]), (craw[:, 0], cre[:, l]), (craw[:, 1], cim[:, l])):
            P.dma("sp", d_tab, dst_, src_, writes=[tab])
        P.barrier("dve", d_tab, (lambda l_: lambda e: e.memset(scr[0][:], 0.0))(l), [tab])
        AR = P.sbuf("AR%d" % l, [128, 16], F32); AI = P.sbuf("AI%d" % l, [128, 16], F32)
        AR16 = P.sbuf("AR16_%d" % l, [128, 16, 16], F32); AI16 = P.sbuf("AI16_%d" % l, [128, 16, 16], F32)
        BT = [P.sbuf("BT%d_%d" % (l, ri), [128, 16, 128], BF16) for ri in range(2)]
        CT = [P.sbuf("CT%d_%d" % (l, ri), [128, 16, 128], BF16) for ri in range(2)]
        dt_, zr, th, rr, cc, ss, t0, t1 = [s[:] for s in scr]
        R, W = [tab], [tab]
        act(dt_, ls[:], AF.Exp, R, W)
        tt("dve", zr, lr[:], dt_, ALU.mult, R, W)
        tt("dve", th, li[:], dt_, ALU.mult, R, W)
        act(rr, zr, AF.Exp, R, W)
        act(cc, th, AF.Sin, R, W, bias=hpiT[:], scale=1.0 / 32)
        act(ss, th, AF.Sin, R, W, scale=1.0 / 32)
        for _ in range(5):
            tt("dve", t0, cc, cc, ALU.mult, R, W)
            tt("dve", t1, ss, ss, ALU.mult, R, W)
            tt("dve", ss, ss, cc, ALU.mult, R, W)
            ts("dve", ss, ss, 2.0, None, ALU.mult, ALU.bypass, R, W)
            tt("dve", cc, t0, t1, ALU.subtract, R, W)
        tt("dve", AR[:], rr, cc, ALU.mult, R, W)
        tt("dve", AI[:], rr, ss, ALU.mult, R, W)
        for s_ in range(16):
            cp("dve", AR16[:, :, s_], AR[:], R, W); cp("dve", AI16[:, :, s_], AI[:], R, W)
        RR = P.sbuf("RR%d" % l, [128, 16], F32)
        cp("dve", RR[:], rr, R, W)
        C1 = P.sbuf("C1_%d" % l, [128, 16, TLP], F32); S1 = P.sbuf("S1_%d" % l, [128, 16, TLP], F32)
        cp("dve", C1[:, :, 0], cc, R, W); cp("dve", S1[:, :, 0], ss, R, W)
        kk_ = 1
        while kk_ < TLP:
            cp("dve", t0, C1[:, :, kk_ - 1], R, W); cp("dve", t1, S1[:, :, kk_ - 1], R, W)
            ts("dve", zr, t1, -1.0, None, ALU.mult, ALU.bypass, R, W)
            for gp in range(16):
                ts("dve", C1[:, gp, kk_:2 * kk_], C1[:, gp, 0:kk_], t0[:, gp:gp + 1], None, ALU.mult, ALU.bypass, R, W)
                stt(C1[:, gp, kk_:2 * kk_], S1[:, gp, 0:kk_], zr[:, gp:gp + 1], C1[:, gp, kk_:2 * kk_], ALU.mult, ALU.add, R, W)
                ts("dve", S1[:, gp, kk_:2 * kk_], S1[:, gp, 0:kk_], t0[:, gp:gp + 1], None, ALU.mult, ALU.bypass, R, W)
                stt(S1[:, gp, kk_:2 * kk_], C1[:, gp, 0:kk_], t1[:, gp:gp + 1], S1[:, gp, kk_:2 * kk_], ALU.mult, ALU.add, R, W)
            kk_ *= 2
        nr, den, cr, ci = dt_, zr, th, rr
        ts("dve", nr, AR[:], -1.0, None, ALU.add, ALU.bypass, R, W)
        tt("dve", t0, lr[:], lr[:], ALU.mult, R, W)
        tt("dve", t1, li[:], li[:], ALU.mult, R, W)
        tt("dve", den, t0, t1, ALU.add, R, W)
        recip(den, den, R, W)
        tt("dve", t0, nr, lr[:], ALU.mult, R, W)
        tt("dve", t1, AI[:], li[:], ALU.mult, R, W)
        tt("dve", t0, t0, t1, ALU.add, R, W)
        tt("dve", cr, t0, den, ALU.mult, R, W)
        tt("dve", t0, AI[:], lr[:], ALU.mult, R, W)
        tt("dve", t1, nr, li[:], ALU.mult, R, W)
        tt("dve", t0, t0, t1, ALU.subtract, R, W)
        tt("dve", ci, t0, den, ALU.mult, R, W)
        nci = cc
        ts("dve", nci, ci, -1.0, None, ALU.mult, ALU.bypass, R, W)
        for gp in range(16):
            ts("dve", bbar[:, 0, gp, :], braw[:, 0, gp, :], cr[:, gp:gp + 1], None, ALU.mult, ALU.bypass, R, W)
            stt(bbar[:, 0, gp, :], braw[:, 1, gp, :], nci[:, gp:gp + 1], bbar[:, 0, gp, :], ALU.mult, ALU.add, R, W)
            ts("dve", bbar[:, 1, gp, :], braw[:, 1, gp, :], cr[:, gp:gp + 1], None, ALU.mult, ALU.bypass, R, W)
            stt(bbar[:, 1, gp, :], braw[:, 0, gp, :], ci[:, gp:gp + 1], bbar[:, 1, gp, :], ALU.mult, ALU.add, R, W)
        for ri in range(2):
            mset("dve", bmf[:], 0.0, W)
            for gp in range(16):
                c0 = 32 * (gp % 4)
                cp("dve", bmf[0:64, gp, c0:c0 + 16], bbar[0:64, ri, gp, :], R, W)
                cp("dve", bmf[64:128, gp, c0 + 16:c0 + 32], bbar[64:128, ri, gp, :], R, W)
            for gp in range(16):
                pm, tm = nextM()
                P.op("pe", (lambda pm_, gp_: lambda e: e.transpose(pm_[:, 0:128], bmf[:, gp_, :], ident_f[:]))(pm, gp),
                     [tab, tconst], [tm])
                cp("act", BT[ri][:, gp, :], pm[:, 0:128], [tm], [tab])
            mset("dve", CT[ri][:], 0.0, W)
            for gp in range(16):
                c0 = 32 * (gp % 4)
                sc = 1.0 if ri == 0 else -1.0
                ts("dve", CT[ri][0:64, gp, c0:c0 + 16], craw[0:64, ri, gp, :], sc, None, ALU.mult, ALU.bypass, R, W)
                ts("dve", CT[ri][64:128, gp, c0 + 16:c0 + 32], craw[64:128, ri, gp, :], sc, None, ALU.mult, ALU.bypass, R, W)
        s5.append(dict(AR=AR, AI=AI, AR16=AR16, AI16=AI16, BT=BT, CT=CT, RR=RR, C1=C1, S1=S1))

    xT = P.sbuf("xT", [128, 8, NT], F32); tx = T(8)
    hT = P.sbuf("hT", [128, 8, NT], BF16); th_ = T(8)
    rstd = P.sbuf("rstd", [128, NT], F32); trs = T()
    qT = P.sbuf("qT", [64, 8, NT], BF16); tq = T()
    kT = [P.sbuf("kT%d" % l, [64, 2, 128 + NT], BF16) for l in range(L)]; tk = T(2)
    vtok = [P.sbuf("vtok%d" % l, [128, NTI + 1, 128], BF16) for l in range(L)]; tv = T(2)
    kvf = P.sbuf("kvf", [128, 256], F32); tkvf = T()
    wkv = P.sbuf("wkv", [128, 8, 256], BF16); twkv = T(); d_wkv = P.new_dsem("d_wkv")
    uT = P.sbuf("uT", [128, 4, NT], BF16); tu = T()
    PT = P.sbuf("PT", [128, 2, 512], BF16); tPT = T()
    den_sb = P.sbuf("den_sb", [64, 512], F32); tden = T()
    attnT = P.sbuf("attnT", [64, 8, NT], F32); tat = T()
    attnB = P.sbuf("attnB", [64, 8, NT], BF16); tatb = T()
    bu = [P.sbuf("bu%d" % ri, [128, 16, 64], F32) for ri in range(2)]; tbu = T(2)
    xs5 = [P.sbuf("xs5_%d" % ri, [128, 16, 64], BF16) for ri in range(2)]; txs = T(2)
    Xs_ = [[P.sbuf("Xs_%d_%d" % (ri, pp), [128, 16, 16], F32) for pp in range(2)] for ri in range(2)]
    tXs_ = [[T() for pp in range(2)] for ri in range(2)]
    Xst = [Xs_, Xs_]; tX = [tXs_, tXs_]
    Xp = [[P.sbuf("Xp%d_%d" % (l, ri), [128, 16], F32) for ri in range(2)] for l in range(L)]
    tXp = [[T() for ri in range(2)] for l in range(L)]
    stmp = [P.sbuf("stmp%d" % i, [128, 16, 16], F32) for i in range(4)]; tst = T(4)
    yT = P.sbuf("yT", [128, 4, NT], F32); ty = T()
    tg1 = T(); tg2 = T()
    g1 = G12[:, 0:4, :]; g2 = G12[:, 4:8, :]
    sbf = P.sbuf("sbf", [128, 4, NT], BF16); tsb = T()
    ssmB = P.sbuf("ssmB", [128, 4, NT], BF16); tsmb = T()
    upb2 = [P.sbuf("upb%d" % i, [128, 2, NT + 32], F32) for i in range(2)]; tup2 = [T(2), T(2)]
    cs2 = [P.sbuf("cs_%d" % i, [128, 2, NT], F32) for i in range(2)]; tcs2 = [T(2), T(2)]; tgs = T(2)
    hid = P.sbuf("hid", [128, 21, NT], BF16); thid = T()
    sq = hid[:, 0:8, :]; tsq = thid
    carry = [P.sbuf("carry%d" % l, [128, MT_UP, 2], F32) for l in range(L)]; tcar = [T(MT_UP), T(MT_UP)]
    scarry1 = P.sbuf("scarry", [128, MT_UP, 32], F32); scarry = [scarry1, scarry1]
    kc_f = S5tmp[:, 0:2].rearrange("p a g t -> p (a g t)").rearrange("p (s c) -> p s c", c=128); tkc = ttmp
    kcT = P.sbuf("kcT", [64, 16, 128], BF16); tkcT = T()
    vc_b = P.sbuf("vc_b", [128, 16, 128], BF16); tvc = T()
    kvn_f = P.sbuf("kvn_f", [4, 2, 256], F32); tkvn = T(2)
    vn_b = P.sbuf("vn_b", [4, 16, 128], BF16); tvn = T()
    yout = G12; tyo = T()

    NSLOT = 4
    wsl = [P.sbuf("wsl%d" % i, [128, 8, 128], BF16) for i in range(NSLOT)]
    twsl = T(NSLOT); dwsl = [P.new_dsem("dw%d" % i) for i in range(NSLOT)]
    wcnt = [0]

    def linear(nblk, load_fn, mm_fn, cb_fn, rtrks, group=1, wtrk=()):
        base = wcnt[0]
        wcnt[0] += nblk

        def issue(b):
            si = (base + b) % NSLOT
            for dst, src in load_fn(b, wsl[si]):
                P.dma("sp", dwsl[si], dst, src, reads=list(wtrk), writes=[twsl[si]])
        for b in range(min(NSLOT - 1, nblk)):
            issue(b)
        for b in range(nblk):
            if b + NSLOT - 1 < nblk:
                issue(b + NSLOT - 1)
            si = (base + b) % NSLOT
            bi_ = (b // group) % 4
            ps, tps = (psA[0], psA[1], psS[0], psS[1])[bi_], (tA[0], tA[1], tS[0], tS[1])[bi_]
            pairs = mm_fn(b, wsl[si])
            out_ap = pairs[0][2]
            for i, (lt, rh, _) in enumerate(pairs):
                mm(out_ap(ps), lt, rh, i == 0 and b % group == 0, i == len(pairs) - 1 and b % group == group - 1,
                   [twsl[si]] + rtrks, [tps])
            if b % group == group - 1:
                cb_fn(b // group, ps, tps)

    def rmsnorm(src, tsrc, nk, npart, gain_fn, ntok, dst_fn, tdst, nfeat):
        act(sq[0:npart, 0:nk, 0:ntok], src[0:npart, 0:nk, 0:ntok], AF.Square, tsrc, [tsq])
        pm, tm = nextM()
        for k in range(nk):
            mm(pm[:, 0:ntok], ones_b[0:npart, :], sq[0:npart, k, 0:ntok], k == 0, k == nk - 1, [tsq, tconst], [tm])
        act(rstd[:, 0:ntok], pm[:, 0:ntok], AF.Sqrt, [tm, tconst], [trs], bias=epsT[:], scale=1.0 / nfeat)
        recip(rstd[:, 0:ntok], rstd[:, 0:ntok], [trs], [trs])
        for k in range(nk):
            stt(dst_fn(k), src[0:npart, k, 0:ntok], gain_fn(k), rstd[0:npart, 0:ntok], ALU.mult, ALU.mult,
                tsrc + [trs, tconst], tdst)

    def run_layer(l, grp):
        sample = grp["sample"]
        ntok = grp["ntok"]; NS = grp["NS"]; TL = grp["TL"]
        bidx = grp.get("b", 0)
        tb = s5[l]
        rmsnorm(xT, tx, 8, 128, lambda k: gmix[:, l, k:k + 1], ntok, lambda k: hT[:, k, 0:ntok], th_, D)
        blocks = [(h * 64, 64) for h in range(8)] + [(512 + kv * 64, 64) for kv in range(2)] + [(768 + q * 128, 128) for q in range(4)]
        koff = 0 if sample else 128

        def w_in_load(b, slot):
            c0, msz = blocks[b]
            return [(slot[:, 0:8, 0:msz], wb_in[l, :, :, c0:c0 + msz])]

        def w_in_mm(b, slot):
            c0, msz = blocks[b]
            return [(slot[:, k, 0:msz], hT[:, k, 0:ntok], (lambda ps, msz=msz: ps[0:msz, 0:ntok])) for k in range(8)]

        def w_in_cb(b, ps, tps):
            if b < 8:
                act(qT[:, b, 0:ntok], ps[0:64, 0:ntok], AF.Copy, [tps], [tq], scale=0.125)
            elif b < 10:
                cp("act", kT[l][:, b - 8, koff:koff + ntok], ps[0:64, 0:ntok], [tps], [tk[l]])
            else:
                cp("act", uT[:, b - 10, 0:ntok], ps[:, 0:ntok], [tps], [tu])
        linear(14, w_in_load, w_in_mm, w_in_cb, th_, wtrk=[twbs[("in", l)]])
        P.dma("sp", d_wkv, wkv[:], wb_in[l, :, :, 512:768], reads=[twbs[("in", l)]], writes=[twkv])
        if not sample:
            for i in range(NTI):
                pm, tm = nextM()
                for k in range(8):
                    mm(pm[:, 0:256], hT[:, k, i * 128:(i + 1) * 128], wkv[:, k, :], k == 0, k == 7, th_ + [twkv], [tm])
                cp("act", vtok[l][:, i + 1, :], pm[:, 128:256], [tm], [tv[l]])
                if bidx == n_batches - 1 and i == NTI - 1:
                    cp("dve", kvf[:], pm[:, 0:256], [tm], [tkvf])
                    P.dma("pool", d_out, o_kp[l], kvf[:, 0:128], reads=[tkvf])
                    P.dma("pool", d_out, o_vp[l], kvf[:, 128:256], reads=[tkvf])
        else:
            for s_ in range(16):
                pm, tm = nextM()
                for k in range(8):
                    mm(pm[0:4, 0:256], hT[:, k, 4 * s_:4 * s_ + 4], wkv[:, k, :], k == 0, k == 7, th_ + [twkv], [tm])
                cp("act", kvn_f[:, s_ % 2, :], pm[0:4, 0:256], [tm], [tkvn[s_ % 2]])
                cp("dve", vn_b[:, s_, :], kvn_f[:, s_ % 2, 128:256], [tkvn[s_ % 2]], [tvn])
                P.dma("pool", d_kvn[s_ % 2], o_ks[l, s_, 124:128, :], kvn_f[:, s_ % 2, 0:128], reads=[tkvn[s_ % 2]])
                P.dma("pool", d_kvn[s_ % 2], o_vs[l, s_, 124:128, :], kvn_f[:, s_ % 2, 128:256], reads=[tkvn[s_ % 2]])
            P.dma("pool", d_out, o_ks[l, :, 0:124, :], ckr[l, :, 4:128, :])
            P.dma("pool", d_out, o_vs[l, :, 0:124, :], cvr[l, :, 4:128, :])
        attn_units = []
        if not sample:
            def mk_unit(i, kv):
                first = (bidx == 0 and i == 0)
                blks = [1] if first else [0, 1]

                def p1():
                    for blk in blks:
                        kcol = i * 128 + blk * 128
                        mm(psS[blk][:, :], kT[l][:, kv, kcol:kcol + 128], qT[:, 4 * kv:4 * kv + 4, i * 128:(i + 1) * 128],
                           True, False, [tk[l], tq], [tS[blk]])
                        mm(psS[blk][:, :], ident_b[:], biasP[:, blk, kv, :], False, True, [tconst], [tS[blk]])
                        act(PT[:, blk, :], psS[blk][:, :], AF.Exp, [tS[blk]], [tPT])
                    for j, blk in enumerate(blks):
                        mm(psO[0:64, :], vtok[l][:, i + blk, kv * 64:(kv + 1) * 64], PT[:, blk, :], j == 0, j == len(blks) - 1,
                           [tv[l], tPT], [tO])
                    for j, blk in enumerate(blks):
                        mm(psD[0:64, :], ones_b[:, 0:64], PT[:, blk, :], j == 0, j == len(blks) - 1, [tconst, tPT], [tD])

                def p2():
                    for g_ in range(4):
                        ts("dve", den_sb[:, g_ * 128:(g_ + 1) * 128], psD[0:64, g_ * 128:(g_ + 1) * 128], esP[:, l, 4 * kv + g_:4 * kv + g_ + 1], None,
                           ALU.add, ALU.bypass, [tD, tconst], [tden])
                    recip(den_sb[:, :], den_sb[:, :], [tden], [tden])
                    tt("dve", attnT[:, 4 * kv:4 * kv + 4, i * 128:(i + 1) * 128], psO[0:64, :].rearrange("p (g q) -> p g q", g=4),
                       den_sb[:, :].rearrange("p (g q) -> p g q", g=4), ALU.mult, [tO, tden], [tat])
                return p1, p2
            for i in range(NTI):
                for kv in range(2):
                    attn_units.append(mk_unit(i, kv))
        if sample:
            pass
        else:
            pass
        if not sample:
            pass
        else:
            P.dma("sp", d_kc, kc_f[:], ck[l], writes=[tkc, ttq[0], ttq[1]])
            P.dma("pool", d_vc, vc_b[:], cv[l], writes=[tvc])
            for kv in range(2):
                for s_ in range(16):
                    pm, tm = nextM()
                    P.op("pe", (lambda pm_, s2, kv2: lambda e: e.transpose(pm_[0:64, 0:128], kc_f[:, s2, kv2 * 64:(kv2 + 1) * 64], ident_f[:]))(pm, s_, kv),
                         [tkc, tconst], [tm])
                    cp("act", kcT[:, s_, :], pm[0:64, 0:128], [tm], [tkcT])
                mm(psS[0][:, 0:256], ident_b[:], biasSc[:, kv, :], True, False, [tconst], [tS[0]])
                for s_ in range(16):
                    mm(psS[0][:, 16 * s_:16 * s_ + 16], kcT[:, s_, :], qT[:, 4 * kv:4 * kv + 4, 4 * s_:4 * s_ + 4],
                       False, s_ == 15, [tkcT, tq], [tS[0]])
                mm(psS[1][0:4, 0:256], ident_b[0:4, 0:4], biasSn[:, kv, :], True, False, [tconst], [tS[1]])
                for s_ in range(16):
                    mm(psS[1][0:4, 16 * s_:16 * s_ + 16], kT[l][:, kv, 4 * s_:4 * s_ + 4], qT[:, 4 * kv:4 * kv + 4, 4 * s_:4 * s_ + 4],
                       False, s_ == 15, [tk[l], tq], [tS[1]])
                act(PT[:, 0, 0:256], psS[0][:, 0:256], AF.Exp, [tS[0]], [tPT])
                act(PT[0:4, 1, 0:256], psS[1][0:4, 0:256], AF.Exp, [tS[1]], [tPT])
                for s_ in range(16):
                    c = slice(16 * s_, 16 * s_ + 16)
                    mm(psO[0:64, c], vc_b[:, s_, kv * 64:(kv + 1) * 64], PT[:, 0, c], True, False, [tvc, tPT], [tO])
                    mm(psO[0:64, c], vn_b[:, s_, kv * 64:(kv + 1) * 64], PT[0:4, 1, c], False, True, [tvn, tPT], [tO])
                for s_ in range(16):
                    c = slice(16 * s_, 16 * s_ + 16)
                    mm(psD[0:64, c], ones_b[:, 0:64], PT[:, 0, c], True, False, [tconst, tPT], [tD])
                    mm(psD[0:64, c], ones_b[0:4, 0:64], PT[0:4, 1, c], False, True, [tconst, tPT], [tD])
                for g_ in range(4):
                    ts("dve", den_sb[:, 0:256].rearrange("p (s g t) -> p g s t", g=4, t=4)[:, g_],
                       psD[0:64, 0:256].rearrange("p (s g t) -> p g s t", g=4, t=4)[:, g_], esP[:, l, 4 * kv + g_:4 * kv + g_ + 1], None,
                       ALU.add, ALU.bypass, [tD, tconst], [tden])
                recip(den_sb[:, 0:256], den_sb[:, 0:256], [tden], [tden])
                tt("dve", attnT[:, 4 * kv:4 * kv + 4, 0:64].rearrange("p g (s t) -> p s g t", t=4),
                   psO[0:64, 0:256].rearrange("p (s g t) -> p s g t", g=4, t=4),
                   den_sb[:, 0:256].rearrange("p (s g t) -> p s g t", g=4, t=4), ALU.mult, [tO, tden], [tat])
        NTL = NS * TL
        ntile = ntok // NTL

        def s5_bu(it):
            tok0 = it * NTL
            kk = 0
            for ri in range(2):
                for qd in range(4):
                    ps, tps = psA[kk % 2], tA[kk % 2]
                    kk += 1
                    for r4 in range(4):
                        gp = 4 * qd + r4
                        mm(ps[:, r4 * NTL:(r4 + 1) * NTL], tb["BT"][ri][:, gp, :], uT[:, qd, tok0:tok0 + NTL], True, True,
                           [tab, tu], [tps])
                    cp("act", bu[ri][:, 4 * qd:4 * qd + 4, 0:NTL], ps[:, 0:4 * NTL].rearrange("p (g t) -> p g t", t=NTL),
                       [tps], [tbu[ri]])

        def s5_y(it):
            tok0 = it * NTL
            for qd in range(4):
                pm, tm = nextM()
                for r4 in range(4):
                    gp = 4 * qd + r4
                    mm(pm[:, 0:NTL], tb["CT"][0][:, gp, :], xs5[0][:, gp, 0:NTL], r4 == 0, False, [tab, txs[0]], [tm])
                    mm(pm[:, 0:NTL], tb["CT"][1][:, gp, :], xs5[1][:, gp, 0:NTL], False, r4 == 3, [tab, txs[1]], [tm])
                stt(yT[:, qd, tok0:tok0 + NTL], uT[:, qd, tok0:tok0 + NTL], dsk[:, l, qd:qd + 1], pm[:, 0:NTL], ALU.mult, ALU.add,
                    [tu, tconst, tm], [ty])

        if sample:
            it = 0
            tok0 = 0
            s5_bu(0)
            ARt = tb["AR16"][:, :, 0:NS]; AIt = tb["AI16"][:, :, 0:NS]
            for t in range(TL):
                stp = grp["step"]
                cur, nxt = stp % 2, 1 - stp % 2
                grp["step"] += 1
                Xr_c, Xi_c = Xst[l][0][cur][:, :, 0:NS], Xst[l][1][cur][:, :, 0:NS]
                Xr_n, Xi_n = Xst[l][0][nxt][:, :, 0:NS], Xst[l][1][nxt][:, :, 0:NS]
                tXr_c, tXi_c, tXr_n, tXi_n = tX[l][0][cur], tX[l][1][cur], tX[l][0][nxt], tX[l][1][nxt]
                bur = bu[0][:, :, 0:NTL].rearrange("p g (s t) -> p g s t", t=TL)[:, :, :, t]
                bui = bu[1][:, :, 0:NTL].rearrange("p g (s t) -> p g s t", t=TL)[:, :, :, t]
                a0, a1, a2, a3 = [s[:, :, 0:NS] for s in stmp]
                tt("dve", a0, Xr_c, ARt, ALU.mult, [tXr_c, tab], [tst[0]])
                tt("dve", a1, Xi_c, AIt, ALU.mult, [tXi_c, tab], [tst[1]])
                tt("dve", a0, a0, a1, ALU.subtract, [tst[0], tst[1]], [tst[0]])
                tt("dve", Xr_n, a0, bur, ALU.add, [tst[0], tbu[0]], [tXr_n])
                tt("dve", a2, Xr_c, AIt, ALU.mult, [tXr_c, tab], [tst[2]])
                tt("dve", a3, Xi_c, ARt, ALU.mult, [tXi_c, tab], [tst[3]])
                tt("dve", a2, a2, a3, ALU.add, [tst[2], tst[3]], [tst[2]])
                tt("dve", Xi_n, a2, bui, ALU.add, [tst[2], tbu[1]], [tXi_n])
                xr_o = xs5[0][:, :, 0:NTL].rearrange("p g (s t) -> p g s t", t=TL)[:, :, :, t]
                xi_o = xs5[1][:, :, 0:NTL].rearrange("p g (s t) -> p g s t", t=TL)[:, :, :, t]
                cp("act", xr_o, Xr_n, [tXr_n], [txs[0]])
                cp("act", xi_o, Xi_n, [tXi_n], [txs[1]])
            s5_y(0)
        else:
            A_, B_, C_, D_ = S5tmp[:, 0], S5tmp[:, 1], S5tmp[:, 2], S5tmp[:, 3]
            tA_, tB_, tC_, tD_ = ttq
            c1 = tb["C1"][:]; s1 = tb["S1"][:]
            bur = bu[0][:, :, 0:TL]; bui = bu[1][:, :, 0:TL]
            Xcr = Xp[l][0]; Xci = Xp[l][1]; tXcr = tXp[l][0]; tXci = tXp[l][1]
            for it in range(ntile):
                s5_bu(it)
                if it < len(attn_units):
                    attn_units[it][0]()
                tt("dve", A_, c1, bur, ALU.mult, [tab, tbu[0], ttmp], [tA_])
                tt("dve", B_, s1, bui, ALU.mult, [tab, tbu[1], ttmp], [tB_])
                tt("dve", A_, A_, B_, ALU.add, [tB_], [tA_])
                tt("dve", B_, c1, bui, ALU.mult, [tab, tbu[1]], [tB_])
                tt("dve", C_, s1, bur, ALU.mult, [tab, tbu[0]], [tC_])
                tt("dve", B_, B_, C_, ALU.subtract, [tC_], [tB_])
                for gp in range(16):
                    P.op("dve", (lambda gp: lambda e: e.tensor_tensor_scan(
                        A_[:, gp, :], tb["RR"][:, gp:gp + 1].to_broadcast([128, TL]), A_[:, gp, :],
                        Xcr[:, gp:gp + 1], ALU.mult, ALU.add))(gp), [tab, tXcr], [tA_])
                    P.op("dve", (lambda gp: lambda e: e.tensor_tensor_scan(
                        B_[:, gp, :], tb["RR"][:, gp:gp + 1].to_broadcast([128, TL]), B_[:, gp, :],
                        Xci[:, gp:gp + 1], ALU.mult, ALU.add))(gp), [tab, tXci], [tB_])
                if it < len(attn_units):
                    attn_units[it][1]()
                if it > 0:
                    s5_y(it - 1)
                tt("dve", C_, c1, A_, ALU.mult, [tab, tA_], [tC_])
                tt("dve", D_, s1, B_, ALU.mult, [tab, tB_], [tD_])
                tt("dve", C_, C_, D_, ALU.subtract, [tD_], [tC_])
                tt("dve", D_, s1, A_, ALU.mult, [tab, tA_, tC_], [tD_])
                tt("dve", A_, c1, B_, ALU.mult, [tab, tB_], [tA_])
                tt("dve", D_, D_, A_, ALU.add, [tA_], [tD_])
                cp("act", xs5[0][:, :, 0:TL], C_, [tC_], [txs[0]])
                cp("act", xs5[1][:, :, 0:TL], D_, [tD_], [txs[1]])
                cp("act", Xcr[:], C_[:, :, TL - 1], [tC_], [tXcr])
                cp("act", Xci[:], D_[:, :, TL - 1], [tD_], [tXci])
            s5_y(ntile - 1)
        if not sample:
            for u_ in attn_units[ntile:]:
                u_[0](); u_[1]()
            cp("dve", kT[l][:, :, 0:128], kT[l][:, :, NT:NT + 128], [tk[l]], [tk[l]])
            cp("dve", vtok[l][:, 0, :], vtok[l][:, NTI, :], [tv[l]], [tv[l]])
        fin = grp["step"] % 2 if sample else 0
        if sample:
            P.dma("pool", d_so[l], o_srs[l], Xst[l][0][fin][:], reads=[tX[l][0][fin]])
            P.dma("pool", d_so[l], o_sis[l], Xst[l][1][fin][:], reads=[tX[l][1][fin]])
        elif bidx == n_batches - 1:
            P.dma("pool", d_out, o_srp[l], Xp[l][0][:], reads=[tXp[l][0]])
            P.dma("pool", d_out, o_sip[l], Xp[l][1][:], reads=[tXp[l][1]])
        Y = yT[:, :, 0:ntok]; G1 = g1[:, :, 0:ntok]; G2 = g2[:, :, 0:ntok]
        tt("dve", G1, Y, Y, ALU.mult, [ty], [tg1])
        ts("dve", G1, G1, 0.044715, 1.0, ALU.mult, ALU.add, [tg1], [tg1])
        tt("dve", G1, G1, Y, ALU.mult, [tg1, ty], [tg1])
        act(G1, G1, AF.Tanh, [tg1], [tg1], scale=0.7978845608028654)
        ts("dve", G2, Y, 0.5, None, ALU.mult, ALU.bypass, [ty], [tg2])
        stt(G1, G1, 1.0, G2, ALU.add, ALU.mult, [tg1, tg2], [tg1])
        cp("act", sbf[:, :, 0:ntok], G1, [tg1], [tsb])
        ts("dve", G2, G1, 0.5, None, ALU.mult, ALU.bypass, [tg1], [tg2])

        def glu_load(b, slot):
            return [(slot[:, 0:4, :], wb_glu[l, :, :, b * 128:(b + 1) * 128])]

        def glu_mm(b, slot):
            return [(slot[:, k, :], sbf[:, k, 0:ntok], (lambda ps: ps[:, 0:ntok])) for k in range(4)]

        def glu_cb(b, ps, tps):
            act(g1[:, b, 0:ntok], ps[:, 0:ntok], AF.Tanh, [tps, tconst], [tg1], bias=hbglu[:, l, b:b + 1], scale=0.5)
            stt(yT[:, b, 0:ntok], g1[:, b, 0:ntok], 1.0, g2[:, b, 0:ntok], ALU.add, ALU.mult, [tg1, tg2], [ty])
        linear(4, glu_load, glu_mm, glu_cb, [tsb], wtrk=[twbs[("glu", l)]])
        rmsnorm(attnT, [tat], 8, 64, lambda k: gattn[:, l, k:k + 1], ntok, lambda k: attnB[:, k, 0:ntok], [tatb], 512)
        rmsnorm(yT, [ty], 4, 128, lambda k: gssm[:, l, k:k + 1], ntok, lambda k: ssmB[:, k, 0:ntok], [tsmb], 512)

        def wo_load(b, slot):
            blk = b // 2
            if b % 2 == 0:
                return [(slot[0:64, 0:8, :], wb_outa[l, :, :, blk * 128:(blk + 1) * 128])]
            return [(slot[:, 0:4, :], wb_outb[l, :, :, blk * 128:(blk + 1) * 128])]

        def wo_mm(b, slot):
            o = (lambda ps: ps[:, 0:ntok])
            if b % 2 == 0:
                return [(slot[0:64, k, :], attnB[:, k, 0:ntok], o) for k in range(8)]
            return [(slot[:, k, :], ssmB[:, k, 0:ntok], o) for k in range(4)]

        def wo_cb(b, ps, tps):
            tt("dve", xT[:, b, 0:ntok], ps[:, 0:ntok], xT[:, b, 0:ntok], ALU.add, [tps, tx[b]], [tx[b]])
        linear(16, wo_load, wo_mm, wo_cb, [tatb, tsmb], group=2, wtrk=[twbs[("outa", l)], twbs[("outb", l)]])
        rmsnorm(xT, tx, 8, 128, lambda k: gffn[:, l, k:k + 1], ntok, lambda k: hT[:, k, 0:ntok], th_, D)
        car = scarry[l] if sample else carry[l]
        CTL = ntok // NS
        W2 = CTL + 2

        def up_load(b, slot):
            m = (b // 2) + 21 * (b % 2)
            return [(slot[:, 0:8, :], wb_up[l, :, :, m * 128:(m + 1) * 128])]

        def up_mm(b, slot):
            return [(slot[:, k, :], hT[:, k, 0:ntok], (lambda ps: ps[:, 0:ntok])) for k in range(8)]

        def up_cb(b, ps, tps):
            w = b % 2
            m = (b // 2) + 21 * w
            pp = (b // 2) % 2
            upb = upb2[pp]; cs_ = cs2[pp]; tup = tup2[pp]; tcs = tcs2[pp]
            ub = upb[:, w, 0:NS * W2].rearrange("p (s t) -> p s t", t=W2)
            cv_ = car[:, m, 0:NS * 2].rearrange("p (s t) -> p s t", t=2)
            cp("act", ub[:, :, 0:2], cv_, [tcar[l][m]], [tup[w]])
            cp("act", ub[:, :, 2:W2], ps[:, 0:ntok].rearrange("p (s t) -> p s t", t=CTL), [tps], [tup[w]])
            cp("dve", cv_, ub[:, :, CTL:CTL + 2], [tup[w]], [tcar[l][m]])
            cv3 = cs_[:, w, 0:ntok].rearrange("p (s t) -> p s t", t=CTL)
            act(cv3, ub[:, :, 2:W2], AF.Identity, [tup[w], tconst], [tcs[w]], bias=cbs[:, l, m:m + 1], scale=cws[:, l, m, 2:3])
            stt(cv3, ub[:, :, 1:1 + CTL], cws[:, l, m, 1:2], cv3, ALU.mult, ALU.add, [tup[w], tconst, tcs[w]], [tcs[w]])
            stt(cv3, ub[:, :, 0:CTL], cws[:, l, m, 0:1], cv3, ALU.mult, ALU.add, [tup[w], tconst, tcs[w]], [tcs[w]])
            if w == 1:
                mh = b // 2
                A_ = cs_[:, 0, 0:ntok]; G_ = cs_[:, 1, 0:ntok]
                act(g1[:, pp, 0:ntok], G_, AF.Tanh, [tcs[1], tg1], [tgs[pp]], scale=0.5)
                stt(g1[:, pp, 0:ntok], g1[:, pp, 0:ntok], 1.0, G_, ALU.add, ALU.mult, [tgs[pp], tcs[1]], [tgs[pp]])
                stt(hid[:, mh, 0:ntok], g1[:, pp, 0:ntok], 0.5, A_, ALU.mult, ALU.mult, [tgs[pp], tcs[0]], [thid])
        linear(42, up_load, up_mm, up_cb, th_, wtrk=[twbs[("up", l)]])
        if sample:
            P.dma("pool", d_out, o_cs[l], car[:, :, 0:32].rearrange("p m (s t) -> p m s t", t=2), reads=tcar[l])
        elif bidx == n_batches - 1:
            P.dma("pool", d_out, o_cp[l], car[:, :, 0:2], reads=tcar[l])

        def dn_load(b, slot):
            k0, nk = 7 * (b % 3), 7
            return [(slot[:, 0:nk, :], wb_down[l, :, k0:k0 + nk, (b // 3) * 128:(b // 3 + 1) * 128])]

        def dn_mm(b, slot):
            k0, nk = 7 * (b % 3), 7
            return [(slot[:, k, :], hid[:, k0 + k, 0:ntok], (lambda ps: ps[:, 0:ntok])) for k in range(nk)]

        def dn_cb(b, ps, tps):
            tt("dve", xT[:, b, 0:ntok], ps[:, 0:ntok], xT[:, b, 0:ntok], ALU.add, [tps, tx[b]], [tx[b]])
        linear(24, dn_load, dn_mm, dn_cb, [thid], group=3, wtrk=[twbs[("down", l)]])

    d_x = P.new_dsem("d_x"); d_y = P.new_dsem("d_y")
    for l in range(L):
        mset("dve", carry[l][:], 0.0, tcar[l])
        for ri in range(2):
            mset("dve", Xp[l][ri][:], 0.0, [tXp[l][ri]])
    pstep = [0, 0]
    for b in range(n_batches):
        P.dma("pool", d_x, xT[:], xpT[:, :, b * NT:(b + 1) * NT], writes=tx)
        for l in range(L):
            grp = dict(sample=False, ntok=NT, NS=1, TL=TLP, b=b, step=pstep[l])
            run_layer(l, grp)
            pstep[l] = grp["step"]
        rmsnorm(xT, tx, 8, 128, lambda k: gfin[:, k:k + 1], NT, lambda k: yout[:, k, :], [tyo, tg1, tg2] + tgs, D)
        P.dma("pool", d_y, o_ypT[:, :, b * NT:(b + 1) * NT], yout[:], reads=[tyo, tg1, tg2] + tgs)
    if do_sample:
        P.dma("pool", d_x, xT[:, :, 0:64], xsT, writes=tx)
        for l in range(L):
            cur = pstep[l] % 2
            d_s1 = P.new_dsem("d_s1_%d" % l); d_s2 = P.new_dsem("d_s2_%d" % l); d_s3 = P.new_dsem("d_s3_%d" % l)
            P.dma("pool", d_s1, Xst[l][0][cur][:], sre[l], writes=[tX[l][0][cur]])
            P.dma("pool", d_s2, Xst[l][1][cur][:], sim[l], writes=[tX[l][1][cur]])
            P.dma("pool", d_s3, scarry[l][:, :, 0:32].rearrange("p m (s t) -> p m s t", t=2), sconv[l], writes=tcar[l])
            grp = dict(sample=True, ntok=64, NS=16, TL=4, step=pstep[l])
            run_layer(l, grp)
        rmsnorm(xT, tx, 8, 128, lambda k: gfin[:, k:k + 1], 64, lambda k: yout[:, k, 0:64], [tyo, tg1, tg2] + tgs, D)
        P.dma("pool", d_y, o_ysT, yout[:, :, 0:64], reads=[tyo, tg1, tg2] + tgs)
    P.finish()
    return nc


def _consts():
    ident = np.eye(128, dtype=np.float32)
    ones = np.ones((128, 128), np.float32)
    slopes = np.array([2.0 ** -(h + 1) for h in range(8)], np.float32).reshape(2, 4)
    k = np.arange(128)[:, None]; q = np.arange(128)[None, :]
    biasP = np.full((128, 2, 2, 4, 128), NEG, np.float32)
    for kv in range(2):
        for g in range(4):
            sl = slopes[kv, g]
            dcur = q - k
            biasP[:, 1, kv, g, :] = np.where(dcur >= 0, -sl * dcur, NEG)
            dprev = q + 128 - k
            biasP[:, 0, kv, g, :] = np.where(dprev < 128, -sl * dprev, NEG)
    biasP = biasP.reshape(128, 2, 2, 512)
    j = np.arange(128)[:, None]; tq = np.arange(4)[None, :]
    bSc = np.full((128, 2, 16, 4, 4), NEG, np.float32)
    bSn = np.full((4, 2, 16, 4, 4), NEG, np.float32)
    tk_ = np.arange(4)[:, None]
    for kv in range(2):
        for g in range(4):
            sl = slopes[kv, g]
            dist = tq + 128 - j
            bSc[:, kv, :, g, :] = np.where(dist < 128, -sl * dist, NEG)[:, None, :]
            dn = tq - tk_
            bSn[:, kv, :, g, :] = np.where(dn >= 0, -sl * dn, NEG)[:, None, :]
    return dict(c_ident=ident, c_ones=ones, c_biasP=biasP, c_biasSc=bSc.reshape(128, 2, 256), c_biasSn=bSn.reshape(4, 2, 256))


def _ktile(w, kt):
    Lw, K, N = w.shape
    return np.ascontiguousarray(w.reshape(Lw, kt, K // kt, N).transpose(0, 2, 1, 3))


def _vec(v, kt):
    Lw, F = v.shape
    return np.ascontiguousarray(v.reshape(Lw, kt, F // kt).transpose(2, 0, 1))


def _l0(a):
    sh = a.shape
    a = a.reshape(sh[0], 16, 2, 64, *sh[3:])
    perm = (2, 3, 0, 1) + tuple(range(4, a.ndim))
    a = a.transpose(perm)
    return np.ascontiguousarray(a.reshape(128, sh[0], 16, *sh[3:]))


_PROG = {}


def kernel(x_prompt, x_sample, cache_k_win, cache_v_win, state_ssm_re, state_ssm_im, state_conv,
           norm_mix, w_in, sinks, lam_re, lam_im, log_step, b_re, b_im, c_re, c_im, d_skip,
           w_glu, b_glu, g_attn, g_ssm, w_out, norm_ffn, w_up, conv_w, conv_b, w_down, norm_final):
    f = lambda a: np.ascontiguousarray(np.asarray(a, dtype=np.float32))
    (x_prompt, x_sample, cache_k_win, cache_v_win, state_ssm_re, state_ssm_im, state_conv, norm_mix, w_in, sinks,
     lam_re, lam_im, log_step, b_re, b_im, c_re, c_im, d_skip, w_glu, b_glu, g_attn, g_ssm, w_out, norm_ffn, w_up,
     conv_w, conv_b, w_down, norm_final) = [f(a) for a in (
        x_prompt, x_sample, cache_k_win, cache_v_win, state_ssm_re, state_ssm_im, state_conv, norm_mix, w_in, sinks,
        lam_re, lam_im, log_step, b_re, b_im, c_re, c_im, d_skip, w_glu, b_glu, g_attn, g_ssm, w_out, norm_ffn, w_up,
        conv_w, conv_b, w_down, norm_final)]
    if "nc" not in _PROG:
        _PROG["nc"] = build_program()
    nc = _PROG["nc"]
    shared = dict(_consts())
    shared.update(
        w_in=_ktile(w_in, 8), w_outa=_ktile(w_out[:, :512], 8), w_outb=_ktile(w_out[:, 512:], 4),
        w_glu=_ktile(w_glu, 4), w_up=_ktile(w_up, 8), w_down=_ktile(w_down, 21),
        g_mix=_vec(norm_mix, 8), g_ffn=_vec(norm_ffn, 8), g_fin=_vec(norm_final[None], 8)[:, 0],
        g_attn=_vec(g_attn, 8), g_ssm=_vec(g_ssm, 4), b_glu=_vec(b_glu, 4), d_skip=_vec(d_skip, 4),
        cw=np.ascontiguousarray(conv_w.reshape(L, 3, MT_UP, 128).transpose(3, 0, 2, 1)),
        cb=_vec(conv_b, MT_UP),
        sinkP=np.ascontiguousarray(np.broadcast_to(sinks[None], (64, L, 8))),
        lamre=_l0(lam_re), lamim=_l0(lam_im),
        lstep=_l0(np.broadcast_to(log_step[:, :, None], (L, 32, 64))),
        bre=_l0(b_re), bim=_l0(b_im),
        cre=_l0(c_re.transpose(0, 1, 3, 2)), cim=_l0(c_im.transpose(0, 1, 3, 2)),
    )
    shared = {k: np.ascontiguousarray(v, dtype=np.float32) for k, v in shared.items()}
    in_maps = []
    for c in range(8):
        s0 = 16 * c
        m = dict(shared)
        m["xpT"] = np.ascontiguousarray(x_prompt[c].T.reshape(8, 128, SEQ).transpose(1, 0, 2))
        m["xsT"] = np.ascontiguousarray(x_sample[s0:s0 + 16].reshape(64, D).T.reshape(8, 128, 64).transpose(1, 0, 2))
        ckc = cache_k_win[:, s0:s0 + 16].reshape(L, 16, 128, 128); cvc = cache_v_win[:, s0:s0 + 16].reshape(L, 16, 128, 128)
        m["ck"] = np.ascontiguousarray(ckc.transpose(0, 2, 1, 3)); m["cv"] = np.ascontiguousarray(cvc.transpose(0, 2, 1, 3))
        m["ckr"] = np.ascontiguousarray(ckc); m["cvr"] = np.ascontiguousarray(cvc)
        def st(a):
            a = a[:, s0:s0 + 16].reshape(L, 16, 16, 2, 64).transpose(0, 3, 4, 2, 1)
            return np.ascontiguousarray(a.reshape(L, 128, 16, 16))
        m["sre"] = st(state_ssm_re); m["sim"] = st(state_ssm_im)
        m["sconv"] = np.ascontiguousarray(state_conv[:, s0:s0 + 16].reshape(L, 16, 2, MT_UP, 128).transpose(0, 4, 3, 1, 2))
        in_maps.append(m)
    res = run_bass_kernel_spmd(nc, in_maps, core_ids=list(range(8)))
    R = res.results
    B = 8
    y_p = np.stack([R[c]["o_ypT"].transpose(1, 0, 2).reshape(D, SEQ).T for c in range(B)])
    y_s = np.concatenate([R[c]["o_ysT"].transpose(1, 0, 2).reshape(D, 64).T.reshape(16, 4, D) for c in range(B)])
    k_p = np.stack([R[c]["o_kp"] for c in range(B)], 1).reshape(L, B, 128, 2, 64)
    v_p = np.stack([R[c]["o_vp"] for c in range(B)], 1).reshape(L, B, 128, 2, 64)

    def unst_p(key):
        a = np.stack([R[c][key] for c in range(B)], 1)
        a = a.reshape(L, B, 2, 64, 16).transpose(0, 1, 4, 2, 3)
        return np.ascontiguousarray(a.reshape(L, B, 32, 64))
    sr_p = unst_p("o_srp"); si_p = unst_p("o_sip")
    c_p = np.stack([R[c]["o_cp"] for c in range(B)], 1)
    c_p = np.ascontiguousarray(c_p.transpose(0, 1, 4, 3, 2).reshape(L, B, 2, 2 * DFF))
    k_s = np.concatenate([R[c]["o_ks"] for c in range(B)], 1).reshape(L, 128, 128, 2, 64)
    v_s = np.concatenate([R[c]["o_vs"] for c in range(B)], 1).reshape(L, 128, 128, 2, 64)

    def unst_s(key):
        a = np.stack([R[c][key] for c in range(B)], 1)
        a = a.reshape(L, B, 2, 64, 16, 16).transpose(0, 1, 5, 4, 2, 3)
        return np.ascontiguousarray(a.reshape(L, B * 16, 32, 64))
    sr_s = unst_s("o_srs"); si_s = unst_s("o_sis")
    c_s = np.stack([R[c]["o_cs"] for c in range(B)], 1)
    c_s = np.ascontiguousarray(c_s.transpose(0, 1, 4, 5, 3, 2).reshape(L, B * 16, 2, 2 * DFF))
    outs = (y_p, y_s, k_p, v_p, sr_p, si_p, c_p, k_s, v_s, sr_s, si_s, c_s)
    return tuple(np.ascontiguousarray(o, dtype=np.float32) for o in outs)
```

```python
import numpy as np
import concourse.bass as bass
import concourse.mybir as mybir

F32 = mybir.dt.float32
BF16 = mybir.dt.bfloat16
I32 = mybir.dt.int32
ALU = mybir.AluOpType
AF = mybir.ActivationFunctionType
AX = mybir.AxisListType

ENGS = ("pe", "act", "dve", "pool", "sp")


class Trk:
    __slots__ = ("name", "w", "rs", "excl")

    def __init__(self, name="", excl=False):
        self.name = name
        self.excl = excl
        self.w = None
        self.rs = []


class Op:
    __slots__ = ("eng", "idx", "fn", "deps", "flag", "dma", "val")

    def __init__(self, eng, idx, fn, deps, dma):
        self.eng, self.idx, self.fn, self.deps, self.dma = eng, idx, fn, deps, dma
        self.flag = False
        self.val = None


class Prog:
    def __init__(self, nc):
        self.nc = nc
        self.ops = {e: [] for e in ENGS}
        self.dsems = []
        self._ctx = []

    def enter(self, cm):
        v = cm.__enter__()
        self._ctx.append(cm)
        return v

    def sbuf(self, name, shape, dt):
        return self.enter(self.nc.sbuf_tensor(name, list(shape), dt))

    def psum(self, name, shape, dt=F32):
        return self.enter(self.nc.psum_tensor(name, list(shape), dt))

    def new_dsem(self, name):
        s = self.enter(self.nc.semaphore(name))
        d = {"sem": s, "cnt": 0}
        self.dsems.append(d)
        return d

    def _deps(self, eng, reads, writes):
        deps = []
        for t in reads:
            if t.w is not None:
                deps.append(t.w)
        for t in writes:
            if t.w is not None:
                deps.append(t.w)
            deps.extend(t.rs)
        return deps

    def op(self, eng, fn, reads=(), writes=()):
        ex = [t for t in reads if t.excl]
        if ex:
            reads = [t for t in reads if not t.excl]
            writes = list(writes) + ex
        deps = self._deps(eng, reads, writes)
        o = Op(eng, len(self.ops[eng]), fn, deps, None)
        self.ops[eng].append(o)
        for t in reads:
            t.rs.append(o)
        for t in writes:
            t.w = o
            t.rs = []
        return o

    def dma(self, q, dsem, out, in_, reads=(), writes=(), **kw):
        deps = self._deps(q, reads, writes)

        def fn(e):
            return e.dma_start(out=out, in_=in_, **kw)
        o = Op(q, len(self.ops[q]), fn, deps, dsem)
        dsem["cnt"] += 16
        o.val = dsem["cnt"]
        o.flag = True
        self.ops[q].append(o)
        for t in reads:
            t.rs.append(o)
        for t in writes:
            t.w = o
            t.rs = []
        return o

    def barrier(self, eng, dsem, fn, trks):
        dep = Op("sp", -1, None, [], dsem)
        dep.val = dsem["cnt"]
        dep.flag = True
        deps = [dep] + [t.w for t in trks if t.w is not None]
        o = Op(eng, len(self.ops[eng]), fn, deps, None)
        self.ops[eng].append(o)
        for t in trks:
            t.w = o
            t.rs = []
        return o

    def finish(self, final_waits=()):
        nc = self.nc
        for e in ENGS:
            for o in self.ops[e]:
                for d in o.deps:
                    if d.dma is None:
                        if d.eng == "pe" and e == "pe":
                            continue
                        d.flag = True
        esem = {}
        for e in ENGS:
            esem[e] = self.enter(nc.semaphore("esem_" + e))
            c = 0
            for o in self.ops[e]:
                if o.dma is None:
                    if o.flag:
                        c += 1
                        o.val = c
                    else:
                        o.val = None
        nxt = {}
        for e in ENGS:
            arr = [None] * len(self.ops[e])
            cur = None
            for i in range(len(self.ops[e]) - 1, -1, -1):
                o = self.ops[e][i]
                if o.dma is None and o.flag:
                    cur = o.val
                arr[i] = cur
            nxt[e] = arr
        self.nwaits = 0
        prog = self

        def emit(ename):
            def body(eh):
                seen = {}
                for o in prog.ops[ename]:
                    need = {}
                    for d in o.deps:
                        if d.dma is not None:
                            key = ("d", id(d.dma))
                            sem, val = d.dma["sem"], d.val
                        else:
                            if d.eng == "pe" and ename == "pe":
                                continue
                            key = ("e", d.eng)
                            sem, val = esem[d.eng], d.val
                            assert val is not None
                        if seen.get(key, 0) >= val:
                            continue
                        if key not in need or need[key][1] < val:
                            need[key] = (sem, val)
                    for key, (sem, val) in need.items():
                        eh.wait_ge(sem, val)
                        seen[key] = val
                        prog.nwaits += 1
                    if o.fn is None:
                        continue
                    ins = o.fn(eh)
                    if o.dma is not None:
                        ins.then_inc(o.dma["sem"], 16)
                    elif o.flag:
                        ins.then_inc(esem[ename], 1)
                if ename == "sp":
                    for d in prog.dsems:
                        if d["cnt"] > 0:
                            eh.wait_ge(d["sem"], d["cnt"])
            return body

        with nc.Block() as block:
            block.tensor(emit("pe"))
            block.scalar(emit("act"))
            block.vector(emit("dve"))
            block.gpsimd(emit("pool"))
            block.sync(emit("sp"))
        for cm in reversed(self._ctx):
            cm.__exit__(None, None, None)
        self._ctx = []

import math
from concourse.bass_utils import run_bass_kernel_spmd

D = 1024; L = 2; SEQ = 4096; NB = 16; NT = 256; TLP = 64; NTI = 2; DFF = 2688; MT_UP = 42
NEG = -30000.0
EPS = 1e-5
PI = math.pi


def build_program(n_batches=NB, do_sample=True):
    nc = bass.Bass("TRN2", target_bir_lowering=False)
    P = Prog(nc)

    def din(name, shape):
        return nc.dram_tensor(name, list(shape), F32, kind="ExternalInput").ap()

    def dout(name, shape):
        return nc.dram_tensor(name, list(shape), F32, kind="ExternalOutput").ap()

    xpT = din("xpT", [128, 8, SEQ]); xsT = din("xsT", [128, 8, 64])
    ck = din("ck", [L, 128, 16, 128]); cv = din("cv", [L, 128, 16, 128])
    ckr = din("ckr", [L, 16, 128, 128]); cvr = din("cvr", [L, 16, 128, 128])
    sre = din("sre", [L, 128, 16, 16]); sim = din("sim", [L, 128, 16, 16])
    sconv = din("sconv", [L, 128, MT_UP, 16, 2])
    w_in = din("w_in", [L, 128, 8, 1280])
    w_outa = din("w_outa", [L, 64, 8, 1024]); w_outb = din("w_outb", [L, 128, 4, 1024])
    w_glu = din("w_glu", [L, 128, 4, 512])
    w_up = din("w_up", [L, 128, 8, 2 * DFF]); w_down = din("w_down", [L, 128, 21, 1024])
    g_mix = din("g_mix", [128, L, 8]); g_ffn = din("g_ffn", [128, L, 8]); g_fin = din("g_fin", [128, 8])
    g_attn = din("g_attn", [64, L, 8]); g_ssm = din("g_ssm", [128, L, 4])
    b_glu = din("b_glu", [128, L, 4]); d_skip = din("d_skip", [128, L, 4])
    cw = din("cw", [128, L, MT_UP, 3]); cb = din("cb", [128, L, MT_UP])
    sinkP = din("sinkP", [64, L, 8])
    lamre = din("lamre", [128, L, 16]); lamim = din("lamim", [128, L, 16]); lstep = din("lstep", [128, L, 16])
    bre = din("bre", [128, L, 16, 16]); bim = din("bim", [128, L, 16, 16])
    cre = din("cre", [128, L, 16, 16]); cim = din("cim", [128, L, 16, 16])
    c_ident = din("c_ident", [128, 128]); c_ones = din("c_ones", [128, 128])
    c_biasP = din("c_biasP", [128, 2, 2, 512]); c_biasSc = din("c_biasSc", [128, 2, 256]); c_biasSn = din("c_biasSn", [4, 2, 256])

    o_ypT = dout("o_ypT", [128, 8, SEQ]); o_ysT = dout("o_ysT", [128, 8, 64])
    o_kp = dout("o_kp", [L, 128, 128]); o_vp = dout("o_vp", [L, 128, 128])
    o_srp = dout("o_srp", [L, 128, 16]); o_sip = dout("o_sip", [L, 128, 16])
    o_cp = dout("o_cp", [L, 128, MT_UP, 2])
    o_ks = dout("o_ks", [L, 16, 128, 128]); o_vs = dout("o_vs", [L, 16, 128, 128])
    o_srs = dout("o_srs", [L, 128, 16, 16]); o_sis = dout("o_sis", [L, 128, 16, 16])
    o_cs = dout("o_cs", [L, 128, MT_UP, 16, 2])

    def dscr(name, shape):
        return nc.dram_tensor(name, list(shape), BF16, kind="Internal").ap()
    wb_in = dscr("wb_in", [L, 128, 8, 1280]); wb_outa = dscr("wb_outa", [L, 64, 8, 1024]); wb_outb = dscr("wb_outb", [L, 128, 4, 1024])
    wb_glu = dscr("wb_glu", [L, 128, 4, 512]); wb_up = dscr("wb_up", [L, 128, 8, 2 * DFF]); wb_down = dscr("wb_down", [L, 128, 21, 1024])

    def T(n=1):
        return [Trk() for _ in range(n)] if n > 1 else Trk()

    def act(out, in_, func, reads, writes, bias=None, scale=None):
        kw = {}
        if bias is not None:
            kw["bias"] = bias
        if scale is not None:
            kw["scale"] = scale
        return P.op("act", lambda e: e.activation(out, in_, func, **kw), reads, writes)

    def tt(eng, out, a, b, op, reads, writes):
        return P.op(eng, lambda e: e.tensor_tensor(out, a, b, op), reads, writes)

    def ts(eng, out, a, s1, s2, op0, op1, reads, writes):
        return P.op(eng, lambda e: e.tensor_scalar(out, a, s1, s2, op0, op1), reads, writes)

    def stt(out, a, s, b, op0, op1, reads, writes):
        return P.op("dve", lambda e: e.scalar_tensor_tensor(out, a, s, b, op0, op1), reads, writes)

    def cp(eng, out, in_, reads, writes):
        if eng == "act":
            return P.op("act", lambda e: e.activation(out, in_, AF.Copy), reads, writes)
        return P.op(eng, lambda e: e.tensor_copy(out, in_), reads, writes)

    def mm(out, lhsT, rhs, start, stop, reads, writes):
        return P.op("pe", lambda e: e.matmul(out, lhsT, rhs, start=start, stop=stop), reads, writes)

    def mset(eng, ap, val, writes):
        return P.op(eng, lambda e: e.memset(ap, val), (), writes)

    def recip(out, in_, reads, writes):
        return P.op("dve", lambda e: e.reciprocal(out, in_), reads, writes)

    d_pre = P.new_dsem("d_pre"); d_kc = P.new_dsem("d_kc"); d_vc = P.new_dsem("d_vc")
    d_kvn = [P.new_dsem("d_kvn0"), P.new_dsem("d_kvn1")]; d_so = [P.new_dsem("d_so0"), P.new_dsem("d_so1")]
    d_out = P.new_dsem("d_out")

    d_pre2 = P.new_dsem("d_pre2")

    def load(dst, src, trk, q="sp"):
        return P.dma(q, d_pre if q == "sp" else d_pre2, dst, src, writes=[trk])

    ident_f = P.sbuf("ident_f", [128, 128], F32); ident_b = P.sbuf("ident_b", [128, 128], BF16)
    ones_b = P.sbuf("ones_b", [128, 128], BF16)
    biasP = P.sbuf("biasP", [128, 2, 2, 512], BF16)
    biasSc = P.sbuf("biasSc", [128, 2, 256], BF16); biasSn = P.sbuf("biasSn", [4, 2, 256], BF16)
    tconst = T()
    load(ident_f[:], c_ident, tconst)
    load(ident_b[:], c_ident, tconst, "pool"); load(ones_b[:], c_ones, tconst, "pool")
    load(biasP[:], c_biasP, tconst, "pool"); load(biasSc[:], c_biasSc, tconst, "pool"); load(biasSn[:], c_biasSn, tconst, "pool")
    gmix = P.sbuf("gmix", [128, L, 8], F32); gffn = P.sbuf("gffn", [128, L, 8], F32); gfin = P.sbuf("gfin", [128, 8], F32)
    gattn = P.sbuf("gattn", [64, L, 8], F32); gssm = P.sbuf("gssm", [128, L, 4], F32)
    bglu = P.sbuf("bglu", [128, L, 4], F32); hbglu = P.sbuf("hbglu", [128, L, 4], F32); dsk = P.sbuf("dsk", [128, L, 4], F32)
    cws = P.sbuf("cws", [128, L, MT_UP, 3], F32); cbs = P.sbuf("cbs", [128, L, MT_UP], F32)
    esP = P.sbuf("esP", [64, L, 8], F32)
    for dst, src in ((gmix, g_mix), (gffn, g_ffn), (gfin, g_fin), (gattn, g_attn), (gssm, g_ssm), (bglu, b_glu),
                     (dsk, d_skip), (cws, cw), (cbs, cb), (esP, sinkP)):
        load(dst[:], src, tconst)
    epsT = P.sbuf("epsT", [128, 1], F32); hpiT = P.sbuf("hpiT", [128, 1], F32)
    P.barrier("dve", d_pre, lambda e: e.memset(epsT[:], EPS), [tconst])
    P.barrier("dve", d_pre2, lambda e: e.memset(hpiT[:], PI / 2), [tconst])
    mset("dve", epsT[:], EPS, [tconst]); mset("dve", hpiT[:], PI / 2, [tconst])
    act(esP[:], esP[:], AF.Exp, [tconst], [tconst])
    ts("dve", hbglu[:], bglu[:], 0.5, None, ALU.mult, ALU.bypass, [tconst], [tconst])

    twbs = {}
    for l_ in range(L):
        for nm_, dst_, src_, kt_ in (("in", wb_in, w_in, 8), ("outa", wb_outa, w_outa, 8), ("outb", wb_outb, w_outb, 4),
                                     ("glu", wb_glu, w_glu, 4), ("up", wb_up, w_up, 8), ("down", wb_down, w_down, 21)):
            d_cvt = P.new_dsem("d_cvt_%s%d" % (nm_, l_)); t_ = Trk()
            twbs[(nm_, l_)] = t_
            for k_ in range(kt_):
                P.dma("pool", d_cvt, dst_[l_, :, k_, :], src_[l_, :, k_, :], writes=[t_])


    XT = lambda n=1: [Trk(excl=True) for _ in range(n)] if n > 1 else Trk(excl=True)
    psA = [P.psum("psA%d" % i, [128, 512]) for i in range(2)]; tA = XT(2)
    psS = [P.psum("psS%d" % i, [128, 512]) for i in range(2)]; tS = XT(2)
    psO = P.psum("psO", [128, 512]); tO = XT()
    psD = P.psum("psD", [128, 512]); tD = XT()
    psM = [P.psum("psM%d" % i, [128, 512]) for i in range(2)]; tM = XT(2)
    cntM = [0]

    def nextM():
        i = cntM[0] % 2
        cntM[0] += 1
        return psM[i], tM[i]

    s5 = []
    scr = [P.sbuf("s5scr%d" % i, [128, 16], F32) for i in range(8)]
    G12 = P.sbuf("G12", [128, 8, NT], F32)
    bmf = G12[:].rearrange("p a (b c) -> p (a b) c", c=128)
    S5tmp = P.sbuf("S5tmp", [128, 4, 16, TLP], F32); ttmp = T(); ttq = T(4)
    braw = S5tmp[:, 0].rearrange("p g t -> p (g t)")[:, 0:512].rearrange("p (a g h) -> p a g h", a=2, g=16)
    bbar = S5tmp[:, 1].rearrange("p g t -> p (g t)")[:, 0:512].rearrange("p (a g h) -> p a g h", a=2, g=16)
    craw = S5tmp[:, 2].rearrange("p g t -> p (g t)")[:, 0:512].rearrange("p (a g h) -> p a g h", a=2, g=16)
    tab = T()
    for l in range(L):
        lr = P.sbuf("lr%d" % l, [128, 16], F32); li = P.sbuf("li%d" % l, [128, 16], F32); ls = P.sbuf("ls%d" % l, [128, 16], F32)
        d_tab = P.new_dsem("d_tab%d" % l)
        for dst_, src_ in ((lr[:], lamre[:, l, :]), (li[:], lamim[:, l, :]), (ls[:], lstep[:, l, :]), (braw[:, 0], bre[:, l]),
                           (braw[:, 1], bim[:, l]), (craw[:, 0], cre[:, l]), (craw[:, 1], cim[:, l])):
            P.dma("sp", d_tab, dst_, src_, writes=[tab])
        P.barrier("dve", d_tab, (lambda l_: lambda e: e.memset(scr[0][:], 0.0))(l), [tab])
        AR = P.sbuf("AR%d" % l, [128, 16], F32); AI = P.sbuf("AI%d" % l, [128, 16], F32)
        AR16 = P.sbuf("AR16_%d" % l, [128, 16, 16], F32); AI16 = P.sbuf("AI16_%d" % l, [128, 16, 16], F32)
        BT = [P.sbuf("BT%d_%d" % (l, ri), [128, 16, 128], BF16) for ri in range(2)]
        CT = [P.sbuf("CT%d_%d" % (l, ri), [128, 16, 128], BF16) for ri in range(2)]
        dt_, zr, th, rr, cc, ss, t0, t1 = [s[:] for s in scr]
        R, W = [tab], [tab]
        act(dt_, ls[:], AF.Exp, R, W)
        tt("dve", zr, lr[:], dt_, ALU.mult, R, W)
        tt("dve", th, li[:], dt_, ALU.mult, R, W)
        act(rr, zr, AF.Exp, R, W)
        act(cc, th, AF.Sin, R, W, bias=hpiT[:], scale=1.0 / 32)
        act(ss, th, AF.Sin, R, W, scale=1.0 / 32)
        for _ in range(5):
            tt("dve", t0, cc, cc, ALU.mult, R, W)
            tt("dve", t1, ss, ss, ALU.mult, R, W)
            tt("dve", ss, ss, cc, ALU.mult, R, W)
            ts("dve", ss, ss, 2.0, None, ALU.mult, ALU.bypass, R, W)
            tt("dve", cc, t0, t1, ALU.subtract, R, W)
        tt("dve", AR[:], rr, cc, ALU.mult, R, W)
        tt("dve", AI[:], rr, ss, ALU.mult, R, W)
        for s_ in range(16):
            cp("dve", AR16[:, :, s_], AR[:], R, W); cp("dve", AI16[:, :, s_], AI[:], R, W)
        RR = P.sbuf("RR%d" % l, [128, 16], F32)
        cp("dve", RR[:], rr, R, W)
        C1 = P.sbuf("C1_%d" % l, [128, 16, TLP], F32); S1 = P.sbuf("S1_%d" % l, [128, 16, TLP], F32)
        cp("dve", C1[:, :, 0], cc, R, W); cp("dve", S1[:, :, 0], ss, R, W)
        kk_ = 1
        while kk_ < TLP:
            cp("dve", t0, C1[:, :, kk_ - 1], R, W); cp("dve", t1, S1[:, :, kk_ - 1], R, W)
            ts("dve", zr, t1, -1.0, None, ALU.mult, ALU.bypass, R, W)
            for gp in range(16):
                ts("dve", C1[:, gp, kk_:2 * kk_], C1[:, gp, 0:kk_], t0[:, gp:gp + 1], None, ALU.mult, ALU.bypass, R, W)
                stt(C1[:, gp, kk_:2 * kk_], S1[:, gp, 0:kk_], zr[:, gp:gp + 1], C1[:, gp, kk_:2 * kk_], ALU.mult, ALU.add, R, W)
                ts("dve", S1[:, gp, kk_:2 * kk_], S1[:, gp, 0:kk_], t0[:, gp:gp + 1], None, ALU.mult, ALU.bypass, R, W)
                stt(S1[:, gp, kk_:2 * kk_], C1[:, gp, 0:kk_], t1[:, gp:gp + 1], S1[:, gp, kk_:2 * kk_], ALU.mult, ALU.add, R, W)
            kk_ *= 2
        nr, den, cr, ci = dt_, zr, th, rr
        ts("dve", nr, AR[:], -1.0, None, ALU.add, ALU.bypass, R, W)
        tt("dve", t0, lr[:], lr[:], ALU.mult, R, W)
        tt("dve", t1, li[:], li[:], ALU.mult, R, W)
        tt("dve", den, t0, t1, ALU.add, R, W)
        recip(den, den, R, W)
        tt("dve", t0, nr, lr[:], ALU.mult, R, W)
        tt("dve", t1, AI[:], li[:], ALU.mult, R, W)
        tt("dve", t0, t0, t1, ALU.add, R, W)
        tt("dve", cr, t0, den, ALU.mult, R, W)
        tt("dve", t0, AI[:], lr[:], ALU.mult, R, W)
        tt("dve", t1, nr, li[:], ALU.mult, R, W)
        tt("dve", t0, t0, t1, ALU.subtract, R, W)
        tt("dve", ci, t0, den, ALU.mult, R, W)
        nci = cc
        ts("dve", nci, ci, -1.0, None, ALU.mult, ALU.bypass, R, W)
        for gp in range(16):
            ts("dve", bbar[:, 0, gp, :], braw[:, 0, gp, :], cr[:, gp:gp + 1], None, ALU.mult, ALU.bypass, R, W)
            stt(bbar[:, 0, gp, :], braw[:, 1, gp, :], nci[:, gp:gp + 1], bbar[:, 0, gp, :], ALU.mult, ALU.add, R, W)
            ts("dve", bbar[:, 1, gp, :], braw[:, 1, gp, :], cr[:, gp:gp + 1], None, ALU.mult, ALU.bypass, R, W)
            stt(bbar[:, 1, gp, :], braw[:, 0, gp, :], ci[:, gp:gp + 1], bbar[:, 1, gp, :], ALU.mult, ALU.add, R, W)
        for ri in range(2):
            mset("dve", bmf[:], 0.0, W)
            for gp in range(16):
                c0 = 32 * (gp % 4)
                cp("dve", bmf[0:64, gp, c0:c0 + 16], bbar[0:64, ri, gp, :], R, W)
                cp("dve", bmf[64:128, gp, c0 + 16:c0 + 32], bbar[64:128, ri, gp, :], R, W)
            for gp in range(16):
                pm, tm = nextM()
                P.op("pe", (lambda pm_, gp_: lambda e: e.transpose(pm_[:, 0:128], bmf[:, gp_, :], ident_f[:]))(pm, gp),
                     [tab, tconst], [tm])
                cp("act", BT[ri][:, gp, :], pm[:, 0:128], [tm], [tab])
            mset("dve", CT[ri][:], 0.0, W)
            for gp in range(16):
                c0 = 32 * (gp % 4)
                sc = 1.0 if ri == 0 else -1.0
                ts("dve", CT[ri][0:64, gp, c0:c0 + 16], craw[0:64, ri, gp, :], sc, None, ALU.mult, ALU.bypass, R, W)
                ts("dve", CT[ri][64:128, gp, c0 + 16:c0 + 32], craw[64:128, ri, gp, :], sc, None, ALU.mult, ALU.bypass, R, W)
        s5.append(dict(AR=AR, AI=AI, AR16=AR16, AI16=AI16, BT=BT, CT=CT, RR=RR, C1=C1, S1=S1))

    xT = P.sbuf("xT", [128, 8, NT], F32); tx = T(8)
    hT = P.sbuf("hT", [128, 8, NT], BF16); th_ = T(8)
    rstd = P.sbuf("rstd", [128, NT], F32); trs = T()
    qT = P.sbuf("qT", [64, 8, NT], BF16); tq = T()
    kT = [P.sbuf("kT%d" % l, [64, 2, 128 + NT], BF16) for l in range(L)]; tk = T(2)
    vtok = [P.sbuf("vtok%d" % l, [128, NTI + 1, 128], BF16) for l in range(L)]; tv = T(2)
    kvf = P.sbuf("kvf", [128, 256], F32); tkvf = T()
    wkv = P.sbuf("wkv", [128, 8, 256], BF16); twkv = T(); d_wkv = P.new_dsem("d_wkv")
    uT = P.sbuf("uT", [128, 4, NT], BF16); tu = T()
    PT = P.sbuf("PT", [128, 2, 512], BF16); tPT = T()
    den_sb = P.sbuf("den_sb", [64, 512], F32); tden = T()
    attnT = P.sbuf("attnT", [64, 8, NT], F32); tat = T()
    attnB = P.sbuf("attnB", [64, 8, NT], BF16); tatb = T()
    bu = [P.sbuf("bu%d" % ri, [128, 16, 64], F32) for ri in range(2)]; tbu = T(2)
    xs5 = [P.sbuf("xs5_%d" % ri, [128, 16, 64], BF16) for ri in range(2)]; txs = T(2)
    Xs_ = [[P.sbuf("Xs_%d_%d" % (ri, pp), [128, 16, 16], F32) for pp in range(2)] for ri in range(2)]
    tXs_ = [[T() for pp in range(2)] for ri in range(2)]
    Xst = [Xs_, Xs_]; tX = [tXs_, tXs_]
    Xp = [[P.sbuf("Xp%d_%d" % (l, ri), [128, 16], F32) for ri in range(2)] for l in range(L)]
    tXp = [[T() for ri in range(2)] for l in range(L)]
    stmp = [P.sbuf("stmp%d" % i, [128, 16, 16], F32) for i in range(4)]; tst = T(4)
    yT = P.sbuf("yT", [128, 4, NT], F32); ty = T()
    tg1 = T(); tg2 = T()
    g1 = G12[:, 0:4, :]; g2 = G12[:, 4:8, :]
    sbf = P.sbuf("sbf", [128, 4, NT], BF16); tsb = T()
    ssmB = P.sbuf("ssmB", [128, 4, NT], BF16); tsmb = T()
    upb2 = [P.sbuf("upb%d" % i, [128, 2, NT + 32], F32) for i in range(2)]; tup2 = [T(2), T(2)]
    cs2 = [P.sbuf("cs_%d" % i, [128, 2, NT], F32) for i in range(2)]; tcs2 = [T(2), T(2)]; tgs = T(2)
    hid = P.sbuf("hid", [128, 21, NT], BF16); thid = T()
    sq = hid[:, 0:8, :]; tsq = thid
    carry = [P.sbuf("carry%d" % l, [128, MT_UP, 2], F32) for l in range(L)]; tcar = [T(MT_UP), T(MT_UP)]
    scarry1 = P.sbuf("scarry", [128, MT_UP, 32], F32); scarry = [scarry1, scarry1]
    kc_f = S5tmp[:, 0:2].rearrange("p a g t -> p (a g t)").rearrange("p (s c) -> p s c", c=128); tkc = ttmp
    kcT = P.sbuf("kcT", [64, 16, 128], BF16); tkcT = T()
    vc_b = P.sbuf("vc_b", [128, 16, 128], BF16); tvc = T()
    kvn_f = P.sbuf("kvn_f", [4, 2, 256], F32); tkvn = T(2)
    vn_b = P.sbuf("vn_b", [4, 16, 128], BF16); tvn = T()
    yout = G12; tyo = T()

    NSLOT = 5
    wsl = [P.sbuf("wsl%d" % i, [128, 8, 128], BF16) for i in range(NSLOT)]
    twsl = T(NSLOT); dwsl = [P.new_dsem("dw%d" % i) for i in range(NSLOT)]
    wcnt = [0]

    def linear(nblk, load_fn, mm_fn, cb_fn, rtrks, group=1, wtrk=()):
        base = wcnt[0]
        wcnt[0] += nblk

        def issue(b):
            si = (base + b) % NSLOT
            for dst, src in load_fn(b, wsl[si]):
                P.dma("sp", dwsl[si], dst, src, reads=list(wtrk), writes=[twsl[si]])
        for b in range(min(NSLOT - 1, nblk)):
            issue(b)
        for b in range(nblk):
            if b + NSLOT - 1 < nblk:
                issue(b + NSLOT - 1)
            si = (base + b) % NSLOT
            bi_ = (b // group) % 4
            ps, tps = (psA[0], psA[1], psS[0], psS[1])[bi_], (tA[0], tA[1], tS[0], tS[1])[bi_]
            pairs = mm_fn(b, wsl[si])
            out_ap = pairs[0][2]
            for i, (lt, rh, _) in enumerate(pairs):
                mm(out_ap(ps), lt, rh, i == 0 and b % group == 0, i == len(pairs) - 1 and b % group == group - 1,
                   [twsl[si]] + rtrks, [tps])
            if b % group == group - 1:
                cb_fn(b // group, ps, tps)

    def rmsnorm(src, tsrc, nk, npart, gain_fn, ntok, dst_fn, tdst, nfeat):
        act(sq[0:npart, 0:nk, 0:ntok], src[0:npart, 0:nk, 0:ntok], AF.Square, tsrc, [tsq])
        pm, tm = nextM()
        for k in range(nk):
            mm(pm[:, 0:ntok], ones_b[0:npart, :], sq[0:npart, k, 0:ntok], k == 0, k == nk - 1, [tsq, tconst], [tm])
        act(rstd[:, 0:ntok], pm[:, 0:ntok], AF.Sqrt, [tm, tconst], [trs], bias=epsT[:], scale=1.0 / nfeat)
        recip(rstd[:, 0:ntok], rstd[:, 0:ntok], [trs], [trs])
        for k in range(nk):
            stt(dst_fn(k), src[0:npart, k, 0:ntok], gain_fn(k), rstd[0:npart, 0:ntok], ALU.mult, ALU.mult,
                tsrc + [trs, tconst], tdst)

    def run_layer(l, grp):
        sample = grp["sample"]
        ntok = grp["ntok"]; NS = grp["NS"]; TL = grp["TL"]
        bidx = grp.get("b", 0)
        tb = s5[l]
        rmsnorm(xT, tx, 8, 128, lambda k: gmix[:, l, k:k + 1], ntok, lambda k: hT[:, k, 0:ntok], th_, D)
        blocks = [(h * 64, 64) for h in range(8)] + [(512 + kv * 64, 64) for kv in range(2)] + [(768 + q * 128, 128) for q in range(4)]
        koff = 0 if sample else 128

        def w_in_load(b, slot):
            c0, msz = blocks[b]
            return [(slot[:, 0:8, 0:msz], wb_in[l, :, :, c0:c0 + msz])]

        def w_in_mm(b, slot):
            c0, msz = blocks[b]
            return [(slot[:, k, 0:msz], hT[:, k, 0:ntok], (lambda ps, msz=msz: ps[0:msz, 0:ntok])) for k in range(8)]

        def w_in_cb(b, ps, tps):
            if b < 8:
                act(qT[:, b, 0:ntok], ps[0:64, 0:ntok], AF.Copy, [tps], [tq], scale=0.125)
            elif b < 10:
                cp("act", kT[l][:, b - 8, koff:koff + ntok], ps[0:64, 0:ntok], [tps], [tk[l]])
            else:
                cp("act", uT[:, b - 10, 0:ntok], ps[:, 0:ntok], [tps], [tu])
        linear(14, w_in_load, w_in_mm, w_in_cb, th_, wtrk=[twbs[("in", l)]])
        P.dma("sp", d_wkv, wkv[:], wb_in[l, :, :, 512:768], reads=[twbs[("in", l)]], writes=[twkv])
        if not sample:
            for i in range(NTI):
                pm, tm = nextM()
                for k in range(8):
                    mm(pm[:, 0:256], hT[:, k, i * 128:(i + 1) * 128], wkv[:, k, :], k == 0, k == 7, th_ + [twkv], [tm])
                cp("act", vtok[l][:, i + 1, :], pm[:, 128:256], [tm], [tv[l]])
                if bidx == n_batches - 1 and i == NTI - 1:
                    cp("dve", kvf[:], pm[:, 0:256], [tm], [tkvf])
                    P.dma("pool", d_out, o_kp[l], kvf[:, 0:128], reads=[tkvf])
                    P.dma("pool", d_out, o_vp[l], kvf[:, 128:256], reads=[tkvf])
        else:
            for s_ in range(16):
                pm, tm = nextM()
                for k in range(8):
                    mm(pm[0:4, 0:256], hT[:, k, 4 * s_:4 * s_ + 4], wkv[:, k, :], k == 0, k == 7, th_ + [twkv], [tm])
                cp("act", kvn_f[:, s_ % 2, :], pm[0:4, 0:256], [tm], [tkvn[s_ % 2]])
                cp("dve", vn_b[:, s_, :], kvn_f[:, s_ % 2, 128:256], [tkvn[s_ % 2]], [tvn])
                P.dma("pool", d_kvn[s_ % 2], o_ks[l, s_, 124:128, :], kvn_f[:, s_ % 2, 0:128], reads=[tkvn[s_ % 2]])
                P.dma("pool", d_kvn[s_ % 2], o_vs[l, s_, 124:128, :], kvn_f[:, s_ % 2, 128:256], reads=[tkvn[s_ % 2]])
            P.dma("pool", d_out, o_ks[l, :, 0:124, :], ckr[l, :, 4:128, :])
            P.dma("pool", d_out, o_vs[l, :, 0:124, :], cvr[l, :, 4:128, :])
        attn_units = []
        if not sample:
            def mk_unit(i, kv):
                first = (bidx == 0 and i == 0)
                blks = [1] if first else [0, 1]

                def p1():
                    for blk in blks:
                        kcol = i * 128 + blk * 128
                        mm(psS[blk][:, :], kT[l][:, kv, kcol:kcol + 128], qT[:, 4 * kv:4 * kv + 4, i * 128:(i + 1) * 128],
                           True, False, [tk[l], tq], [tS[blk]])
                        mm(psS[blk][:, :], ident_b[:], biasP[:, blk, kv, :], False, True, [tconst], [tS[blk]])
                        act(PT[:, blk, :], psS[blk][:, :], AF.Exp, [tS[blk]], [tPT])
                    for j, blk in enumerate(blks):
                        mm(psO[0:64, :], vtok[l][:, i + blk, kv * 64:(kv + 1) * 64], PT[:, blk, :], j == 0, j == len(blks) - 1,
                           [tv[l], tPT], [tO])
                    for j, blk in enumerate(blks):
                        mm(psD[0:64, :], ones_b[:, 0:64], PT[:, blk, :], j == 0, j == len(blks) - 1, [tconst, tPT], [tD])

                def p2():
                    for g_ in range(4):
                        ts("dve", den_sb[:, g_ * 128:(g_ + 1) * 128], psD[0:64, g_ * 128:(g_ + 1) * 128], esP[:, l, 4 * kv + g_:4 * kv + g_ + 1], None,
                           ALU.add, ALU.bypass, [tD, tconst], [tden])
                    recip(den_sb[:, :], den_sb[:, :], [tden], [tden])
                    tt("dve", attnT[:, 4 * kv:4 * kv + 4, i * 128:(i + 1) * 128], psO[0:64, :].rearrange("p (g q) -> p g q", g=4),
                       den_sb[:, :].rearrange("p (g q) -> p g q", g=4), ALU.mult, [tO, tden], [tat])
                return p1, p2
            for i in range(NTI):
                for kv in range(2):
                    attn_units.append(mk_unit(i, kv))
        if sample:
            pass
        else:
            pass
        if not sample:
            pass
        else:
            P.dma("sp", d_kc, kc_f[:], ck[l], writes=[tkc, ttq[0], ttq[1]])
            P.dma("pool", d_vc, vc_b[:], cv[l], writes=[tvc])
            for kv in range(2):
                for s_ in range(16):
                    pm, tm = nextM()
                    P.op("pe", (lambda pm_, s2, kv2: lambda e: e.transpose(pm_[0:64, 0:128], kc_f[:, s2, kv2 * 64:(kv2 + 1) * 64], ident_f[:]))(pm, s_, kv),
                         [tkc, tconst], [tm])
                    cp("act", kcT[:, s_, :], pm[0:64, 0:128], [tm], [tkcT])
                mm(psS[0][:, 0:256], ident_b[:], biasSc[:, kv, :], True, False, [tconst], [tS[0]])
                for s_ in range(16):
                    mm(psS[0][:, 16 * s_:16 * s_ + 16], kcT[:, s_, :], qT[:, 4 * kv:4 * kv + 4, 4 * s_:4 * s_ + 4],
                       False, s_ == 15, [tkcT, tq], [tS[0]])
                mm(psS[1][0:4, 0:256], ident_b[0:4, 0:4], biasSn[:, kv, :], True, False, [tconst], [tS[1]])
                for s_ in range(16):
                    mm(psS[1][0:4, 16 * s_:16 * s_ + 16], kT[l][:, kv, 4 * s_:4 * s_ + 4], qT[:, 4 * kv:4 * kv + 4, 4 * s_:4 * s_ + 4],
                       False, s_ == 15, [tk[l], tq], [tS[1]])
                act(PT[:, 0, 0:256], psS[0][:, 0:256], AF.Exp, [tS[0]], [tPT])
                act(PT[0:4, 1, 0:256], psS[1][0:4, 0:256], AF.Exp, [tS[1]], [tPT])
                for s_ in range(16):
                    c = slice(16 * s_, 16 * s_ + 16)
                    mm(psO[0:64, c], vc_b[:, s_, kv * 64:(kv + 1) * 64], PT[:, 0, c], True, False, [tvc, tPT], [tO])
                    mm(psO[0:64, c], vn_b[:, s_, kv * 64:(kv + 1) * 64], PT[0:4, 1, c], False, True, [tvn, tPT], [tO])
                for s_ in range(16):
                    c = slice(16 * s_, 16 * s_ + 16)
                    mm(psD[0:64, c], ones_b[:, 0:64], PT[:, 0, c], True, False, [tconst, tPT], [tD])
                    mm(psD[0:64, c], ones_b[0:4, 0:64], PT[0:4, 1, c], False, True, [tconst, tPT], [tD])
                for g_ in range(4):
                    ts("dve", den_sb[:, 0:256].rearrange("p (s g t) -> p g s t", g=4, t=4)[:, g_],
                       psD[0:64, 0:256].rearrange("p (s g t) -> p g s t", g=4, t=4)[:, g_], esP[:, l, 4 * kv + g_:4 * kv + g_ + 1], None,
                       ALU.add, ALU.bypass, [tD, tconst], [tden])
                recip(den_sb[:, 0:256], den_sb[:, 0:256], [tden], [tden])
                tt("dve", attnT[:, 4 * kv:4 * kv + 4, 0:64].rearrange("p g (s t) -> p s g t", t=4),
                   psO[0:64, 0:256].rearrange("p (s g t) -> p s g t", g=4, t=4),
                   den_sb[:, 0:256].rearrange("p (s g t) -> p s g t", g=4, t=4), ALU.mult, [tO, tden], [tat])
        NTL = NS * TL
        ntile = ntok // NTL

        def s5_bu(it):
            tok0 = it * NTL
            kk = 0
            for ri in range(2):
                for qd in range(4):
                    ps, tps = psA[kk % 2], tA[kk % 2]
                    kk += 1
                    for r4 in range(4):
                        gp = 4 * qd + r4
                        mm(ps[:, r4 * NTL:(r4 + 1) * NTL], tb["BT"][ri][:, gp, :], uT[:, qd, tok0:tok0 + NTL], True, True,
                           [tab, tu], [tps])
                    cp("act", bu[ri][:, 4 * qd:4 * qd + 4, 0:NTL], ps[:, 0:4 * NTL].rearrange("p (g t) -> p g t", t=NTL),
                       [tps], [tbu[ri]])

        def s5_y(it):
            tok0 = it * NTL
            for qd in range(4):
                pm, tm = nextM()
                for r4 in range(4):
                    gp = 4 * qd + r4
                    mm(pm[:, 0:NTL], tb["CT"][0][:, gp, :], xs5[0][:, gp, 0:NTL], r4 == 0, False, [tab, txs[0]], [tm])
                    mm(pm[:, 0:NTL], tb["CT"][1][:, gp, :], xs5[1][:, gp, 0:NTL], False, r4 == 3, [tab, txs[1]], [tm])
                stt(yT[:, qd, tok0:tok0 + NTL], uT[:, qd, tok0:tok0 + NTL], dsk[:, l, qd:qd + 1], pm[:, 0:NTL], ALU.mult, ALU.add,
                    [tu, tconst, tm], [ty])

        if sample:
            it = 0
            tok0 = 0
            s5_bu(0)
            ARt = tb["AR16"][:, :, 0:NS]; AIt = tb["AI16"][:, :, 0:NS]
            for t in range(TL):
                stp = grp["step"]
                cur, nxt = stp % 2, 1 - stp % 2
                grp["step"] += 1
                Xr_c, Xi_c = Xst[l][0][cur][:, :, 0:NS], Xst[l][1][cur][:, :, 0:NS]
                Xr_n, Xi_n = Xst[l][0][nxt][:, :, 0:NS], Xst[l][1][nxt][:, :, 0:NS]
                tXr_c, tXi_c, tXr_n, tXi_n = tX[l][0][cur], tX[l][1][cur], tX[l][0][nxt], tX[l][1][nxt]
                bur = bu[0][:, :, 0:NTL].rearrange("p g (s t) -> p g s t", t=TL)[:, :, :, t]
                bui = bu[1][:, :, 0:NTL].rearrange("p g (s t) -> p g s t", t=TL)[:, :, :, t]
                a0, a1, a2, a3 = [s[:, :, 0:NS] for s in stmp]
                tt("dve", a0, Xr_c, ARt, ALU.mult, [tXr_c, tab], [tst[0]])
                tt("dve", a1, Xi_c, AIt, ALU.mult, [tXi_c, tab], [tst[1]])
                tt("dve", a0, a0, a1, ALU.subtract, [tst[0], tst[1]], [tst[0]])
                tt("dve", Xr_n, a0, bur, ALU.add, [tst[0], tbu[0]], [tXr_n])
                tt("dve", a2, Xr_c, AIt, ALU.mult, [tXr_c, tab], [tst[2]])
                tt("dve", a3, Xi_c, ARt, ALU.mult, [tXi_c, tab], [tst[3]])
                tt("dve", a2, a2, a3, ALU.add, [tst[2], tst[3]], [tst[2]])
                tt("dve", Xi_n, a2, bui, ALU.add, [tst[2], tbu[1]], [tXi_n])
                xr_o = xs5[0][:, :, 0:NTL].rearrange("p g (s t) -> p g s t", t=TL)[:, :, :, t]
                xi_o = xs5[1][:, :, 0:NTL].rearrange("p g (s t) -> p g s t", t=TL)[:, :, :, t]
                cp("act", xr_o, Xr_n, [tXr_n], [txs[0]])
                cp("act", xi_o, Xi_n, [tXi_n], [txs[1]])
            s5_y(0)
        else:
            A_, B_, C_, D_ = S5tmp[:, 0], S5tmp[:, 1], S5tmp[:, 2], S5tmp[:, 3]
            tA_, tB_, tC_, tD_ = ttq
            c1 = tb["C1"][:]; s1 = tb["S1"][:]
            bur = bu[0][:, :, 0:TL]; bui = bu[1][:, :, 0:TL]
            Xcr = Xp[l][0]; Xci = Xp[l][1]; tXcr = tXp[l][0]; tXci = tXp[l][1]
            for it in range(ntile):
                s5_bu(it)
                if it < len(attn_units):
                    attn_units[it][0]()
                tt("dve", A_, c1, bur, ALU.mult, [tab, tbu[0], ttmp], [tA_])
                tt("dve", B_, s1, bui, ALU.mult, [tab, tbu[1], ttmp], [tB_])
                tt("dve", A_, A_, B_, ALU.add, [tB_], [tA_])
                tt("dve", B_, c1, bui, ALU.mult, [tab, tbu[1]], [tB_])
                tt("dve", C_, s1, bur, ALU.mult, [tab, tbu[0]], [tC_])
                tt("dve", B_, B_, C_, ALU.subtract, [tC_], [tB_])
                for gp in range(16):
                    P.op("dve", (lambda gp: lambda e: e.tensor_tensor_scan(
                        A_[:, gp, :], tb["RR"][:, gp:gp + 1].to_broadcast([128, TL]), A_[:, gp, :],
                        Xcr[:, gp:gp + 1], ALU.mult, ALU.add))(gp), [tab, tXcr], [tA_])
                    P.op("dve", (lambda gp: lambda e: e.tensor_tensor_scan(
                        B_[:, gp, :], tb["RR"][:, gp:gp + 1].to_broadcast([128, TL]), B_[:, gp, :],
                        Xci[:, gp:gp + 1], ALU.mult, ALU.add))(gp), [tab, tXci], [tB_])
                if it < len(attn_units):
                    attn_units[it][1]()
                if it > 0:
                    s5_y(it - 1)
                tt("dve", C_, c1, A_, ALU.mult, [tab, tA_], [tC_])
                tt("dve", D_, s1, B_, ALU.mult, [tab, tB_], [tD_])
                tt("dve", C_, C_, D_, ALU.subtract, [tD_], [tC_])
                tt("dve", D_, s1, A_, ALU.mult, [tab, tA_, tC_], [tD_])
                tt("dve", A_, c1, B_, ALU.mult, [tab, tB_], [tA_])
                tt("dve", D_, D_, A_, ALU.add, [tA_], [tD_])
                cp("act", xs5[0][:, :, 0:TL], C_, [tC_], [txs[0]])
                cp("act", xs5[1][:, :, 0:TL], D_, [tD_], [txs[1]])
                cp("act", Xcr[:], C_[:, :, TL - 1], [tC_], [tXcr])
                cp("act", Xci[:], D_[:, :, TL - 1], [tD_], [tXci])
            s5_y(ntile - 1)
        if not sample:
            for u_ in attn_units[ntile:]:
                u_[0](); u_[1]()
            cp("dve", kT[l][:, :, 0:128], kT[l][:, :, NT:NT + 128], [tk[l]], [tk[l]])
            cp("dve", vtok[l][:, 0, :], vtok[l][:, NTI, :], [tv[l]], [tv[l]])
        fin = grp["step"] % 2 if sample else 0
        if sample:
            P.dma("pool", d_so[l], o_srs[l], Xst[l][0][fin][:], reads=[tX[l][0][fin]])
            P.dma("pool", d_so[l], o_sis[l], Xst[l][1][fin][:], reads=[tX[l][1][fin]])
        elif bidx == n_batches - 1:
            P.dma("pool", d_out, o_srp[l], Xp[l][0][:], reads=[tXp[l][0]])
            P.dma("pool", d_out, o_sip[l], Xp[l][1][:], reads=[tXp[l][1]])
        Y = yT[:, :, 0:ntok]; G1 = g1[:, :, 0:ntok]; G2 = g2[:, :, 0:ntok]
        tt("dve", G1, Y, Y, ALU.mult, [ty], [tg1])
        ts("dve", G1, G1, 0.044715, 1.0, ALU.mult, ALU.add, [tg1], [tg1])
        tt("dve", G1, G1, Y, ALU.mult, [tg1, ty], [tg1])
        act(G1, G1, AF.Tanh, [tg1], [tg1], scale=0.7978845608028654)
        ts("dve", G2, Y, 0.5, None, ALU.mult, ALU.bypass, [ty], [tg2])
        stt(G1, G1, 1.0, G2, ALU.add, ALU.mult, [tg1, tg2], [tg1])
        cp("act", sbf[:, :, 0:ntok], G1, [tg1], [tsb])
        ts("dve", G2, G1, 0.5, None, ALU.mult, ALU.bypass, [tg1], [tg2])

        def glu_load(b, slot):
            return [(slot[:, 0:4, :], wb_glu[l, :, :, b * 128:(b + 1) * 128])]

        def glu_mm(b, slot):
            return [(slot[:, k, :], sbf[:, k, 0:ntok], (lambda ps: ps[:, 0:ntok])) for k in range(4)]

        def glu_cb(b, ps, tps):
            act(g1[:, b, 0:ntok], ps[:, 0:ntok], AF.Tanh, [tps, tconst], [tg1], bias=hbglu[:, l, b:b + 1], scale=0.5)
            stt(yT[:, b, 0:ntok], g1[:, b, 0:ntok], 1.0, g2[:, b, 0:ntok], ALU.add, ALU.mult, [tg1, tg2], [ty])
        linear(4, glu_load, glu_mm, glu_cb, [tsb], wtrk=[twbs[("glu", l)]])
        rmsnorm(attnT, [tat], 8, 64, lambda k: gattn[:, l, k:k + 1], ntok, lambda k: attnB[:, k, 0:ntok], [tatb], 512)
        rmsnorm(yT, [ty], 4, 128, lambda k: gssm[:, l, k:k + 1], ntok, lambda k: ssmB[:, k, 0:ntok], [tsmb], 512)

        def wo_load(b, slot):
            blk = b // 2
            if b % 2 == 0:
                return [(slot[0:64, 0:8, :], wb_outa[l, :, :, blk * 128:(blk + 1) * 128])]
            return [(slot[:, 0:4, :], wb_outb[l, :, :, blk * 128:(blk + 1) * 128])]

        def wo_mm(b, slot):
            o = (lambda ps: ps[:, 0:ntok])
            if b % 2 == 0:
                return [(slot[0:64, k, :], attnB[:, k, 0:ntok], o) for k in range(8)]
            return [(slot[:, k, :], ssmB[:, k, 0:ntok], o) for k in range(4)]

        def wo_cb(b, ps, tps):
            tt("dve", xT[:, b, 0:ntok], ps[:, 0:ntok], xT[:, b, 0:ntok], ALU.add, [tps, tx[b]], [tx[b]])
        linear(16, wo_load, wo_mm, wo_cb, [tatb, tsmb], group=2, wtrk=[twbs[("outa", l)], twbs[("outb", l)]])
        rmsnorm(xT, tx, 8, 128, lambda k: gffn[:, l, k:k + 1], ntok, lambda k: hT[:, k, 0:ntok], th_, D)
        car = scarry[l] if sample else carry[l]
        CTL = ntok // NS
        W2 = CTL + 2

        def up_load(b, slot):
            m = (b // 2) + 21 * (b % 2)
            return [(slot[:, 0:8, :], wb_up[l, :, :, m * 128:(m + 1) * 128])]

        def up_mm(b, slot):
            return [(slot[:, k, :], hT[:, k, 0:ntok], (lambda ps: ps[:, 0:ntok])) for k in range(8)]

        def up_cb(b, ps, tps):
            w = b % 2
            m = (b // 2) + 21 * w
            pp = (b // 2) % 2
            upb = upb2[pp]; cs_ = cs2[pp]; tup = tup2[pp]; tcs = tcs2[pp]
            ub = upb[:, w, 0:NS * W2].rearrange("p (s t) -> p s t", t=W2)
            cv_ = car[:, m, 0:NS * 2].rearrange("p (s t) -> p s t", t=2)
            cp("act", ub[:, :, 0:2], cv_, [tcar[l][m]], [tup[w]])
            cp("act", ub[:, :, 2:W2], ps[:, 0:ntok].rearrange("p (s t) -> p s t", t=CTL), [tps], [tup[w]])
            cp("dve", cv_, ub[:, :, CTL:CTL + 2], [tup[w]], [tcar[l][m]])
            cv3 = cs_[:, w, 0:ntok].rearrange("p (s t) -> p s t", t=CTL)
            act(cv3, ub[:, :, 2:W2], AF.Identity, [tup[w], tconst], [tcs[w]], bias=cbs[:, l, m:m + 1], scale=cws[:, l, m, 2:3])
            stt(cv3, ub[:, :, 1:1 + CTL], cws[:, l, m, 1:2], cv3, ALU.mult, ALU.add, [tup[w], tconst, tcs[w]], [tcs[w]])
            stt(cv3, ub[:, :, 0:CTL], cws[:, l, m, 0:1], cv3, ALU.mult, ALU.add, [tup[w], tconst, tcs[w]], [tcs[w]])
            if w == 1:
                mh = b // 2
                A_ = cs_[:, 0, 0:ntok]; G_ = cs_[:, 1, 0:ntok]
                act(g1[:, pp, 0:ntok], G_, AF.Tanh, [tcs[1], tg1], [tgs[pp]], scale=0.5)
                stt(g1[:, pp, 0:ntok], g1[:, pp, 0:ntok], 1.0, G_, ALU.add, ALU.mult, [tgs[pp], tcs[1]], [tgs[pp]])
                stt(hid[:, mh, 0:ntok], g1[:, pp, 0:ntok], 0.5, A_, ALU.mult, ALU.mult, [tgs[pp], tcs[0]], [thid])
        linear(42, up_load, up_mm, up_cb, th_, wtrk=[twbs[("up", l)]])
        if sample:
            P.dma("pool", d_out, o_cs[l], car[:, :, 0:32].rearrange("p m (s t) -> p m s t", t=2), reads=tcar[l])
        elif bidx == n_batches - 1:
            P.dma("pool", d_out, o_cp[l], car[:, :, 0:2], reads=tcar[l])

        def dn_load(b, slot):
            k0, nk = 7 * (b % 3), 7
            return [(slot[:, 0:nk, :], wb_down[l, :, k0:k0 + nk, (b // 3) * 128:(b // 3 + 1) * 128])]

        def dn_mm(b, slot):
            k0, nk = 7 * (b % 3), 7
            return [(slot[:, k, :], hid[:, k0 + k, 0:ntok], (lambda ps: ps[:, 0:ntok])) for k in range(nk)]

        def dn_cb(b, ps, tps):
            tt("dve", xT[:, b, 0:ntok], ps[:, 0:ntok], xT[:, b, 0:ntok], ALU.add, [tps, tx[b]], [tx[b]])
        linear(24, dn_load, dn_mm, dn_cb, [thid], group=3, wtrk=[twbs[("down", l)]])

    d_x = P.new_dsem("d_x"); d_y = P.new_dsem("d_y")
    for l in range(L):
        mset("dve", carry[l][:], 0.0, tcar[l])
        for ri in range(2):
            mset("dve", Xp[l][ri][:], 0.0, [tXp[l][ri]])
    pstep = [0, 0]
    for b in range(n_batches):
        P.dma("pool", d_x, xT[:], xpT[:, :, b * NT:(b + 1) * NT], writes=tx)
        for l in range(L):
            grp = dict(sample=False, ntok=NT, NS=1, TL=TLP, b=b, step=pstep[l])
            run_layer(l, grp)
            pstep[l] = grp["step"]
        rmsnorm(xT, tx, 8, 128, lambda k: gfin[:, k:k + 1], NT, lambda k: yout[:, k, :], [tyo, tg1, tg2] + tgs, D)
        P.dma("pool", d_y, o_ypT[:, :, b * NT:(b + 1) * NT], yout[:], reads=[tyo, tg1, tg2] + tgs)
    if do_sample:
        P.dma("pool", d_x, xT[:, :, 0:64], xsT, writes=tx)
        for l in range(L):
            cur = pstep[l] % 2
            d_s1 = P.new_dsem("d_s1_%d" % l); d_s2 = P.new_dsem("d_s2_%d" % l); d_s3 = P.new_dsem("d_s3_%d" % l)
            P.dma("pool", d_s1, Xst[l][0][cur][:], sre[l], writes=[tX[l][0][cur]])
            P.dma("pool", d_s2, Xst[l][1][cur][:], sim[l], writes=[tX[l][1][cur]])
            P.dma("pool", d_s3, scarry[l][:, :, 0:32].rearrange("p m (s t) -> p m s t", t=2), sconv[l], writes=tcar[l])
            grp = dict(sample=True, ntok=64, NS=16, TL=4, step=pstep[l])
            run_layer(l, grp)
        rmsnorm(xT, tx, 8, 128, lambda k: gfin[:, k:k + 1], 64, lambda k: yout[:, k, 0:64], [tyo, tg1, tg2] + tgs, D)
        P.dma("pool", d_y, o_ysT, yout[:, :, 0:64], reads=[tyo, tg1, tg2] + tgs)
    P.finish()
    return nc


def _consts():
    ident = np.eye(128, dtype=np.float32)
    ones = np.ones((128, 128), np.float32)
    slopes = np.array([2.0 ** -(h + 1) for h in range(8)], np.float32).reshape(2, 4)
    k = np.arange(128)[:, None]; q = np.arange(128)[None, :]
    biasP = np.full((128, 2, 2, 4, 128), NEG, np.float32)
    for kv in range(2):
        for g in range(4):
            sl = slopes[kv, g]
            dcur = q - k
            biasP[:, 1, kv, g, :] = np.where(dcur >= 0, -sl * dcur, NEG)
            dprev = q + 128 - k
            biasP[:, 0, kv, g, :] = np.where(dprev < 128, -sl * dprev, NEG)
    biasP = biasP.reshape(128, 2, 2, 512)
    j = np.arange(128)[:, None]; tq = np.arange(4)[None, :]
    bSc = np.full((128, 2, 16, 4, 4), NEG, np.float32)
    bSn = np.full((4, 2, 16, 4, 4), NEG, np.float32)
    tk_ = np.arange(4)[:, None]
    for kv in range(2):
        for g in range(4):
            sl = slopes[kv, g]
            dist = tq + 128 - j
            bSc[:, kv, :, g, :] = np.where(dist < 128, -sl * dist, NEG)[:, None, :]
            dn = tq - tk_
            bSn[:, kv, :, g, :] = np.where(dn >= 0, -sl * dn, NEG)[:, None, :]
    return dict(c_ident=ident, c_ones=ones, c_biasP=biasP, c_biasSc=bSc.reshape(128, 2, 256), c_biasSn=bSn.reshape(4, 2, 256))


def _ktile(w, kt):
    Lw, K, N = w.shape
    return np.ascontiguousarray(w.reshape(Lw, kt, K // kt, N).transpose(0, 2, 1, 3))


def _vec(v, kt):
    Lw, F = v.shape
    return np.ascontiguousarray(v.reshape(Lw, kt, F // kt).transpose(2, 0, 1))


def _l0(a):
    sh = a.shape
    a = a.reshape(sh[0], 16, 2, 64, *sh[3:])
    perm = (2, 3, 0, 1) + tuple(range(4, a.ndim))
    a = a.transpose(perm)
    return np.ascontiguousarray(a.reshape(128, sh[0], 16, *sh[3:]))


_PROG = {}


def kernel(x_prompt, x_sample, cache_k_win, cache_v_win, state_ssm_re, state_ssm_im, state_conv,
           norm_mix, w_in, sinks, lam_re, lam_im, log_step, b_re, b_im, c_re, c_im, d_skip,
           w_glu, b_glu, g_attn, g_ssm, w_out, norm_ffn, w_up, conv_w, conv_b, w_down, norm_final):
    f = lambda a: np.ascontiguousarray(np.asarray(a, dtype=np.float32))
    (x_prompt, x_sample, cache_k_win, cache_v_win, state_ssm_re, state_ssm_im, state_conv, norm_mix, w_in, sinks,
     lam_re, lam_im, log_step, b_re, b_im, c_re, c_im, d_skip, w_glu, b_glu, g_attn, g_ssm, w_out, norm_ffn, w_up,
     conv_w, conv_b, w_down, norm_final) = [f(a) for a in (
        x_prompt, x_sample, cache_k_win, cache_v_win, state_ssm_re, state_ssm_im, state_conv, norm_mix, w_in, sinks,
        lam_re, lam_im, log_step, b_re, b_im, c_re, c_im, d_skip, w_glu, b_glu, g_attn, g_ssm, w_out, norm_ffn, w_up,
        conv_w, conv_b, w_down, norm_final)]
    if "nc" not in _PROG:
        _PROG["nc"] = build_program()
    nc = _PROG["nc"]
    shared = dict(_consts())
    shared.update(
        w_in=_ktile(w_in, 8), w_outa=_ktile(w_out[:, :512], 8), w_outb=_ktile(w_out[:, 512:], 4),
        w_glu=_ktile(w_glu, 4), w_up=_ktile(w_up, 8), w_down=_ktile(w_down, 21),
        g_mix=_vec(norm_mix, 8), g_ffn=_vec(norm_ffn, 8), g_fin=_vec(norm_final[None], 8)[:, 0],
        g_attn=_vec(g_attn, 8), g_ssm=_vec(g_ssm, 4), b_glu=_vec(b_glu, 4), d_skip=_vec(d_skip, 4),
        cw=np.ascontiguousarray(conv_w.reshape(L, 3, MT_UP, 128).transpose(3, 0, 2, 1)),
        cb=_vec(conv_b, MT_UP),
        sinkP=np.ascontiguousarray(np.broadcast_to(sinks[None], (64, L, 8))),
        lamre=_l0(lam_re), lamim=_l0(lam_im),
        lstep=_l0(np.broadcast_to(log_step[:, :, None], (L, 32, 64))),
        bre=_l0(b_re), bim=_l0(b_im),
        cre=_l0(c_re.transpose(0, 1, 3, 2)), cim=_l0(c_im.transpose(0, 1, 3, 2)),
    )
    shared = {k: np.ascontiguousarray(v, dtype=np.float32) for k, v in shared.items()}
    in_maps = []
    for c in range(8):
        s0 = 16 * c
        m = dict(shared)
        m["xpT"] = np.ascontiguousarray(x_prompt[c].T.reshape(8, 128, SEQ).transpose(1, 0, 2))
        m["xsT"] = np.ascontiguousarray(x_sample[s0:s0 + 16].reshape(64, D).T.reshape(8, 128, 64).transpose(1, 0, 2))
        ckc = cache_k_win[:, s0:s0 + 16].reshape(L, 16, 128, 128); cvc = cache_v_win[:, s0:s0 + 16].reshape(L, 16, 128, 128)
        m["ck"] = np.ascontiguousarray(ckc.transpose(0, 2, 1, 3)); m["cv"] = np.ascontiguousarray(cvc.transpose(0, 2, 1, 3))
        m["ckr"] = np.ascontiguousarray(ckc); m["cvr"] = np.ascontiguousarray(cvc)
        def st(a):
            a = a[:, s0:s0 + 16].reshape(L, 16, 16, 2, 64).transpose(0, 3, 4, 2, 1)
            return np.ascontiguousarray(a.reshape(L, 128, 16, 16))
        m["sre"] = st(state_ssm_re); m["sim"] = st(state_ssm_im)
        m["sconv"] = np.ascontiguousarray(state_conv[:, s0:s0 + 16].reshape(L, 16, 2, MT_UP, 128).transpose(0, 4, 3, 1, 2))
        in_maps.append(m)
    res = run_bass_kernel_spmd(nc, in_maps, core_ids=list(range(8)))
    R = res.results
    B = 8
    y_p = np.stack([R[c]["o_ypT"].transpose(1, 0, 2).reshape(D, SEQ).T for c in range(B)])
    y_s = np.concatenate([R[c]["o_ysT"].transpose(1, 0, 2).reshape(D, 64).T.reshape(16, 4, D) for c in range(B)])
    k_p = np.stack([R[c]["o_kp"] for c in range(B)], 1).reshape(L, B, 128, 2, 64)
    v_p = np.stack([R[c]["o_vp"] for c in range(B)], 1).reshape(L, B, 128, 2, 64)

    def unst_p(key):
        a = np.stack([R[c][key] for c in range(B)], 1)
        a = a.reshape(L, B, 2, 64, 16).transpose(0, 1, 4, 2, 3)
        return np.ascontiguousarray(a.reshape(L, B, 32, 64))
    sr_p = unst_p("o_srp"); si_p = unst_p("o_sip")
    c_p = np.stack([R[c]["o_cp"] for c in range(B)], 1)
    c_p = np.ascontiguousarray(c_p.transpose(0, 1, 4, 3, 2).reshape(L, B, 2, 2 * DFF))
    k_s = np.concatenate([R[c]["o_ks"] for c in range(B)], 1).reshape(L, 128, 128, 2, 64)
    v_s = np.concatenate([R[c]["o_vs"] for c in range(B)], 1).reshape(L, 128, 128, 2, 64)

    def unst_s(key):
        a = np.stack([R[c][key] for c in range(B)], 1)
        a = a.reshape(L, B, 2, 64, 16, 16).transpose(0, 1, 5, 4, 2, 3)
        return np.ascontiguousarray(a.reshape(L, B * 16, 32, 64))
    sr_s = unst_s("o_srs"); si_s = unst_s("o_sis")
    c_s = np.stack([R[c]["o_cs"] for c in range(B)], 1)
    c_s = np.ascontiguousarray(c_s.transpose(0, 1, 4, 5, 3, 2).reshape(L, B * 16, 2, 2 * DFF))
    outs = (y_p, y_s, k_p, v_p, sr_p, si_p, c_p, k_s, v_s, sr_s, si_s, c_s)
    return tuple(np.ascontiguousarray(o, dtype=np.float32) for o in outs)
```
